# Optimizing a Trainium2 kernel written in Bass

```python
import math
import jax, jax.numpy as jnp
from jax import lax
import numpy as np

D_MODEL = 1024
BATCH = 8
SEQ = 2048
DEPTH = 1

HEAD_DIM = 64
FOX_HEADS = D_MODEL // 128
DIFF_HEADS = D_MODEL // 256
FOX_WIDTH = FOX_HEADS * HEAD_DIM
DIFF_WIDTH = DIFF_HEADS * 2 * HEAD_DIM
MIX_WIDTH = FOX_WIDTH + DIFF_WIDTH
IN_COLS = 3 * FOX_WIDTH + FOX_HEADS + 3 * DIFF_WIDTH
ROPE_THETA = 500000.0
ROT_DIM = HEAD_DIM // 4
Q_BLOCK = 128
PEER_HEADS = 8
PEER_KEYS = 128
PEER_EXPERTS = PEER_KEYS * PEER_KEYS
PEER_KEY_DIM = 128
PEER_HALF = PEER_KEY_DIM // 2
PEER_TOPK = 16
TOKEN_CHUNK = 128
NORM_EPS = 1e-6
SUBLN_EPS = 1e-5

kernel_name = "hybrid_fox_diffattn_peer_adaln"


def _rmsnorm(x, g, eps):
    xf = x.astype(jnp.float32)
    y = xf * lax.rsqrt(jnp.mean(xf * xf, axis=-1, keepdims=True) + eps)
    return (y * g.astype(jnp.float32)).astype(x.dtype)


def _partial_rope(t, pos):
    half = ROT_DIM // 2
    inv = ROPE_THETA ** (-jnp.arange(0, ROT_DIM, 2, dtype=jnp.float32) / ROT_DIM)
    ang = pos[:, None] * inv[None, :]
    cos, sin = jnp.cos(ang), jnp.sin(ang)
    tr = t[..., :ROT_DIM].astype(jnp.float32)
    t1, t2 = tr[..., :half], tr[..., half:]
    rot = jnp.concatenate([t1 * cos - t2 * sin, t2 * cos + t1 * sin], axis=-1)
    return jnp.concatenate([rot.astype(t.dtype), t[..., ROT_DIM:]], axis=-1)


def _fox_attention(q, k, v, logf):
    B, H, S, dh = q.shape
    nb = S // Q_BLOCK
    F = jnp.cumsum(logf, axis=-1)
    qb = q.reshape(B, H, nb, Q_BLOCK, dh).transpose(2, 0, 1, 3, 4)
    Fb = F.reshape(B, H, nb, Q_BLOCK).transpose(2, 0, 1, 3)
    kf = k.astype(jnp.float32)
    kpos = jnp.arange(S)
    scale = dh ** -0.5

    def block(args):
        qi, Fi, bi = args
        s = jnp.einsum('bhqd,bhkd->bhqk', qi.astype(jnp.float32), kf) * scale
        s = s + Fi[..., None] - F[:, :, None, :]
        qpos = bi * Q_BLOCK + jnp.arange(Q_BLOCK)
        s = jnp.where(kpos[None, :] <= qpos[:, None], s, -jnp.inf)
        p = jax.nn.softmax(s, axis=-1)
        return jnp.einsum('bhqk,bhkd->bhqd', p.astype(v.dtype), v)

    o = lax.map(block, (qb, Fb, jnp.arange(nb)))
    return o.transpose(1, 2, 0, 3, 4).reshape(B, H, S, dh)


def _diff_attention(q, k, v, lam):
    B, H, _, S, dh = q.shape
    nb = S // Q_BLOCK
    qb = q.reshape(B, H, 2, nb, Q_BLOCK, dh).transpose(3, 0, 1, 2, 4, 5)
    kf = k.astype(jnp.float32)
    kpos = jnp.arange(S)
    scale = dh ** -0.5

    def block(args):
        qi, bi = args
        s = jnp.einsum('bhcqd,bhckd->bhcqk', qi.astype(jnp.float32), kf) * scale
        qpos = bi * Q_BLOCK + jnp.arange(Q_BLOCK)
        s = jnp.where(kpos[None, :] <= qpos[:, None], s, -jnp.inf)
        p = jax.nn.softmax(s, axis=-1)
        pd = p[:, :, 0] - lam * p[:, :, 1]
        return jnp.einsum('bhqk,bhkd->bhqd', pd.astype(v.dtype), v)

    o = lax.map(block, (qb, jnp.arange(nb)))
    return o.transpose(1, 2, 0, 3, 4).reshape(B, H, S, 2 * dh)


def _peer(h, w_pq, sub_keys, u_tab, v_tab):
    B, S, D = h.shape
    q = (h @ w_pq).reshape(B, S, PEER_HEADS, 2, PEER_HALF).astype(jnp.float32)
    sk = sub_keys.astype(jnp.float32)
    s1 = jnp.einsum('bshd,nd->bshn', q[..., 0, :], sk[0])
    s2 = jnp.einsum('bshd,nd->bshn', q[..., 1, :], sk[1])
    sc1, i1 = lax.top_k(s1, PEER_TOPK)
    sc2, i2 = lax.top_k(s2, PEER_TOPK)
    comb = (sc1[..., :, None] + sc2[..., None, :]).reshape(B, S, PEER_HEADS, PEER_TOPK * PEER_TOPK)
    top, ci = lax.top_k(comb, PEER_TOPK)
    e_idx = (jnp.take_along_axis(i1, ci // PEER_TOPK, axis=-1) * PEER_KEYS
             + jnp.take_along_axis(i2, ci % PEER_TOPK, axis=-1))
    g = jax.nn.softmax(top, axis=-1)
    T = B * S
    nc = T // TOKEN_CHUNK
    K = PEER_HEADS * PEER_TOPK
    hc = h.reshape(nc, TOKEN_CHUNK, D)
    ic = e_idx.reshape(nc, TOKEN_CHUNK, K)
    gc = g.reshape(nc, TOKEN_CHUNK, K)

    def chunk(args):
        hx, ix, gx = args
        u = u_tab[ix]
        a = jnp.einsum('ckd,cd->ck', u.astype(jnp.float32), hx.astype(jnp.float32))
        w = jax.nn.gelu(a, approximate=False) * gx
        return jnp.einsum('ck,ckd->cd', w.astype(h.dtype), v_tab[ix])

    out = lax.map(chunk, (hc, ic, gc))
    return out.reshape(B, S, D)


def setup_inputs(seed: int = 0) -> dict:
    key = jax.random.key(seed)
    ks = jax.random.split(key, 20)
    f32 = jnp.float32
    nrm = lambda k, shape, s: (jax.random.normal(k, shape, f32) * s)
    return {
        "x": nrm(ks[0], (BATCH, SEQ, D_MODEL), 1.0),
        "c": nrm(ks[1], (BATCH, D_MODEL), 1.0),
        "w_ada": nrm(ks[2], (DEPTH, D_MODEL, 6 * D_MODEL), 0.5 * D_MODEL ** -0.5),
        "b_ada": nrm(ks[3], (DEPTH, 6 * D_MODEL), 0.02),
        "g_attn": 1.0 + nrm(ks[4], (DEPTH, D_MODEL), 0.02),
        "w_in": nrm(ks[5], (DEPTH, D_MODEL, IN_COLS), D_MODEL ** -0.5),
        "b_f": nrm(ks[6], (DEPTH, FOX_HEADS), 0.1),
        "lambda_q1": nrm(ks[7], (DEPTH, HEAD_DIM), 0.1),
        "lambda_k1": nrm(ks[8], (DEPTH, HEAD_DIM), 0.1),
        "lambda_q2": nrm(ks[9], (DEPTH, HEAD_DIM), 0.1),
        "lambda_k2": nrm(ks[10], (DEPTH, HEAD_DIM), 0.1),
        "g_subln": 1.0 + nrm(ks[11], (DEPTH, 2 * HEAD_DIM), 0.02),
        "w_o": nrm(ks[12], (DEPTH, MIX_WIDTH, D_MODEL), MIX_WIDTH ** -0.5),
        "g_ffn": 1.0 + nrm(ks[13], (DEPTH, D_MODEL), 0.02),
        "w_pq": nrm(ks[14], (DEPTH, D_MODEL, PEER_HEADS * PEER_KEY_DIM), D_MODEL ** -0.5),
        "sub_keys": nrm(ks[15], (DEPTH, 2, PEER_KEYS, PEER_HALF), PEER_HALF ** -0.5),
        "u_experts": nrm(ks[16], (DEPTH, PEER_EXPERTS, D_MODEL), D_MODEL ** -0.5),
        "v_experts": nrm(ks[17], (DEPTH, PEER_EXPERTS, D_MODEL), PEER_TOPK ** -0.5),
        "g_final": 1.0 + nrm(ks[18], (D_MODEL,), 0.02),
    }


def reference(x, c, w_ada, b_ada, g_attn, w_in, b_f, lambda_q1, lambda_k1, lambda_q2,
              lambda_k2, g_subln, w_o, g_ffn, w_pq, sub_keys, u_experts, v_experts, g_final):
    B, S, D = x.shape
    pos = jnp.arange(S, dtype=jnp.float32)
    split_at = np.cumsum([FOX_WIDTH, FOX_WIDTH, FOX_WIDTH, FOX_HEADS,
                          DIFF_WIDTH, DIFF_WIDTH]).tolist()
    for l in range(DEPTH):
        mod = (jax.nn.silu(c.astype(jnp.float32)) @ w_ada[l].astype(jnp.float32)
               + b_ada[l].astype(jnp.float32)).astype(x.dtype)
        sh1, sc1, gt1, sh2, sc2, gt2 = [m[:, None, :] for m in jnp.split(mod, 6, axis=-1)]

        h = _rmsnorm(x, g_attn[l], NORM_EPS) * (1 + sc1) + sh1
        proj = h @ w_in[l]
        fq, fk, fv, fg, dq, dk, dv = jnp.split(proj, split_at, axis=-1)

        to_heads = lambda t, H, dh: t.reshape(B, S, H, dh).transpose(0, 2, 1, 3)
        fq, fk, fv = (to_heads(t, FOX_HEADS, HEAD_DIM) for t in (fq, fk, fv))
        logf = jax.nn.log_sigmoid((fg + b_f[l]).astype(jnp.float32)).transpose(0, 2, 1)
        o_fox = _fox_attention(fq, fk, fv, logf)

        to_pairs = lambda t: t.reshape(B, S, DIFF_HEADS, 2, HEAD_DIM).transpose(0, 2, 3, 1, 4)
        dq = _partial_rope(to_pairs(dq), pos)
        dk = _partial_rope(to_pairs(dk), pos)
        dv = to_heads(dv, DIFF_HEADS, 2 * HEAD_DIM)
        lam_init = 0.8 - 0.6 * math.exp(-0.3 * l)
        lam = (jnp.exp(jnp.sum(lambda_q1[l].astype(jnp.float32) * lambda_k1[l].astype(jnp.float32)))
               - jnp.exp(jnp.sum(lambda_q2[l].astype(jnp.float32) * lambda_k2[l].astype(jnp.float32)))
               + lam_init)
        o_diff = _diff_attention(dq, dk, dv, lam)
        o_diff = _rmsnorm(o_diff, g_subln[l], SUBLN_EPS) * (1.0 - lam_init)

        mixed = jnp.concatenate([
            o_fox.transpose(0, 2, 1, 3).reshape(B, S, FOX_WIDTH),
            o_diff.transpose(0, 2, 1, 3).reshape(B, S, DIFF_WIDTH).astype(o_fox.dtype)], axis=-1)
        x = x + gt1 * (mixed @ w_o[l])

        h2 = _rmsnorm(x, g_ffn[l], NORM_EPS) * (1 + sc2) + sh2
        x = x + gt2 * _peer(h2, w_pq[l], sub_keys[l], u_experts[l], v_experts[l])
    return _rmsnorm(x, g_final, NORM_EPS)
```

```python
import math
import numpy as np
import concourse.bass as bass
import concourse.mybir as mybir
from concourse.bass_utils import run_bass_kernel_spmd

F32 = mybir.dt.float32
BF16 = mybir.dt.bfloat16
U32 = mybir.dt.uint32
I32 = mybir.dt.int32
AF = mybir.ActivationFunctionType
ALU = mybir.AluOpType
AX = mybir.AxisListType

D = 1024
NCORES = 8
SEQ = 2048
NEG = -1.0e30


class Buf:
    __slots__ = ("name", "w", "r")

    def __init__(self, name=""):
        self.name = name
        self.w = None
        self.r = {}


class Prog:
    ENGS = ("sync", "act", "dve", "pool", "pe")

    def __init__(self):
        self.streams = {e: [] for e in self.ENGS}
        self.cnt = {e: 0 for e in self.ENGS}
        self.dma_tot = {}
        self.pending = {e: [] for e in self.ENGS}

    def _deps(self, R, W, extra, waw_eng=None):
        deps = []
        for b in R:
            if b.w is not None:
                deps.append(b.w)
        for b in W:
            if b.w is not None and b.w[0] != waw_eng:
                deps.append(b.w)
            for k, v in b.r.items():
                deps.append((k, v))
        for t in extra:
            if t is not None:
                deps.append(t)
        return deps

    def _mark(self, tok, R, W):
        for b in R:
            k, v = tok
            if b.r.get(k, 0) < v:
                b.r[k] = v
        for b in W:
            b.w = tok
            b.r = {}

    def op(self, eng, fn, R=(), W=(), extra=(), skip=(), waw_ok=False):
        deps = [d for d in self._deps(R, W, (), eng if waw_ok else None) if d[0] not in skip] + [t for t in extra if t is not None] + self.pending[eng]
        self.pending[eng] = []
        self.cnt[eng] += 1
        tok = (eng, self.cnt[eng])
        self.streams[eng].append((fn, deps, tok, None))
        self._mark(tok, R, W)
        return tok

    def dma(self, queue, fn, sem, R=(), W=(), extra=(), skip=()):
        deps = [d for d in self._deps(R, W, ()) if d[0] not in skip] + [t for t in extra if t is not None] + self.pending[queue]
        self.pending[queue] = []
        self.dma_tot[sem] = self.dma_tot.get(sem, 0) + 16
        tok = ("dma:" + sem, self.dma_tot[sem])
        self.streams[queue].append((fn, deps, tok, sem))
        self._mark(tok, R, W)
        return tok

    def barrier(self):
        toks = [(e, self.cnt[e]) for e in self.ENGS if e != "sync" and self.cnt[e] > 0]
        toks += [("dma:" + s, v) for s, v in self.dma_tot.items()]
        for e in self.ENGS:
            self.pending[e] = self.pending[e] + toks

    def emit(self, nc, stack):
        sems = {}
        for e in self.ENGS:
            if e != "sync":
                sems[e] = stack.enter_context(nc.semaphore("s_" + e))
        for s in self.dma_tot:
            sems["dma:" + s] = stack.enter_context(nc.semaphore("d_" + s))
        block = stack.enter_context(nc.Block())

        def run(ename):
            def body(eng):
                seen = {}
                for fn, deps, tok, dsem in self.streams[ename]:
                    need = {}
                    for k, v in deps:
                        if k == "pe" and ename == "pe":
                            continue
                        if need.get(k, 0) < v:
                            need[k] = v
                    for k, v in need.items():
                        if seen.get(k, 0) < v:
                            eng.wait_ge(sems[k], v)
                            seen[k] = v
                    ins = fn(eng)
                    if dsem is not None:
                        ins.then_inc(sems["dma:" + dsem], 16)
                    else:
                        ins.then_inc(sems[ename], 1)
            return body

        block.sync(run("sync"))
        block.scalar(run("act"))
        block.vector(run("dve"))
        block.gpsimd(run("pool"))
        block.tensor(run("pe"))


class Alloc:
    def __init__(self, nc, lo=16512, hi=229376):
        self.nc = nc
        self.lo = lo
        self.hi = hi
        self.n = 0

    def zone(self, start, size):
        return {"start": start, "end": start + size, "cur": start}

    def alloc(self, z, name, shape, dtype):
        esz = {F32: 4, BF16: 2, U32: 4, I32: 4}[dtype]
        nbytes = esz
        for s in shape[1:]:
            nbytes *= s
        nbytes = (nbytes + 63) // 64 * 64
        off = z["cur"]
        assert off + nbytes <= z["end"], (name, off, nbytes, z)
        assert off + nbytes <= self.hi
        z["cur"] = off + nbytes
        self.n += 1
        return self.nc.alloc_sbuf_tensor_at("%s_%d" % (name, self.n), list(shape), dtype, offset=off)


def build(NT=16, stage=99, dbg=False, nunits=8):
    S = NT * 128
    NG = NT // 4
    nc = bass.Bass("TRN2", target_bir_lowering=False)
    P = Prog()

    def dram_in(name, shape, dt=F32):
        return nc.dram_tensor(name, list(shape), dt, kind="ExternalInput").ap()

    x_d = dram_in("x", [S, D])
    cT_d = dram_in("cT", [128, 8])
    wada_d = dram_in("w_ada", [D, 6 * D])
    badaT_d = dram_in("b_adaT", [128, 48])
    gattnT_d = dram_in("g_attnT", [128, 8])
    gffnT_d = dram_in("g_ffnT", [128, 8])
    gfin_d = dram_in("g_final_bc", [128, D])
    wun_d = dram_in("w_units", [8, D, 384])
    wfg_d = dram_in("w_fg", [D, 8])
    bf_d = dram_in("b_f", [8, 1])
    lam_d = dram_in("lam_bc", [128, 256])
    gsub_d = dram_in("g_subln_bc", [128, 128])
    wo_d = dram_in("w_o", [D, D])
    wpq_d = dram_in("w_pq", [D, D])
    sk_d = dram_in("skblk", [128, 256])
    u_d = dram_in("u_exp", [16384, D])
    v_d = dram_in("v_exp", [16384, D])
    ident_d = dram_in("ident", [128, 128])
    cmask_d = dram_in("cmask", [128, 128])
    cos_d = dram_in("rope_cos", [128, S])
    sin_d = dram_in("rope_sin", [128, S])
    rperm_d = dram_in("rpermT", [128, 128])
    iota_d = dram_in("iota16", [128, 16])
    y_d = nc.dram_tensor("y", [S, D], F32, kind="ExternalOutput").ap()
    uvb_d = nc.dram_tensor("uv_bf16", [16384, 2 * D], BF16, kind="Internal").ap()
    B_cv = Buf("cv")
    CVR = 1024
    cv_list = [(src, off, c) for (src, off) in ((u_d, 0), (v_d, D)) for c in range(16384 // CVR)]
    cv_pos = [0]

    def convert_some(n):
        for _ in range(n):
            if cv_pos[0] >= len(cv_list):
                return
            src, off, c = cv_list[cv_pos[0]]
            cv_pos[0] += 1
            P.dma("pool", (lambda e, src=src, off=off, c=c: e.dma_start(out=uvb_d[c * CVR:(c + 1) * CVR, off:off + D],
                                                                       in_=src[c * CVR:(c + 1) * CVR, :])),
                  "cv", W=[B_cv])
    dbg_d = {}

    def dbg_out(name, shape, dt=F32):
        dbg_d[name] = nc.dram_tensor("dbg_" + name, list(shape), dt, kind="ExternalOutput").ap()
        return dbg_d[name]

    A = Alloc(nc)
    LO = 16512
    KB = 1024
    zc = A.zone(LO, 26 * KB)
    z1 = A.zone(zc["end"], 98 * KB)
    z2 = A.zone(z1["end"], 48 * KB)
    z3 = A.zone(z2["end"], 229376 - z2["end"])
    assert z3["end"] - z3["start"] >= 34 * KB, z3

    ps = [nc.alloc_psum_tensor("ps%d" % i, [128, 512], F32) for i in range(8)]
    psb = [Buf("ps%d" % i) for i in range(8)]

    ident_f = A.alloc(zc, "ident_f", [128, 128], F32)
    ident_b = A.alloc(zc, "ident_b", [128, 128], BF16)
    ones_f = A.alloc(zc, "ones_f", [128, 128], F32)
    cmask_f = A.alloc(zc, "cmask_f", [128, 128], F32)
    maskneg = A.alloc(zc, "maskneg", [128, 128], F32)
    modT = A.alloc(zc, "modT", [128, 48], F32)
    scl1T = A.alloc(zc, "scl1T", [128, 8], F32)
    scl2T = A.alloc(zc, "scl2T", [128, 8], F32)
    gt1_bc = A.alloc(zc, "gt1_bc", [128, D], F32)
    gt2_bc = A.alloc(zc, "gt2_bc", [128, D], F32)
    sc2_bc = A.alloc(zc, "sc2_bc", [128, D], F32)
    sh2_bc = A.alloc(zc, "sh2_bc", [128, D], F32)
    gfin = A.alloc(zc, "gfin", [128, D], F32)
    lam_t = A.alloc(zc, "lam_t", [128, 256], F32)
    lam_s = A.alloc(zc, "lam_s", [128, 8], F32)
    gsub = A.alloc(zc, "gsub", [128, 128], F32)
    smallc = A.alloc(zc, "smallc", [128, 64], F32)
    iota16 = A.alloc(zc, "iota16", [128, 16], F32)
    B_const = Buf("const")
    B_mod = Buf("mod")
    B_bc = Buf("bc")

    cT = A.alloc(z3, "cT", [128, 8], F32)
    scT2 = A.alloc(z3, "scT2", [128, 8, 2], F32)
    badaT = A.alloc(z3, "badaT", [128, 48], F32)
    gattnT = A.alloc(z3, "gattnT", [128, 8], F32)
    gffnT = A.alloc(z3, "gffnT", [128, 8], F32)
    sc1_bc = A.alloc(z3, "sc1_bc", [128, D], F32)
    sh1_bc = A.alloc(z3, "sh1_bc", [128, D], F32)
    diag = [A.alloc(z3, "diag%d" % i, [128, 128], F32) for i in range(2)]
    diag_b = [Buf("diag%d" % i) for i in range(2)]
    WCOLS = 256
    wst = [A.alloc(z3, "wst%d" % i, [128, 8, WCOLS], F32) for i in range(2)]
    wst_b = [Buf("wst%d" % i) for i in range(2)]

    qs = "sync"
    for (dst, src) in ((ident_f, ident_d), (cmask_f, cmask_d), (cT, cT_d), (badaT, badaT_d),
                       (gattnT, gattnT_d), (gffnT, gffnT_d), (gfin, gfin_d), (lam_t, lam_d),
                       (gsub, gsub_d), (iota16, iota_d)):
        P.dma(qs, (lambda e, d=dst, s=src: e.dma_start(out=d[:], in_=s)), "c0", W=[B_const])
    P.op("pool", lambda e: e.memset(ones_f[:], 1.0), W=[B_const])
    P.op("dve", lambda e: e.tensor_copy(out=ident_b[:], in_=ident_f[:]), R=[B_const], W=[B_const])
    P.op("dve", lambda e: e.tensor_scalar(out=maskneg[:], in0=cmask_f[:], scalar1=-1.0, scalar2=30000.0, op0=ALU.add, op1=ALU.mult),
         R=[B_const], W=[B_const])
    B_sc = Buf("scT")
    P.op("act", lambda e: e.activation(out=scT2[:, :, 0], in_=cT[:], func=AF.Silu), R=[B_const], W=[B_sc])
    P.op("act", lambda e: e.activation(out=scT2[:, :, 1], in_=cT[:], func=AF.Silu), R=[B_const], W=[B_sc])
    B_lam = Buf("lam")
    junk64 = smallc[:, 0:64]
    P.op("dve", lambda e: e.scalar_tensor_tensor(out=junk64, in0=lam_t[:, 0:64], scalar=1.0, in1=lam_t[:, 64:128],
                                                 op0=ALU.mult, op1=ALU.mult, accum_out=lam_s[:, 0:1]),
         R=[B_const], W=[B_lam])
    P.op("dve", lambda e: e.scalar_tensor_tensor(out=junk64, in0=lam_t[:, 128:192], scalar=1.0, in1=lam_t[:, 192:256],
                                                 op0=ALU.mult, op1=ALU.mult, accum_out=lam_s[:, 1:2]),
         R=[B_const, B_lam], W=[B_lam])
    P.op("act", lambda e: e.activation(out=lam_s[:, 2:4], in_=lam_s[:, 0:2], func=AF.Exp), R=[B_lam], W=[B_lam])
    lam_init = 0.8 - 0.6 * math.exp(-0.3 * 0)
    P.op("dve", lambda e: e.scalar_tensor_tensor(out=lam_s[:, 5:6], in0=lam_s[:, 3:4], scalar=-lam_init,
                                                 in1=lam_s[:, 2:3], op0=ALU.add, op1=ALU.subtract),
         R=[B_lam], W=[B_lam])

    ps_mod = ps[0]
    wada_v = wada_d.rearrange("(kc p) n -> p kc n", p=128)
    NGRP = 6 * D // WCOLS
    for g in range(NGRP):
        sl = g % 2
        P.dma("sync", (lambda e, sl=sl, g=g: e.dma_start(out=wst[sl][:], in_=wada_v[:, :, g * WCOLS:(g + 1) * WCOLS])),
              "wst%d" % sl, W=[wst_b[sl]])
        for cc in range(WCOLS // 128):
            j = g * (WCOLS // 128) + cc
            for kc in range(8):
                P.op("pe", (lambda e, sl=sl, cc=cc, kc=kc, j=j: e.matmul(
                    ps_mod[:, 2 * j:2 * j + 2], lhsT=wst[sl][:, kc, cc * 128:(cc + 1) * 128],
                    rhs=scT2[:, kc, :], start=(kc == 0), stop=(kc == 7))),
                    R=[wst_b[sl], B_sc], W=[psb[0]])
    pm_v = ps_mod[:, 0:96].rearrange("p (j t) -> p j t", t=2)
    P.op("dve", lambda e: e.tensor_tensor(out=modT[:], in0=pm_v[:, :, 0], in1=badaT[:], op=ALU.add),
         R=[psb[0], B_const], W=[B_mod])
    P.op("dve", lambda e: e.scalar_tensor_tensor(out=scl1T[:], in0=modT[:, 8:16], scalar=1.0, in1=gattnT[:],
                                                 op0=ALU.add, op1=ALU.mult), R=[B_mod, B_const], W=[B_mod])
    P.op("dve", lambda e: e.scalar_tensor_tensor(out=scl2T[:], in0=modT[:, 32:40], scalar=1.0, in1=gffnT[:],
                                                 op0=ALU.add, op1=ALU.mult), R=[B_mod, B_const], W=[B_mod])
    bc_list = ((sc1_bc, scl1T, 0), (sh1_bc, modT, 0), (gt1_bc, modT, 16),
               (sc2_bc, scl2T, 0), (sh2_bc, modT, 24), (gt2_bc, modT, 40))
    n_d = 0
    for bi, (dst, srcT, c0) in enumerate(bc_list):
        for half in range(2):
            pb = 1 + (bi * 2 + half) % 2
            for jj in range(4):
                j = half * 4 + jj
                dsl = n_d % 2
                n_d += 1
                P.op("dve", (lambda e, dsl=dsl, srcT=srcT, col=c0 + j: e.tensor_scalar(
                    out=diag[dsl][:], in0=ident_f[:], scalar1=srcT[:, col:col + 1], scalar2=None, op0=ALU.mult)),
                    R=[B_mod, B_const], W=[diag_b[dsl]])
                P.op("pe", (lambda e, dsl=dsl, pb=pb, jj=jj: e.matmul(
                    ps[pb][:, jj * 128:(jj + 1) * 128], lhsT=ones_f[:], rhs=diag[dsl][:], start=True, stop=True)),
                    R=[diag_b[dsl], B_const], W=[psb[pb]])
            P.op("act", (lambda e, dst=dst, pb=pb, half=half: e.copy(out=dst[:, half * 512:(half + 1) * 512], in_=ps[pb][:])),
                 R=[psb[pb]], W=[B_bc])

    if dbg:
        o = dbg_out("modT", [128, 48])
        P.dma("sync", lambda e, o=o: e.dma_start(out=o, in_=modT[:]), "dbg", R=[B_mod])
        o2 = dbg_out("gt1_bc", [128, D])
        P.dma("sync", lambda e, o2=o2: e.dma_start(out=o2, in_=gt1_bc[:]), "dbg", R=[B_bc])
        o3 = dbg_out("lam", [128, 8])
        P.dma("sync", lambda e, o3=o3: e.dma_start(out=o3, in_=lam_s[:]), "dbg", R=[B_lam])

    hT = A.alloc(z1, "hT", [128, 8, S], BF16)
    hT_b = [Buf("hT%d" % i) for i in range(NT)]
    xst = [A.alloc(z1, "xst%d" % i, [128, D], F32) for i in range(2)]
    xst_b = [Buf("xst%d" % i) for i in range(2)]
    htmp = [A.alloc(z1, "htmp%d" % i, [128, D], F32) for i in range(2)]
    htmp_b = [Buf() for _ in range(2)]
    hbt = [A.alloc(z1, "hbt%d" % i, [128, D], BF16) for i in range(2)]
    hbt_b = [Buf() for _ in range(2)]
    sq_junk = A.alloc(z1, "sq_junk", [128, D], BF16)
    nstat = A.alloc(z1, "nstat", [128, 4 * NT], F32)
    nstat_b = [Buf() for _ in range(NT)]
    x_t = x_d.rearrange("(t p) d -> t p d", p=128)
    B_junk = Buf("junk")

    NCTX = {"nstat": nstat, "nstat_b": nstat_b, "junk": sq_junk, "junk_b": B_junk}

    def norm_tile(i, src_ap, src_buf, scale_bc, shift_bc, out_bf, out_bf_buf, tmp, tmp_buf):
        nstat = NCTX["nstat"]
        nstat_b = NCTX["nstat_b"]
        sq_junk = NCTX["junk"]
        ss = nstat[:, 4 * i:4 * i + 1]
        var = nstat[:, 4 * i + 1:4 * i + 2]
        std = nstat[:, 4 * i + 2:4 * i + 3]
        rstd = nstat[:, 4 * i + 3:4 * i + 4]
        P.op("act", lambda e: e.activation(out=sq_junk[:], in_=src_ap, func=AF.Square, accum_out=ss),
             R=[src_buf], W=[NCTX["junk_b"], nstat_b[i]])
        P.op("dve", lambda e: e.tensor_scalar(out=var, in0=ss, scalar1=1.0 / D, scalar2=1e-6, op0=ALU.mult, op1=ALU.add),
             R=[nstat_b[i]], W=[nstat_b[i]])
        P.op("act", lambda e: e.activation(out=std, in_=var, func=AF.Sqrt), R=[nstat_b[i]], W=[nstat_b[i]])
        P.op("dve", lambda e: e.reciprocal(out=rstd, in_=std), R=[nstat_b[i]], W=[nstat_b[i]])
        P.op("dve", lambda e: e.scalar_tensor_tensor(out=tmp[:], in0=src_ap, scalar=rstd, in1=scale_bc[:],
                                                     op0=ALU.mult, op1=ALU.mult),
             R=[src_buf, nstat_b[i], B_bc, B_const], W=[tmp_buf])
        if out_bf is not None:
            P.op("pool", lambda e: e.tensor_tensor(out=out_bf[:], in0=tmp[:], in1=shift_bc[:], op=ALU.add),
                 R=[tmp_buf, B_bc], W=[out_bf_buf])

    def transpose_to(i, src_bf, src_buf, dstT, dst_buf, pbank, evac_eng):
        pv = ps[pbank][:].bitcast(BF16)
        for j in range(8):
            P.op("pe", (lambda e, j=j: e.transpose(pv[:, j * 128:(j + 1) * 128], src_bf[:, j * 128:(j + 1) * 128], ident_b[:])),
                 R=[src_buf, B_const], W=[psb[pbank]])
        pv3 = pv.rearrange("p (j t) -> p j t", t=128)
        if evac_eng == "act":
            P.op("act", lambda e: e.copy(out=dstT[:, :, i * 128:(i + 1) * 128], in_=pv3), R=[psb[pbank]], W=[dst_buf])
        else:
            P.op("dve", lambda e: e.tensor_copy(out=dstT[:, :, i * 128:(i + 1) * 128], in_=pv3), R=[psb[pbank]], W=[dst_buf])

    for i in range(NT):
        sl = i % 2
        P.dma("sync", (lambda e, sl=sl, i=i: e.dma_start(out=xst[sl][:], in_=x_t[i])), "xst%d" % sl, W=[xst_b[sl]])
        norm_tile(i, xst[sl][:], xst_b[sl], sc1_bc, sh1_bc, hbt[sl], hbt_b[sl], htmp[sl], htmp_b[sl])
        transpose_to(i, hbt[sl], hbt_b[sl], hT, hT_b[i], 3 + sl, "act" if sl == 0 else "dve")

    if dbg:
        o = dbg_out("hT", [128, 8, S], BF16)
        P.dma("sync", lambda e, o=o: e.dma_start(out=o, in_=hT[:]), "dbg", R=hT_b)

    if stage <= 1:
        return finish(nc, P, dbg_d)
    P.barrier()
    z1["cur"] = z1["start"] + 8 * S * 2
    z3["cur"] = z3["start"]

    mixedT = A.alloc(z2, "mixedT", [128, 8, S], BF16)
    mixedT_b = [Buf("mxT%d" % i) for i in range(NT)]
    wo_bf = A.alloc(z2, "wo_bf", [128, 8, D], BF16)
    B_wo = Buf("wo")
    wun = [A.alloc(z1, "wun%d" % i, [128, 8, 384], BF16) for i in range(2)]
    wun_b = [Buf() for _ in range(2)]
    qT = A.alloc(z1, "qT", [128, S], BF16)
    kT = A.alloc(z1, "kT", [128, S], BF16)
    qk_b = [Buf("qT"), Buf("kT")]
    VW = 130
    vtm = A.alloc(z1, "vtm", [128, NT, VW], BF16)
    v_b = Buf("v")
    FQ = A.alloc(z1, "FQ", [128, S], BF16)
    FK = A.alloc(z1, "FK", [128, S], BF16)
    F_b = [Buf("Fs%d" % i) for i in range(4)]
    pT = [A.alloc(z1, "pT%d" % i, [128, 512], BF16) for i in range(3)]
    pT_b = [Buf() for _ in range(3)]
    o1n = A.alloc(z1, "o1n", [128, NT, 128], F32)
    o1n_b = [Buf() for _ in range(NT)]
    ropet = [A.alloc(z1, "ropet%d" % i, [128, 2, 512], F32) for i in range(2)]
    ropet_b = [Buf() for _ in range(2)]
    qf = [A.alloc(z1, "qf%d" % i, [128, 512], F32) for i in range(2)]
    qf_b = [Buf() for _ in range(2)]
    qr = A.alloc(z1, "qr", [128, 512], F32)
    qr_b = Buf()
    epi = A.alloc(z1, "epi", [128, 16], F32)
    epi_b = Buf()
    otmp = [A.alloc(z1, "otmp%d" % i, [128, 128], F32) for i in range(2)]
    otmp_b = [Buf() for _ in range(2)]
    obf = [A.alloc(z1, "obf%d" % i, [128, 128], BF16) for i in range(4)]
    obf_b = [Buf() for _ in range(4)]
    tr_pending = []
    rperm = A.alloc(z1, "rperm", [128, 128], F32)
    B_rp = Buf()
    wfg_b16 = A.alloc(z1, "wfg", [128, 8, 8], BF16)
    bfv = A.alloc(z1, "bfv", [8, 1], F32)
    fgt = A.alloc(z3, "fgt", [8, S], F32)
    Fc = A.alloc(z3, "Fc", [8, S], F32)
    onesr = A.alloc(z3, "onesr", [8, S], BF16)
    fpc = [A.alloc(z3, "fpc%d" % i, [8, S], BF16) for i in range(3)]
    fres = fgt
    B_fg = Buf("fg")

    P.dma("sync", lambda e: e.dma_start(out=rperm[:], in_=rperm_d), "c1", W=[B_rp])
    P.dma("sync", lambda e: e.dma_start(out=bfv[:], in_=bf_d), "c1b", W=[B_fg])
    P.dma("pool", lambda e: e.dma_start(out=wfg_b16[:], in_=wfg_d.rearrange("(kc p) n -> p kc n", p=128)), "c2", W=[B_fg])
    P.dma("pool", lambda e: e.dma_start(out=wo_bf[:], in_=wo_d.rearrange("(kc p) n -> p kc n", p=128)), "wo", W=[B_wo])

    for G in range(NG):
        for kc in range(8):
            P.op("pe", (lambda e, G=G, kc=kc: e.matmul(ps[2][0:8, :], lhsT=wfg_b16[:, kc, :], rhs=hT[:, kc, G * 512:(G + 1) * 512],
                                                      start=(kc == 0), stop=(kc == 7))),
                 R=[B_fg] + hT_b[4 * G:4 * G + 4], W=[psb[2]])
        P.op("act", (lambda e, G=G: e.activation(out=fgt[:, G * 512:(G + 1) * 512], in_=ps[2][0:8, :], func=AF.Sigmoid,
                                                 bias=bfv[:, 0:1], scale=1.0)), R=[psb[2], B_fg], W=[B_fg])
    P.op("act", lambda e: e.activation(out=fgt[:], in_=fgt[:], func=AF.Ln), R=[B_fg], W=[B_fg])
    P.op("pool", lambda e: e.memset(onesr[:], 1.0), W=[B_fg])
    P.op("dve", lambda e: e.tensor_tensor_scan(out=Fc[:], data0=onesr[:], data1=fgt[:], initial=0.0,
                                               op0=ALU.mult, op1=ALU.add), R=[B_fg], W=[B_fg])
    hi, mid, lo = fpc
    P.op("dve", lambda e: e.tensor_copy(out=hi[:], in_=Fc[:]), R=[B_fg], W=[B_fg])
    P.op("dve", lambda e: e.tensor_tensor(out=fres[:], in0=Fc[:], in1=hi[:], op=ALU.subtract), R=[B_fg], W=[B_fg])
    P.op("dve", lambda e: e.tensor_copy(out=mid[:], in_=fres[:]), R=[B_fg], W=[B_fg])
    P.op("dve", lambda e: e.tensor_tensor(out=fres[:], in0=fres[:], in1=mid[:], op=ALU.subtract), R=[B_fg], W=[B_fg])
    P.op("dve", lambda e: e.tensor_copy(out=lo[:], in_=fres[:]), R=[B_fg], W=[B_fg])
    if dbg:
        o = dbg_out("Fc", [8, S])
        P.dma("sync", lambda e, o=o: e.dma_start(out=o, in_=Fc[:]), "dbg", R=[B_fg])

    QT = [qT, FQ]
    KT = [kT, FK]
    qb = [Buf("q0"), Buf("q1")]
    kb = [Buf("k0"), Buf("k1")]

    def build_faug(u):
        for hh in range(2):
            h = 2 * u + hh
            P.op("pool", (lambda e, hh=hh: e.memset(QT[hh][64:96, :], -1.0)), W=[qb[hh]])
            P.op("pool", (lambda e, hh=hh: e.memset(KT[hh][64:96, :], 1.0)), W=[kb[hh]])
            for r in range(3):
                P.dma("sync", (lambda e, hh=hh, r=r, h=h: e.dma_start(out=QT[hh][64 + r:65 + r, :], in_=fpc[r][h:h + 1, :])),
                      "faq%d" % hh, R=[B_fg], W=[qb[hh]])
                P.dma("sync", (lambda e, hh=hh, r=r, h=h: e.dma_start(out=KT[hh][67 + r:68 + r, :], in_=fpc[r][h:h + 1, :])),
                      "fak%d" % hh, R=[B_fg], W=[kb[hh]])

    n_rope = [0]
    n_qf = [0]
    n_pt = [0]
    n_o = [0]

    def project_unit(u, wsl):
        fox = u < 4
        w = wun[wsl]
        if fox:
            for hh in range(2):
                for which, dst, dbuf, c0 in ((0, QT[hh], qb[hh], 0), (1, KT[hh], kb[hh], 128)):
                    for G in range(NG):
                        pb = 6 + (G % 2)
                        for kc in range(8):
                            P.op("pe", (lambda e, pb=pb, kc=kc, cc=c0 + 64 * hh, G=G: e.matmul(
                                ps[pb][0:64, :], lhsT=w[:, kc, cc:cc + 64], rhs=hT[:, kc, G * 512:(G + 1) * 512],
                                start=(kc == 0), stop=(kc == 7))),
                                R=[wun_b[wsl]] + hT_b[4 * G:4 * G + 4], W=[psb[pb]])
                        sc = 0.125 if which == 0 else 1.0
                        P.op("act", (lambda e, pb=pb, dst=dst, G=G, sc=sc: e.activation(
                            out=dst[0:64, G * 512:(G + 1) * 512], in_=ps[pb][0:64, :], func=AF.Copy, scale=sc)),
                            R=[psb[pb]], W=[dbuf])
        else:
            for which, dstT, dbuf, c0 in ((0, qT, qb[0], 0), (1, kT, kb[0], 128)):
                for G in range(NG):
                    pb = 6 + (G % 2)
                    for kc in range(8):
                        P.op("pe", (lambda e, pb=pb, kc=kc, c0=c0, G=G: e.matmul(
                            ps[pb][:], lhsT=w[:, kc, c0:c0 + 128], rhs=hT[:, kc, G * 512:(G + 1) * 512],
                            start=(kc == 0), stop=(kc == 7))),
                            R=[wun_b[wsl]] + hT_b[4 * G:4 * G + 4], W=[psb[pb]])
                    sc = 0.125 if which == 0 else 1.0
                    rs = n_rope[0] % 2
                    n_rope[0] += 1
                    P.dma("sync", (lambda e, rs=rs, G=G: e.dma_start(out=ropet[rs][:, 0, :], in_=cos_d[:, G * 512:(G + 1) * 512])),
                          "rope%d" % rs, W=[ropet_b[rs]])
                    P.dma("sync", (lambda e, rs=rs, G=G: e.dma_start(out=ropet[rs][:, 1, :], in_=sin_d[:, G * 512:(G + 1) * 512])),
                          "rope%d" % rs, W=[ropet_b[rs]])
                    fs = n_qf[0] % 2
                    n_qf[0] += 1
                    P.op("act", (lambda e, pb=pb, fs=fs: e.copy(out=qf[fs][:], in_=ps[pb][:])), R=[psb[pb]], W=[qf_b[fs]])
                    P.op("pe", (lambda e, fs=fs: e.matmul(ps[2][:], lhsT=rperm[:], rhs=qf[fs][:], start=True, stop=True)),
                         R=[B_rp, qf_b[fs]], W=[psb[2]])
                    P.op("dve", (lambda e, rs=rs, sc=sc: e.scalar_tensor_tensor(out=qr[:], in0=ps[2][:], scalar=sc, in1=ropet[rs][:, 1, :],
                                                                          op0=ALU.mult, op1=ALU.mult)),
                         R=[psb[2], ropet_b[rs]], W=[qr_b])
                    P.op("pool", (lambda e, fs=fs, rs=rs: e.tensor_tensor(out=qf[fs][:], in0=qf[fs][:], in1=ropet[rs][:, 0, :], op=ALU.mult)),
                         R=[ropet_b[rs]], W=[qf_b[fs]])
                    P.op("dve", (lambda e, fs=fs, dstT=dstT, G=G, sc=sc: e.scalar_tensor_tensor(
                        out=dstT[:, G * 512:(G + 1) * 512], in0=qf[fs][:], scalar=sc, in1=qr[:], op0=ALU.mult, op1=ALU.add)),
                        R=[qf_b[fs], qr_b], W=[dbuf])
        P.op("pool", lambda e: e.memset(vtm[:], 1.0), W=[v_b])
        for i in range(NT):
            pb = 6 + (i % 2)
            for kc in range(8):
                P.op("pe", (lambda e, pb=pb, kc=kc, i=i: e.matmul(
                    ps[pb][:, 0:128], lhsT=hT[:, kc, i * 128:(i + 1) * 128], rhs=w[:, kc, 256:384],
                    start=(kc == 0), stop=(kc == 7))),
                    R=[wun_b[wsl], hT_b[i]], W=[psb[pb]])
            if fox:
                vout = vtm[:, i, :].rearrange("p (h c) -> p h c", c=65)[:, :, 0:64]
                vin = ps[pb][:, 0:128].rearrange("p (h c) -> p h c", c=64)
            else:
                vout = vtm[:, i, 0:128]
                vin = ps[pb][:, 0:128]
            if i % 2 == 0:
                P.op("act", (lambda e, vout=vout, vin=vin: e.copy(out=vout, in_=vin)), R=[psb[pb]], W=[v_b])
            else:
                P.op("dve", (lambda e, vout=vout, vin=vin: e.tensor_copy(out=vout, in_=vin)), R=[psb[pb]], W=[v_b])

    def attention_unit(u):
        fox = u < 4
        blocks = [(c, G, kt) for c in range(2) for G in range(NG) for kt in range(4 * G + 4)]

        def emit_qk(bi):
            c, G, kt = blocks[bi]
            sb = bi % 2
            p0 = 64 * c
            lo = max(kt - 4 * G, 0) * 128
            if fox:
                P.op("pe", (lambda e, sb=sb, kt=kt, G=G, c=c, lo=lo: e.matmul(
                    ps[sb][:, lo:512], lhsT=KT[c][0:70, kt * 128:(kt + 1) * 128], rhs=QT[c][0:70, G * 512 + lo:(G + 1) * 512],
                    start=True, stop=True)), R=[qb[c], kb[c]], W=[psb[sb]])
            else:
                P.op("pe", (lambda e, sb=sb, kt=kt, G=G, p0=p0, lo=lo: e.matmul(
                    ps[sb][:, lo:512], lhsT=kT[p0:p0 + 64, kt * 128:(kt + 1) * 128], rhs=qT[p0:p0 + 64, G * 512 + lo:(G + 1) * 512],
                    start=True, stop=True)), R=[qb[0], kb[0]], W=[psb[sb]])

        emit_qk(0)
        for bi, (c, G, kt) in enumerate(blocks):
            sb = bi % 2
            if bi + 1 < len(blocks):
                emit_qk(bi + 1)
            pt = n_pt[0] % 3
            n_pt[0] += 1
            r = kt - 4 * G
            c_lo = max(r, 0) * 128
            if r >= 0:
                P.op("dve", (lambda e, sb=sb, r=r: e.tensor_tensor(out=ps[sb][:, r * 128:(r + 1) * 128],
                                                                   in0=ps[sb][:, r * 128:(r + 1) * 128], in1=maskneg[:], op=ALU.add)),
                     R=[B_const], W=[psb[sb]])
            P.op("act", (lambda e, sb=sb, pt=pt, c_lo=c_lo: e.activation(out=pT[pt][:, c_lo:512], in_=ps[sb][:, c_lo:512], func=AF.Exp)),
                 R=[psb[sb]], W=[pT_b[pt]])
            for qq in range(max(r, 0), 4):
                qt = 4 * G + qq
                ob = 2 + qq
                if fox:
                    rhs = vtm[:, kt, 0:65] if c == 0 else vtm[:, kt, 65:130]
                    ow = 65
                else:
                    rhs = vtm[:, kt, 0:129]
                    ow = 129
                P.op("pe", (lambda e, ob=ob, pt=pt, qq=qq, rhs=rhs, ow=ow, kt=kt, qt=qt: e.matmul(
                    ps[ob][:, 0:ow], lhsT=pT[pt][:, qq * 128:(qq + 1) * 128], rhs=rhs,
                    start=(kt == 0), stop=(kt == qt))), R=[pT_b[pt], v_b], W=[psb[ob]])
            if tr_pending and kt == 2:
                for f in tr_pending:
                    f()
                del tr_pending[:]
            if kt == 4 * G + 3:
                convert_some(1)
                for qq in range(4):
                    epilogue(u, c, 4 * G + qq, 2 + qq)
        for f in tr_pending:
            f()
        del tr_pending[:]

    def epilogue(u, c, qt, ob):
        fox = u < 4
        osl = n_o[0] % 2
        n_o[0] += 1
        if fox:
            rc = epi[:, 2 * c:2 * c + 1]
            P.op("dve", (lambda e, ob=ob, rc=rc: e.reciprocal(out=rc, in_=ps[ob][:, 64:65])), R=[psb[ob]], W=[epi_b])
            P.op("act", (lambda e, ob=ob, rc=rc, qt=qt, c=c: e.activation(
                out=fo_acc[:, qt, c * 64:(c + 1) * 64], in_=ps[ob][:, 0:64], func=AF.Copy, scale=rc)),
                R=[psb[ob], epi_b], W=[fo_b[qt]])
            if c == 1:
                def tr(u=u, qt=qt):
                    P.op("pe", (lambda e, qt=qt: e.transpose(ps[7][:].bitcast(BF16)[:, 0:128], fo_acc[:, qt, :], ident_b[:])),
                         R=[fo_b[qt], B_const], W=[psb[7]])
                    P.op("dve", (lambda e, u=u, qt=qt: e.tensor_copy(out=mixedT[:, u, qt * 128:(qt + 1) * 128],
                                                                   in_=ps[7][:].bitcast(BF16)[:, 0:128])),
                         R=[psb[7]], W=[mixedT_b[qt]])
                tr_pending.append(tr)
        else:
            if c == 0:
                rc = epi[:, 4:5]
                P.op("dve", (lambda e, ob=ob, rc=rc: e.reciprocal(out=rc, in_=ps[ob][:, 128:129])), R=[psb[ob]], W=[epi_b])
                P.op("act", (lambda e, ob=ob, rc=rc, qt=qt: e.activation(out=o1n[:, qt, :], in_=ps[ob][:, 0:128], func=AF.Copy, scale=rc)),
                     R=[psb[ob], epi_b], W=[o1n_b[qt]])
            else:
                rc = epi[:, 5:6]
                nl = epi[:, 6:7]
                P.op("dve", (lambda e, ob=ob, rc=rc: e.reciprocal(out=rc, in_=ps[ob][:, 128:129])), R=[psb[ob]], W=[epi_b])
                P.op("dve", (lambda e, rc=rc, nl=nl: e.tensor_tensor(out=nl, in0=rc, in1=lam_s[:, 5:6], op=ALU.mult)),
                     R=[epi_b, B_lam], W=[epi_b])
                P.op("dve", (lambda e, ob=ob, nl=nl, qt=qt, osl=osl: e.scalar_tensor_tensor(
                    out=otmp[osl][:], in0=ps[ob][:, 0:128], scalar=nl, in1=o1n[:, qt, :], op0=ALU.mult, op1=ALU.add)),
                    R=[psb[ob], epi_b, o1n_b[qt]], W=[otmp_b[osl]])
                ss = epi[:, 8:9]
                var = epi[:, 9:10]
                std = epi[:, 10:11]
                rs = epi[:, 11:12]
                os2 = qt % 4
                P.op("act", (lambda e, osl=osl, os2=os2, ss=ss: e.activation(out=obf[os2][:], in_=otmp[osl][:], func=AF.Square, accum_out=ss)),
                     R=[otmp_b[osl]], W=[obf_b[os2], epi_b])
                P.op("dve", (lambda e, ss=ss, var=var: e.tensor_scalar(out=var, in0=ss, scalar1=1.0 / 128, scalar2=1e-5,
                                                                      op0=ALU.mult, op1=ALU.add)), R=[epi_b], W=[epi_b])
                P.op("act", (lambda e, var=var, std=std: e.activation(out=std, in_=var, func=AF.Sqrt)), R=[epi_b], W=[epi_b])
                P.op("dve", (lambda e, std=std, rs=rs: e.reciprocal(out=rs, in_=std)), R=[epi_b], W=[epi_b])
                P.op("dve", (lambda e, osl=osl, rs=rs: e.scalar_tensor_tensor(
                    out=otmp[osl][:], in0=otmp[osl][:], scalar=rs, in1=gsub[:], op0=ALU.mult, op1=ALU.mult)),
                    R=[epi_b, B_const], W=[otmp_b[osl]])
                P.op("act", (lambda e, osl=osl, os2=os2: e.activation(out=obf[os2][:], in_=otmp[osl][:], func=AF.Copy, scale=1.0 - lam_init)),
                     R=[otmp_b[osl]], W=[obf_b[os2]])

                def tr(u=u, qt=qt, os2=os2):
                    P.op("pe", (lambda e, os2=os2: e.transpose(ps[7][:].bitcast(BF16)[:, 0:128], obf[os2][:], ident_b[:])),
                         R=[obf_b[os2], B_const], W=[psb[7]])
                    P.op("dve", (lambda e, u=u, qt=qt: e.tensor_copy(out=mixedT[:, u, qt * 128:(qt + 1) * 128],
                                                                   in_=ps[7][:].bitcast(BF16)[:, 0:128])),
                         R=[psb[7]], W=[mixedT_b[qt]])
                tr_pending.append(tr)

    fo_acc = A.alloc(z1, "fo_acc", [128, NT, 128], BF16)
    fo_b = [Buf() for _ in range(NT)]

    units = list(range(nunits))
    def load_wun(ui):
        wsl = ui % 2
        u = units[ui]
        P.dma("pool", (lambda e, wsl=wsl, u=u: e.dma_start(out=wun[wsl][:], in_=wun_d[u].rearrange("(kc p) n -> p kc n", p=128))),
              "wun%d" % wsl, W=[wun_b[wsl]])

    if units:
        load_wun(0)
    for ui, u in enumerate(units):
        wsl = ui % 2
        project_unit(u, wsl)
        if ui + 1 < len(units):
            load_wun(ui + 1)
        if u < 4:
            build_faug(u)
        attention_unit(u)

    if dbg:
        for nm, t, shp in (("FQ", FQ, [128, S]), ("FK", FK, [128, S]), ("vtm", vtm, [128, NT, VW]), ("fo_acc", fo_acc, [128, NT, 128])):
            oo = dbg_out(nm, shp, BF16)
            P.dma("sync", (lambda e, oo=oo, t=t: e.dma_start(out=oo, in_=t[:])), "dbg", R=[qb[1], kb[1], v_b] + fo_b)
        o = dbg_out("mixedT", [128, 8, S], BF16)
        P.dma("sync", lambda e, o=o: e.dma_start(out=o, in_=mixedT[:]), "dbg", R=mixedT_b)
    if stage <= 2:
        return finish(nc, P, dbg_d)

    P.barrier()
    z1["cur"] = z1["start"]
    xn = A.alloc(z1, "xn", [128, NT, D], F32)
    xn_b = [Buf("xn%d" % i) for i in range(NT)]
    wtmp = [A.alloc(z1, "wtmp%d" % i, [128, 512], F32) for i in range(2)]
    wtmp_b = [Buf() for _ in range(2)]
    nw = 0
    for i in range(NT):
        P.dma("sync", (lambda e, i=i: e.dma_start(out=xn[:, i, :], in_=x_t[i])), "xn%d" % i, W=[xn_b[i]])
        for half in range(2):
            pb = 2 * (i % 2) + half
            for kc in range(8):
                P.op("pe", (lambda e, pb=pb, kc=kc, i=i, half=half: e.matmul(
                    ps[pb][:], lhsT=mixedT[:, kc, i * 128:(i + 1) * 128], rhs=wo_bf[:, kc, half * 512:(half + 1) * 512],
                    start=(kc == 0), stop=(kc == 7))), R=[mixedT_b[i], B_wo], W=[psb[pb]])
            ws = nw % 2
            nw += 1
            P.op("dve", (lambda e, pb=pb, ws=ws, half=half: e.tensor_tensor(
                out=wtmp[ws][:], in0=ps[pb][:], in1=gt1_bc[:, half * 512:(half + 1) * 512], op=ALU.mult)),
                R=[psb[pb], B_bc], W=[wtmp_b[ws]])
            P.op("pool", (lambda e, ws=ws, i=i, half=half: e.tensor_tensor(
                out=xn[:, i, half * 512:(half + 1) * 512], in0=xn[:, i, half * 512:(half + 1) * 512], in1=wtmp[ws][:], op=ALU.add)),
                R=[wtmp_b[ws]], W=[xn_b[i]])
    if dbg:
        o = dbg_out("x1", [S, D])
        P.dma("sync", (lambda e, o=o: e.dma_start(out=o.rearrange("(t p) d -> p t d", p=128), in_=xn[:])), "dbg", R=xn_b)
    if stage <= 3:
        return finish(nc, P, dbg_d)

    convert_some(len(cv_list))
    P.barrier()
    z2["cur"] = z2["start"]
    z3["cur"] = z3["start"]
    NS = 10
    GK = 1
    uvbuf = [A.alloc(z2, "uvbuf%d" % i, [128, 2 * D], BF16) for i in range(NS)]
    uv_b = [Buf() for _ in range(NS)]
    comb = A.alloc(z2, "comb", [128, 8, 256], F32)
    wpq_bf = A.alloc(z3, "wpq_bf", [128, 8, D], BF16)
    B_wpq = Buf()
    oh = A.alloc(z3, "oh", [128, 8, 256], F32)
    prod = A.alloc(z3, "prod", [128, 8, 256], F32)
    skb = A.alloc(z3, "skb", [128, 256], F32)
    B_sk = Buf()
    h2f = A.alloc(z1, "h2f", [128, D], F32)
    h2b = A.alloc(z1, "h2b", [128, D], BF16)
    h2T = A.alloc(z1, "h2T", [128, 8, 128], BF16)
    qTf = A.alloc(z1, "qTf", [128, 8, 128], F32)
    sc = prod[:].rearrange("p h (t k) -> p (h t) k", k=128)
    sc2 = oh[:].rearrange("p h (t k) -> p (h t) k", k=128)
    junkb = h2b
    junk2 = A.alloc(z3, "junk2", [128, D], BF16)
    B_junk2 = Buf("junk2")
    gateB = A.alloc(z1, "gateB", [128, 128], F32)
    m16 = A.alloc(z1, "m16", [128, 16, 16], F32)
    i16 = A.alloc(z1, "i16", [128, 16, 16], U32)
    i16f = A.alloc(z1, "i16f", [128, 16, 16], F32)
    t16 = A.alloc(z1, "t16", [128, 8, 16], F32)
    ci = A.alloc(z1, "ci", [128, 8, 16], U32)
    ca = A.alloc(z1, "ca", [128, 8, 16], U32)
    cb = A.alloc(z1, "cb", [128, 8, 16], U32)
    caf = A.alloc(z1, "caf", [128, 8, 16], F32)
    cbf = A.alloc(z1, "cbf", [128, 8, 16], F32)
    e1 = A.alloc(z1, "e1", [128, 8, 16], F32)
    e2 = A.alloc(z1, "e2", [128, 8, 16], F32)
    eif = A.alloc(z1, "eif", [128, 128], F32)
    eidx = A.alloc(z1, "eidx", [128, 128], U32)
    gex = A.alloc(z1, "gex", [128, 8, 16], F32)
    gsum = A.alloc(z1, "gsum", [128, 8], F32)
    gate = A.alloc(z1, "gate", [128, 128], F32)
    a_acc = A.alloc(z1, "a_acc", [128, 128], F32)
    wgt = A.alloc(z1, "wgt", [128, 128], F32)
    dgb = [A.alloc(z1, "dgb%d" % i, [128, 128], BF16) for i in range(2)]
    dgb_b = [Buf() for _ in range(2)]
    nstat2 = A.alloc(z1, "nstat2", [128, 8 * NT], F32)
    ytile = [A.alloc(z1, "ytile", [128, D], F32)] * 2
    ytile_b = [Buf()] * 2
    NCTX["nstat"] = nstat2
    NCTX["nstat_b"] = [Buf() for _ in range(2 * NT)]
    NCTX["junk"] = junkb
    Bt = {k: Buf(k) for k in ("h2f", "h2b", "h2T", "qTf", "sc", "sc2", "m16", "i16", "comb", "t16", "ci", "cab", "oh", "prod",
                              "e12", "eidx", "g", "a", "wgt", "htmp")}
    NCTX["junk_b"] = Bt["h2b"]
    Bt["m16b"] = Buf("m16b")
    Bt["t16b"] = Buf("t16b")
    Bt["sc"] = Bt["prod"]
    Bt["sc2"] = Bt["oh"]
    P.dma("pool", lambda e: e.dma_start(out=wpq_bf[:], in_=wpq_d.rearrange("(kc p) n -> p kc n", p=128)), "wpq", W=[B_wpq])
    P.dma("sync", lambda e: e.dma_start(out=skb[:], in_=sk_d), "skb", W=[B_sk])
    y_t = y_d.rearrange("(t p) d -> t p d", p=128)
    m16v = m16[:].rearrange("p (h t) r -> p h t r", t=2)
    i16fv = i16f[:].rearrange("p (h t) r -> p h t r", t=2)
    cn = {"ngu": 0, "ngv": 0, "ncast": 0, "ndg": 0, "nw": nw}
    eidxB = A.alloc(z1, "eidxB", [128, 128], U32)
    wgtB = A.alloc(z1, "wgtB", [128, 128], F32)
    eidx2 = [eidx, eidxB]
    h2f2 = [h2f, gt1_bc]
    h2f_b = [Buf("h2f0"), Buf("h2f1")]
    gate2 = [gate, gateB]
    gate_b = [Buf("gate0"), Buf("gate1")]
    wgt2 = [wgt, wgtB]
    eidx_b = [Buf(), Buf()]
    wgt_b = [Buf(), Buf()]

    def prepA(i):
        norm_tile(i, xn[:, i, :], xn_b[i], sc2_bc, sh2_bc, h2f2[i % 2], h2f_b[i % 2], h2f2[i % 2], h2f_b[i % 2])
        P.op("act", lambda e: e.copy(out=h2b[:], in_=h2f2[i % 2][:]), R=[h2f_b[i % 2]], W=[Bt["h2b"]])
        pv = ps[0][:].bitcast(BF16)
        for j in range(8):
            P.op("pe", (lambda e, j=j, pv=pv: e.transpose(pv[:, j * 128:(j + 1) * 128], h2b[:, j * 128:(j + 1) * 128], ident_b[:])),
                 R=[Bt["h2b"], B_const], W=[psb[0]])
        P.op("act", (lambda e, pv=pv: e.copy(out=h2T[:], in_=pv.rearrange("p (j t) -> p j t", t=128))), R=[psb[0]], W=[Bt["h2T"]])
        for hb in range(2):
            pb = 1 + hb
            for hh in range(4):
                h = hb * 4 + hh
                for kc in range(8):
                    P.op("pe", (lambda e, pb=pb, hh=hh, h=h, kc=kc: e.matmul(
                        ps[pb][:, hh * 128:(hh + 1) * 128], lhsT=wpq_bf[:, kc, h * 128:(h + 1) * 128], rhs=h2T[:, kc, :],
                        start=(kc == 0), stop=(kc == 7))), R=[B_wpq, Bt["h2T"]], W=[psb[pb]])
            P.op("act", (lambda e, pb=pb, hb=hb: e.copy(out=qTf[:, hb * 4:(hb + 1) * 4, :],
                                                       in_=ps[pb][:].rearrange("p (h t) -> p h t", t=128))), R=[psb[pb]], W=[Bt["qTf"]])
        for hp in range(4):
            pb = 2 + hp
            for hh in range(2):
                h = hp * 2 + hh
                P.op("pe", (lambda e, pb=pb, hh=hh, h=h: e.matmul(ps[pb][:, hh * 256:(hh + 1) * 256], lhsT=qTf[:, h, :], rhs=skb[:],
                                                                 start=True, stop=True)), R=[Bt["qTf"], B_sk], W=[psb[pb]])
            P.op("act", (lambda e, pb=pb, hp=hp: e.copy(out=sc[:, hp * 4:(hp + 1) * 4, :],
                                                       in_=ps[pb][:].rearrange("p (g t) -> p g t", t=128))), R=[psb[pb]], W=[Bt["sc"]])

    def prepB(i):
        for sg in range(16):
            P.op("dve", (lambda e, sg=sg: e.max(out=m16[:, sg, 0:8], in_=sc[:, sg, :])), R=[Bt["sc"]], W=[Bt["m16"]], waw_ok=True)
        for sg in range(16):
            P.op("dve", (lambda e, sg=sg: e.match_replace(out=sc2[:, sg, :], in_to_replace=m16[:, sg, 0:8], in_values=sc[:, sg, :], imm_value=NEG)),
                 R=[Bt["sc"], Bt["m16"]], W=[Bt["sc2"]], waw_ok=True)
        for sg in range(16):
            P.op("dve", (lambda e, sg=sg: e.max(out=m16[:, sg, 8:16], in_=sc2[:, sg, :])), R=[Bt["sc2"]], W=[Bt["m16b"]], waw_ok=True)
        for sg in range(16):
            P.op("dve", (lambda e, sg=sg: e.max_index(out=i16[:, sg, 0:8], in_max=m16[:, sg, 0:8], in_values=sc[:, sg, :])),
                 R=[Bt["sc"], Bt["m16"]], W=[Bt["i16"]], waw_ok=True)
        for sg in range(16):
            P.op("dve", (lambda e, sg=sg: e.max_index(out=i16[:, sg, 8:16], in_max=m16[:, sg, 8:16], in_values=sc2[:, sg, :])),
                 R=[Bt["sc2"], Bt["m16b"]], W=[Bt["i16"]], waw_ok=True)
        P.op("pool", lambda e: e.tensor_copy(out=i16f[:], in_=i16[:]), R=[Bt["i16"]], W=[Bt["i16"]])
        P.op("pool", lambda e: e.tensor_tensor(out=comb[:].rearrange("p h (a b) -> p h a b", b=16),
                                               in0=m16v[:, :, 0, :].unsqueeze(3).to_broadcast([128, 8, 16, 16]),
                                               in1=m16v[:, :, 1, :].unsqueeze(2).to_broadcast([128, 8, 16, 16]), op=ALU.add),
             R=[Bt["m16"], Bt["m16b"]], W=[Bt["comb"]])
        for h in range(8):
            P.op("dve", (lambda e, h=h: e.max(out=t16[:, h, 0:8], in_=comb[:, h, :])), R=[Bt["comb"]], W=[Bt["t16"]], waw_ok=True)
        for h in range(8):
            P.op("dve", (lambda e, h=h: e.max_index(out=ci[:, h, 0:8], in_max=t16[:, h, 0:8], in_values=comb[:, h, :])),
                 R=[Bt["comb"], Bt["t16"]], W=[Bt["ci"]], waw_ok=True)
        for h in range(8):
            P.op("dve", (lambda e, h=h: e.match_replace(out=comb[:, h, :], in_to_replace=t16[:, h, 0:8], in_values=comb[:, h, :], imm_value=NEG)),
                 R=[Bt["t16"], Bt["ci"]], W=[Bt["comb"]], waw_ok=True)
        for h in range(8):
            P.op("dve", (lambda e, h=h: e.max(out=t16[:, h, 8:16], in_=comb[:, h, :])), R=[Bt["comb"]], W=[Bt["t16b"]], waw_ok=True)
        for h in range(8):
            P.op("dve", (lambda e, h=h: e.max_index(out=ci[:, h, 8:16], in_max=t16[:, h, 8:16], in_values=comb[:, h, :])),
                 R=[Bt["comb"], Bt["t16b"]], W=[Bt["ci"]], waw_ok=True)
        P.op("dve", lambda e: e.tensor_single_scalar(out=ca[:], in_=ci[:], scalar=4, op=ALU.logical_shift_right), R=[Bt["ci"]], W=[Bt["cab"]])
        P.op("dve", lambda e: e.tensor_single_scalar(out=cb[:], in_=ci[:], scalar=15, op=ALU.bitwise_and), R=[Bt["ci"]], W=[Bt["cab"]], waw_ok=True)
        iota_b = iota16[:].unsqueeze(1).unsqueeze(1).to_broadcast([128, 8, 16, 16])
        for (cf, half, eo) in ((ca, 0, e1), (cb, 1, e2)):
            P.op("dve", (lambda e, cf=cf: e.tensor_tensor(out=oh[:].rearrange("p h (r a) -> p h r a", a=16),
                                                         in0=cf[:].unsqueeze(3).to_broadcast([128, 8, 16, 16]), in1=iota_b, op=ALU.is_equal)),
                 R=[Bt["cab"], B_const], W=[Bt["oh"]])
            P.op("pool", (lambda e, half=half: e.tensor_tensor(out=prod[:].rearrange("p h (r a) -> p h r a", a=16),
                                                             in0=oh[:].rearrange("p h (r a) -> p h r a", a=16),
                                                             in1=i16fv[:, :, half, :].unsqueeze(2).to_broadcast([128, 8, 16, 16]), op=ALU.mult)),
                 R=[Bt["oh"], Bt["i16"]], W=[Bt["prod"]])
            P.op("dve", (lambda e, eo=eo: e.tensor_reduce(out=eo[:], in_=prod[:].rearrange("p h (r a) -> p h r a", a=16), axis=AX.X, op=ALU.add)),
                 R=[Bt["prod"]], W=[Bt["e12"]])

    def prepC(i):
        P.op("dve", lambda e: e.scalar_tensor_tensor(out=eidx2[i % 2][:], in0=e1[:].rearrange("p h r -> p (h r)"), scalar=128.0,
                                                     in1=e2[:].rearrange("p h r -> p (h r)"), op0=ALU.mult, op1=ALU.add),
             R=[Bt["e12"]], W=[eidx_b[i % 2]])
        P.op("dve", lambda e: e.tensor_tensor(out=gex[:], in0=t16[:], in1=t16[:, :, 0:1].to_broadcast([128, 8, 16]), op=ALU.subtract),
             R=[Bt["t16"], Bt["t16b"]], W=[Bt["g"]])
        P.op("act", lambda e: e.activation(out=gex[:], in_=gex[:], func=AF.Exp), R=[Bt["g"]], W=[Bt["g"]])
        P.op("dve", lambda e: e.tensor_reduce(out=gsum[:], in_=gex[:], axis=AX.X, op=ALU.add), R=[Bt["g"]], W=[Bt["g"]])
        P.op("dve", lambda e: e.reciprocal(out=gsum[:], in_=gsum[:]), R=[Bt["g"]], W=[Bt["g"]])
        P.op("dve", lambda e: e.tensor_tensor(out=gate2[i % 2][:].rearrange("p (h r) -> p h r", r=16), in0=gex[:],
                                              in1=gsum[:].unsqueeze(2).to_broadcast([128, 8, 16]), op=ALU.mult), R=[Bt["g"]], W=[gate_b[i % 2]])

    ak_b = [Buf() for _ in range(128 // GK)]
    wk_b = [Buf() for _ in range(128 // GK)]
    slot_of = {}

    def ghead(i, g):
        for k in range(g * GK, (g + 1) * GK):
            s_ = cn['ngu'] % NS
            cn['ngu'] += 1
            slot_of[(i, k)] = s_
            P.dma("pool", (lambda e, s_=s_, k=k, i=i: e.indirect_dma_start(
                out=uvbuf[s_][:], out_offset=None, in_=uvb_d, in_offset=bass.IndirectOffsetOnAxis(ap=eidx2[i % 2][:, k:k + 1], axis=0))),
                "g%d" % s_, R=[eidx_b[i % 2], B_cv], W=[uv_b[s_]], skip=("dve", "dma:g%d" % s_), extra=[eidx_b[i % 2].w])
            P.op("dve", (lambda e, s_=s_, k=k, i=i: e.scalar_tensor_tensor(out=junk2[:], in0=uvbuf[s_][:, 0:D], scalar=1.0, in1=h2f2[i % 2][:],
                                                                       op0=ALU.mult, op1=ALU.mult, accum_out=a_acc[:, k:k + 1])),
                 R=[uv_b[s_], h2f_b[i % 2]], W=[B_junk2, ak_b[g]],
                 extra=([("act", P.cnt["act"])] if (g == 0 and k == 0) else []))
        P.op("act", (lambda e, g=g, i=i: e.activation(out=wgt2[i % 2][:, g * GK:(g + 1) * GK], in_=a_acc[:, g * GK:(g + 1) * GK], func=AF.Gelu)),
             R=[ak_b[g]], W=[wk_b[g]])

    def gtail(i, g):
        P.op("dve", (lambda e, g=g, i=i: e.tensor_tensor(out=wgt2[i % 2][:, g * GK:(g + 1) * GK], in0=wgt2[i % 2][:, g * GK:(g + 1) * GK],
                                                        in1=gate2[i % 2][:, g * GK:(g + 1) * GK], op=ALU.mult)), R=[gate_b[i % 2]], W=[wk_b[g]])
        for k in range(g * GK, (g + 1) * GK):
            s_ = slot_of[(i, k)]
            ds_ = cn['ndg'] % 2
            cn['ndg'] += 1
            P.op("act", (lambda e, ds_=ds_, k=k, i=i: e.activation(out=dgb[ds_][:], in_=ident_b[:], func=AF.Copy, scale=wgt2[i % 2][:, k:k + 1])),
                 R=[wk_b[g], B_const], W=[dgb_b[ds_]])
            for half in range(2):
                P.op("pe", (lambda e, ds_=ds_, s_=s_, half=half, k=k: e.matmul(
                    ps[6 + half][:], lhsT=dgb[ds_][:], rhs=uvbuf[s_][:, D + half * 512:D + (half + 1) * 512],
                    start=(k == 0), stop=(k == 127))), R=[dgb_b[ds_], uv_b[s_]], W=[psb[6 + half]], skip=("dma:g%d" % s_,))

    def fin(i):
        for half in range(2):
            ws = cn['nw'] % 2
            cn['nw'] += 1
            P.op("dve", (lambda e, ws=ws, half=half: e.tensor_tensor(out=wtmp[ws][:], in0=ps[6 + half][:],
                                                                    in1=gt2_bc[:, half * 512:(half + 1) * 512], op=ALU.mult)),
                 R=[psb[6 + half], B_bc], W=[wtmp_b[ws]])
            P.op("pool", (lambda e, ws=ws, i=i, half=half: e.tensor_tensor(
                out=xn[:, i, half * 512:(half + 1) * 512], in0=xn[:, i, half * 512:(half + 1) * 512], in1=wtmp[ws][:], op=ALU.add)),
                R=[wtmp_b[ws]], W=[xn_b[i]])
        ys = i % 2
        norm_tile(NT + i, xn[:, i, :], xn_b[i], gfin, gfin, None, None, ytile[ys], ytile_b[ys])
        P.dma("sync", (lambda e, ys=ys, i=i: e.dma_start(out=y_t[i], in_=ytile[ys][:])), "y0", R=[ytile_b[ys]])


    NGRP_K = 128 // GK
    GA, GB, GC = 8, 56, 64
    prepA(0)
    prepB(0)
    prepC(0)
    for i in range(NT):
        for g in range(NGRP_K):
            ghead(i, g)
            if g >= 1:
                gtail(i, g - 1)
            if i + 1 < NT:
                if g == GA:
                    prepA(i + 1)
                if g == GB:
                    prepB(i + 1)
                if g == GC:
                    prepC(i + 1)
        gtail(i, NGRP_K - 1)
        fin(i)

    return finish(nc, P, dbg_d)


def finish(nc, P, dbg_d):
    from contextlib import ExitStack
    P.barrier()
    P.op("act", lambda e: e.nop())
    with ExitStack() as stack:
        P.emit(nc, stack)
    return nc, dbg_d


def _fm(vec):
    v = np.asarray(vec, np.float32).reshape(-1, 128)
    return np.ascontiguousarray(v.T)


def host_consts(S):
    ident = np.eye(128, dtype=np.float32)
    kk = np.arange(128)[:, None]
    qq = np.arange(128)[None, :]
    cmask = (qq >= kk).astype(np.float32)
    pos = np.arange(S, dtype=np.float32)
    inv = (500000.0 ** (-np.arange(0, 16, 2, dtype=np.float32) / 16.0)).astype(np.float32)
    ang = pos[None, :] * inv[:, None]
    cos = np.ones((128, S), np.float32)
    sin = np.zeros((128, S), np.float32)
    rp = np.zeros((128, 128), np.float32)
    for blk in range(2):
        b = 64 * blk
        for j in range(8):
            cos[b + j] = np.cos(ang[j])
            cos[b + 8 + j] = np.cos(ang[j])
            sin[b + j] = np.sin(ang[j])
            sin[b + 8 + j] = np.sin(ang[j])
            rp[b + j, b + 8 + j] = -1.0
            rp[b + 8 + j, b + j] = 1.0
    iota16 = np.tile(np.arange(16, dtype=np.float32)[None, :], (128, 1))
    return dict(ident=ident, cmask=cmask, rope_cos=cos, rope_sin=sin,
                rpermT=np.ascontiguousarray(rp.T), iota16=iota16)


def host_shared(inp, S):
    w_in = np.asarray(inp["w_in"], np.float32)[0]
    fq, fk, fv = w_in[:, 0:512], w_in[:, 512:1024], w_in[:, 1024:1536]
    fg = w_in[:, 1536:1544]
    dq, dk, dv = w_in[:, 1544:2056], w_in[:, 2056:2568], w_in[:, 2568:3080]
    units = []
    for u in range(4):
        sl = slice(u * 128, (u + 1) * 128)
        units.append(np.concatenate([fq[:, sl], fk[:, sl], fv[:, sl]], axis=1))
    for d in range(4):
        sl = slice(d * 128, (d + 1) * 128)
        units.append(np.concatenate([dq[:, sl], dk[:, sl], dv[:, sl]], axis=1))
    w_units = np.ascontiguousarray(np.stack(units, 0))
    sk = np.asarray(inp["sub_keys"], np.float32)[0]
    skblk = np.zeros((128, 256), np.float32)
    skblk[0:64, 0:128] = sk[0].T
    skblk[64:128, 128:256] = sk[1].T
    lam = np.concatenate([np.asarray(inp[k], np.float32)[0] for k in ("lambda_q1", "lambda_k1", "lambda_q2", "lambda_k2")])
    sh = dict(
        w_ada=np.ascontiguousarray(np.asarray(inp["w_ada"], np.float32)[0]),
        b_adaT=_fm(np.asarray(inp["b_ada"])[0]),
        g_attnT=_fm(np.asarray(inp["g_attn"])[0]),
        g_ffnT=_fm(np.asarray(inp["g_ffn"])[0]),
        g_final_bc=np.ascontiguousarray(np.tile(np.asarray(inp["g_final"], np.float32)[None, :], (128, 1))),
        w_units=w_units,
        w_fg=np.ascontiguousarray(fg),
        b_f=np.ascontiguousarray(np.asarray(inp["b_f"], np.float32)[0].reshape(8, 1)),
        lam_bc=np.ascontiguousarray(np.tile(lam[None, :], (128, 1))),
        g_subln_bc=np.ascontiguousarray(np.tile(np.asarray(inp["g_subln"], np.float32)[0][None, :], (128, 1))),
        w_o=np.ascontiguousarray(np.asarray(inp["w_o"], np.float32)[0]),
        w_pq=np.ascontiguousarray(np.asarray(inp["w_pq"], np.float32)[0]),
        skblk=skblk,
        u_exp=np.ascontiguousarray(np.asarray(inp["u_experts"], np.float32)[0]),
        v_exp=np.ascontiguousarray(np.asarray(inp["v_experts"], np.float32)[0]),
    )
    sh.update(host_consts(S))
    return sh


def make_in_maps(inp, S, ncores):
    sh = host_shared(inp, S)
    x = np.asarray(inp["x"], np.float32)
    c = np.asarray(inp["c"], np.float32)
    maps = []
    for b in range(ncores):
        m = dict(sh)
        m["x"] = np.ascontiguousarray(x[b, :S])
        m["cT"] = _fm(c[b])
        maps.append(m)
    return maps


_CACHE = {}


def kernel(**inputs):
    S = SEQ
    if "nc" not in _CACHE:
        _CACHE["nc"] = build(NT=S // 128)[0]
    nc = _CACHE["nc"]
    maps = make_in_maps(inputs, S, NCORES)
    res = run_bass_kernel_spmd(nc, maps, core_ids=list(range(NCORES)))
    out = np.stack([np.asarray(r["y"], np.float32) for r in res.results], 0)
    return out
```

```python
import math
import numpy as np
import concourse.bass as bass
import concourse.mybir as mybir
from concourse.bass_utils import run_bass_kernel_spmd

F32 = mybir.dt.float32
BF16 = mybir.dt.bfloat16
U32 = mybir.dt.uint32
I32 = mybir.dt.int32
AF = mybir.ActivationFunctionType
ALU = mybir.AluOpType
AX = mybir.AxisListType

D = 1024
NCORES = 8
SEQ = 2048
NEG = -1.0e30


class Buf:
    __slots__ = ("name", "w", "r")

    def __init__(self, name=""):
        self.name = name
        self.w = None
        self.r = {}


class Prog:
    ENGS = ("sync", "act", "dve", "pool", "pe")

    def __init__(self):
        self.streams = {e: [] for e in self.ENGS}
        self.cnt = {e: 0 for e in self.ENGS}
        self.dma_tot = {}
        self.pending = {e: [] for e in self.ENGS}

    def _deps(self, R, W, extra, waw_eng=None):
        deps = []
        for b in R:
            if b.w is not None:
                deps.append(b.w)
        for b in W:
            if b.w is not None and b.w[0] != waw_eng:
                deps.append(b.w)
            for k, v in b.r.items():
                deps.append((k, v))
        for t in extra:
            if t is not None:
                deps.append(t)
        return deps

    def _mark(self, tok, R, W):
        for b in R:
            k, v = tok
            if b.r.get(k, 0) < v:
                b.r[k] = v
        for b in W:
            b.w = tok
            b.r = {}

    def op(self, eng, fn, R=(), W=(), extra=(), skip=(), waw_ok=False):
        deps = [d for d in self._deps(R, W, (), eng if waw_ok else None) if d[0] not in skip] + [t for t in extra if t is not None] + self.pending[eng]
        self.pending[eng] = []
        self.cnt[eng] += 1
        tok = (eng, self.cnt[eng])
        self.streams[eng].append((fn, deps, tok, None))
        self._mark(tok, R, W)
        return tok

    def dma(self, queue, fn, sem, R=(), W=(), extra=(), skip=()):
        deps = [d for d in self._deps(R, W, ()) if d[0] not in skip] + [t for t in extra if t is not None] + self.pending[queue]
        self.pending[queue] = []
        self.dma_tot[sem] = self.dma_tot.get(sem, 0) + 16
        tok = ("dma:" + sem, self.dma_tot[sem])
        self.streams[queue].append((fn, deps, tok, sem))
        self._mark(tok, R, W)
        return tok

    def barrier(self):
        toks = [(e, self.cnt[e]) for e in self.ENGS if e != "sync" and self.cnt[e] > 0]
        toks += [("dma:" + s, v) for s, v in self.dma_tot.items()]
        for e in self.ENGS:
            self.pending[e] = self.pending[e] + toks

    def emit(self, nc, stack):
        sems = {}
        for e in self.ENGS:
            if e != "sync":
                sems[e] = stack.enter_context(nc.semaphore("s_" + e))
        for s in self.dma_tot:
            sems["dma:" + s] = stack.enter_context(nc.semaphore("d_" + s))
        block = stack.enter_context(nc.Block())

        def run(ename):
            def body(eng):
                seen = {}
                for fn, deps, tok, dsem in self.streams[ename]:
                    need = {}
                    for k, v in deps:
                        if k == "pe" and ename == "pe":
                            continue
                        if need.get(k, 0) < v:
                            need[k] = v
                    for k, v in need.items():
                        if seen.get(k, 0) < v:
                            eng.wait_ge(sems[k], v)
                            seen[k] = v
                    ins = fn(eng)
                    if dsem is not None:
                        ins.then_inc(sems["dma:" + dsem], 16)
                    else:
                        ins.then_inc(sems[ename], 1)
            return body

        block.sync(run("sync"))
        block.scalar(run("act"))
        block.vector(run("dve"))
        block.gpsimd(run("pool"))
        block.tensor(run("pe"))


class Alloc:
    def __init__(self, nc, lo=16512, hi=229376):
        self.nc = nc
        self.lo = lo
        self.hi = hi
        self.n = 0

    def zone(self, start, size):
        return {"start": start, "end": start + size, "cur": start}

    def alloc(self, z, name, shape, dtype):
        esz = {F32: 4, BF16: 2, U32: 4, I32: 4}[dtype]
        nbytes = esz
        for s in shape[1:]:
            nbytes *= s
        nbytes = (nbytes + 63) // 64 * 64
        off = z["cur"]
        assert off + nbytes <= z["end"], (name, off, nbytes, z)
        assert off + nbytes <= self.hi
        z["cur"] = off + nbytes
        self.n += 1
        return self.nc.alloc_sbuf_tensor_at("%s_%d" % (name, self.n), list(shape), dtype, offset=off)


def build(NT=16, stage=99, dbg=False, nunits=8):
    S = NT * 128
    NG = NT // 4
    nc = bass.Bass("TRN2", target_bir_lowering=False)
    P = Prog()

    def dram_in(name, shape, dt=F32):
        return nc.dram_tensor(name, list(shape), dt, kind="ExternalInput").ap()

    x_d = dram_in("x", [S, D])
    cT_d = dram_in("cT", [128, 8])
    wada_d = dram_in("w_ada", [D, 6 * D])
    badaT_d = dram_in("b_adaT", [128, 48])
    gattnT_d = dram_in("g_attnT", [128, 8])
    gffnT_d = dram_in("g_ffnT", [128, 8])
    gfin_d = dram_in("g_final_bc", [128, D])
    wun_d = dram_in("w_units", [8, D, 384])
    wfg_d = dram_in("w_fg", [D, 8])
    bf_d = dram_in("b_f", [8, 1])
    lam_d = dram_in("lam_bc", [128, 256])
    gsub_d = dram_in("g_subln_bc", [128, 128])
    wo_d = dram_in("w_o", [D, D])
    wpq_d = dram_in("w_pq", [D, D])
    sk_d = dram_in("skblk", [128, 256])
    u_d = dram_in("u_exp", [16384, D])
    v_d = dram_in("v_exp", [16384, D])
    ident_d = dram_in("ident", [128, 128])
    cmask_d = dram_in("cmask", [128, 128])
    cos_d = dram_in("rope_cos", [128, S])
    sin_d = dram_in("rope_sin", [128, S])
    rperm_d = dram_in("rpermT", [128, 128])
    iota_d = dram_in("iota16", [128, 16])
    y_d = nc.dram_tensor("y", [S, D], F32, kind="ExternalOutput").ap()
    uvb_d = nc.dram_tensor("uv_bf16", [16384, 2 * D], BF16, kind="Internal").ap()
    B_cv = Buf("cv")
    CVR = 1024
    cv_list = [(src, off, c) for (src, off) in ((u_d, 0), (v_d, D)) for c in range(16384 // CVR)]
    cv_pos = [0]

    def convert_some(n):
        for _ in range(n):
            if cv_pos[0] >= len(cv_list):
                return
            src, off, c = cv_list[cv_pos[0]]
            cv_pos[0] += 1
            P.dma("pool", (lambda e, src=src, off=off, c=c: e.dma_start(out=uvb_d[c * CVR:(c + 1) * CVR, off:off + D],
                                                                       in_=src[c * CVR:(c + 1) * CVR, :])),
                  "cv", W=[B_cv])
    dbg_d = {}

    def dbg_out(name, shape, dt=F32):
        dbg_d[name] = nc.dram_tensor("dbg_" + name, list(shape), dt, kind="ExternalOutput").ap()
        return dbg_d[name]

    A = Alloc(nc)
    LO = 16512
    KB = 1024
    zc = A.zone(LO, 26 * KB)
    z1 = A.zone(zc["end"], 98 * KB)
    z2 = A.zone(z1["end"], 48 * KB)
    z3 = A.zone(z2["end"], 229376 - z2["end"])
    assert z3["end"] - z3["start"] >= 34 * KB, z3

    ps = [nc.alloc_psum_tensor("ps%d" % i, [128, 512], F32) for i in range(8)]
    psb = [Buf("ps%d" % i) for i in range(8)]

    ident_f = A.alloc(zc, "ident_f", [128, 128], F32)
    ident_b = A.alloc(zc, "ident_b", [128, 128], BF16)
    ones_f = A.alloc(zc, "ones_f", [128, 128], F32)
    cmask_f = A.alloc(zc, "cmask_f", [128, 128], F32)
    maskneg = A.alloc(zc, "maskneg", [128, 128], F32)
    modT = A.alloc(zc, "modT", [128, 48], F32)
    scl1T = A.alloc(zc, "scl1T", [128, 8], F32)
    scl2T = A.alloc(zc, "scl2T", [128, 8], F32)
    gt1_bc = A.alloc(zc, "gt1_bc", [128, D], F32)
    gt2_bc = A.alloc(zc, "gt2_bc", [128, D], F32)
    sc2_bc = A.alloc(zc, "sc2_bc", [128, D], F32)
    sh2_bc = A.alloc(zc, "sh2_bc", [128, D], F32)
    gfin = A.alloc(zc, "gfin", [128, D], F32)
    lam_t = A.alloc(zc, "lam_t", [128, 256], F32)
    lam_s = A.alloc(zc, "lam_s", [128, 8], F32)
    gsub = A.alloc(zc, "gsub", [128, 128], F32)
    smallc = A.alloc(zc, "smallc", [128, 64], F32)
    iota16 = A.alloc(zc, "iota16", [128, 16], F32)
    B_const = Buf("const")
    B_mod = Buf("mod")
    B_bc = Buf("bc")

    cT = A.alloc(z3, "cT", [128, 8], F32)
    scT2 = A.alloc(z3, "scT2", [128, 8, 2], F32)
    badaT = A.alloc(z3, "badaT", [128, 48], F32)
    gattnT = A.alloc(z3, "gattnT", [128, 8], F32)
    gffnT = A.alloc(z3, "gffnT", [128, 8], F32)
    sc1_bc = A.alloc(z3, "sc1_bc", [128, D], F32)
    sh1_bc = A.alloc(z3, "sh1_bc", [128, D], F32)
    diag = [A.alloc(z3, "diag%d" % i, [128, 128], F32) for i in range(2)]
    diag_b = [Buf("diag%d" % i) for i in range(2)]
    WCOLS = 256
    wst = [A.alloc(z3, "wst%d" % i, [128, 8, WCOLS], F32) for i in range(2)]
    wst_b = [Buf("wst%d" % i) for i in range(2)]

    qs = "sync"
    for (dst, src) in ((ident_f, ident_d), (cmask_f, cmask_d), (cT, cT_d), (badaT, badaT_d),
                       (gattnT, gattnT_d), (gffnT, gffnT_d), (gfin, gfin_d), (lam_t, lam_d),
                       (gsub, gsub_d), (iota16, iota_d)):
        P.dma(qs, (lambda e, d=dst, s=src: e.dma_start(out=d[:], in_=s)), "c0", W=[B_const])
    P.op("pool", lambda e: e.memset(ones_f[:], 1.0), W=[B_const])
    P.op("dve", lambda e: e.tensor_copy(out=ident_b[:], in_=ident_f[:]), R=[B_const], W=[B_const])
    P.op("dve", lambda e: e.tensor_scalar(out=maskneg[:], in0=cmask_f[:], scalar1=-1.0, scalar2=30000.0, op0=ALU.add, op1=ALU.mult),
         R=[B_const], W=[B_const])
    B_sc = Buf("scT")
    P.op("act", lambda e: e.activation(out=scT2[:, :, 0], in_=cT[:], func=AF.Silu), R=[B_const], W=[B_sc])
    P.op("act", lambda e: e.activation(out=scT2[:, :, 1], in_=cT[:], func=AF.Silu), R=[B_const], W=[B_sc])
    B_lam = Buf("lam")
    junk64 = smallc[:, 0:64]
    P.op("dve", lambda e: e.scalar_tensor_tensor(out=junk64, in0=lam_t[:, 0:64], scalar=1.0, in1=lam_t[:, 64:128],
                                                 op0=ALU.mult, op1=ALU.mult, accum_out=lam_s[:, 0:1]),
         R=[B_const], W=[B_lam])
    P.op("dve", lambda e: e.scalar_tensor_tensor(out=junk64, in0=lam_t[:, 128:192], scalar=1.0, in1=lam_t[:, 192:256],
                                                 op0=ALU.mult, op1=ALU.mult, accum_out=lam_s[:, 1:2]),
         R=[B_const, B_lam], W=[B_lam])
    P.op("act", lambda e: e.activation(out=lam_s[:, 2:4], in_=lam_s[:, 0:2], func=AF.Exp), R=[B_lam], W=[B_lam])
    lam_init = 0.8 - 0.6 * math.exp(-0.3 * 0)
    P.op("dve", lambda e: e.scalar_tensor_tensor(out=lam_s[:, 5:6], in0=lam_s[:, 3:4], scalar=-lam_init,
                                                 in1=lam_s[:, 2:3], op0=ALU.add, op1=ALU.subtract),
         R=[B_lam], W=[B_lam])

    ps_mod = ps[0]
    wada_v = wada_d.rearrange("(kc p) n -> p kc n", p=128)
    NGRP = 6 * D // WCOLS
    for g in range(NGRP):
        sl = g % 2
        P.dma("sync", (lambda e, sl=sl, g=g: e.dma_start(out=wst[sl][:], in_=wada_v[:, :, g * WCOLS:(g + 1) * WCOLS])),
              "wst%d" % sl, W=[wst_b[sl]])
        for cc in range(WCOLS // 128):
            j = g * (WCOLS // 128) + cc
            for kc in range(8):
                P.op("pe", (lambda e, sl=sl, cc=cc, kc=kc, j=j: e.matmul(
                    ps_mod[:, 2 * j:2 * j + 2], lhsT=wst[sl][:, kc, cc * 128:(cc + 1) * 128],
                    rhs=scT2[:, kc, :], start=(kc == 0), stop=(kc == 7))),
                    R=[wst_b[sl], B_sc], W=[psb[0]])
    pm_v = ps_mod[:, 0:96].rearrange("p (j t) -> p j t", t=2)
    P.op("dve", lambda e: e.tensor_tensor(out=modT[:], in0=pm_v[:, :, 0], in1=badaT[:], op=ALU.add),
         R=[psb[0], B_const], W=[B_mod])
    P.op("dve", lambda e: e.scalar_tensor_tensor(out=scl1T[:], in0=modT[:, 8:16], scalar=1.0, in1=gattnT[:],
                                                 op0=ALU.add, op1=ALU.mult), R=[B_mod, B_const], W=[B_mod])
    P.op("dve", lambda e: e.scalar_tensor_tensor(out=scl2T[:], in0=modT[:, 32:40], scalar=1.0, in1=gffnT[:],
                                                 op0=ALU.add, op1=ALU.mult), R=[B_mod, B_const], W=[B_mod])
    bc_list = ((sc1_bc, scl1T, 0), (sh1_bc, modT, 0), (gt1_bc, modT, 16),
               (sc2_bc, scl2T, 0), (sh2_bc, modT, 24), (gt2_bc, modT, 40))
    n_d = 0
    for bi, (dst, srcT, c0) in enumerate(bc_list):
        for half in range(2):
            pb = 1 + (bi * 2 + half) % 2
            for jj in range(4):
                j = half * 4 + jj
                dsl = n_d % 2
                n_d += 1
                P.op("dve", (lambda e, dsl=dsl, srcT=srcT, col=c0 + j: e.tensor_scalar(
                    out=diag[dsl][:], in0=ident_f[:], scalar1=srcT[:, col:col + 1], scalar2=None, op0=ALU.mult)),
                    R=[B_mod, B_const], W=[diag_b[dsl]])
                P.op("pe", (lambda e, dsl=dsl, pb=pb, jj=jj: e.matmul(
                    ps[pb][:, jj * 128:(jj + 1) * 128], lhsT=ones_f[:], rhs=diag[dsl][:], start=True, stop=True)),
                    R=[diag_b[dsl], B_const], W=[psb[pb]])
            P.op("act", (lambda e, dst=dst, pb=pb, half=half: e.copy(out=dst[:, half * 512:(half + 1) * 512], in_=ps[pb][:])),
                 R=[psb[pb]], W=[B_bc])

    if dbg:
        o = dbg_out("modT", [128, 48])
        P.dma("sync", lambda e, o=o: e.dma_start(out=o, in_=modT[:]), "dbg", R=[B_mod])
        o2 = dbg_out("gt1_bc", [128, D])
        P.dma("sync", lambda e, o2=o2: e.dma_start(out=o2, in_=gt1_bc[:]), "dbg", R=[B_bc])
        o3 = dbg_out("lam", [128, 8])
        P.dma("sync", lambda e, o3=o3: e.dma_start(out=o3, in_=lam_s[:]), "dbg", R=[B_lam])

    hT = A.alloc(z1, "hT", [128, 8, S], BF16)
    hT_b = [Buf("hT%d" % i) for i in range(NT)]
    xst = [A.alloc(z1, "xst%d" % i, [128, D], F32) for i in range(2)]
    xst_b = [Buf("xst%d" % i) for i in range(2)]
    htmp = [A.alloc(z1, "htmp%d" % i, [128, D], F32) for i in range(2)]
    htmp_b = [Buf() for _ in range(2)]
    hbt = [A.alloc(z1, "hbt%d" % i, [128, D], BF16) for i in range(2)]
    hbt_b = [Buf() for _ in range(2)]
    sq_junk = A.alloc(z1, "sq_junk", [128, D], BF16)
    nstat = A.alloc(z1, "nstat", [128, 4 * NT], F32)
    nstat_b = [Buf() for _ in range(NT)]
    x_t = x_d.rearrange("(t p) d -> t p d", p=128)
    B_junk = Buf("junk")

    NCTX = {"nstat": nstat, "nstat_b": nstat_b, "junk": sq_junk, "junk_b": B_junk}

    def norm_tile(i, src_ap, src_buf, scale_bc, shift_bc, out_bf, out_bf_buf, tmp, tmp_buf):
        nstat = NCTX["nstat"]
        nstat_b = NCTX["nstat_b"]
        sq_junk = NCTX["junk"]
        ss = nstat[:, 4 * i:4 * i + 1]
        var = nstat[:, 4 * i + 1:4 * i + 2]
        std = nstat[:, 4 * i + 2:4 * i + 3]
        rstd = nstat[:, 4 * i + 3:4 * i + 4]
        P.op("act", lambda e: e.activation(out=sq_junk[:], in_=src_ap, func=AF.Square, accum_out=ss),
             R=[src_buf], W=[NCTX["junk_b"], nstat_b[i]])
        P.op("dve", lambda e: e.tensor_scalar(out=var, in0=ss, scalar1=1.0 / D, scalar2=1e-6, op0=ALU.mult, op1=ALU.add),
             R=[nstat_b[i]], W=[nstat_b[i]])
        P.op("act", lambda e: e.activation(out=std, in_=var, func=AF.Sqrt), R=[nstat_b[i]], W=[nstat_b[i]])
        P.op("dve", lambda e: e.reciprocal(out=rstd, in_=std), R=[nstat_b[i]], W=[nstat_b[i]])
        P.op("dve", lambda e: e.scalar_tensor_tensor(out=tmp[:], in0=src_ap, scalar=rstd, in1=scale_bc[:],
                                                     op0=ALU.mult, op1=ALU.mult),
             R=[src_buf, nstat_b[i], B_bc, B_const], W=[tmp_buf])
        if out_bf is not None:
            P.op("pool", lambda e: e.tensor_tensor(out=out_bf[:], in0=tmp[:], in1=shift_bc[:], op=ALU.add),
                 R=[tmp_buf, B_bc], W=[out_bf_buf])

    def transpose_to(i, src_bf, src_buf, dstT, dst_buf, pbank, evac_eng):
        pv = ps[pbank][:].bitcast(BF16)
        for j in range(8):
            P.op("pe", (lambda e, j=j: e.transpose(pv[:, j * 128:(j + 1) * 128], src_bf[:, j * 128:(j + 1) * 128], ident_b[:])),
                 R=[src_buf, B_const], W=[psb[pbank]])
        pv3 = pv.rearrange("p (j t) -> p j t", t=128)
        if evac_eng == "act":
            P.op("act", lambda e: e.copy(out=dstT[:, :, i * 128:(i + 1) * 128], in_=pv3), R=[psb[pbank]], W=[dst_buf])
        else:
            P.op("dve", lambda e: e.tensor_copy(out=dstT[:, :, i * 128:(i + 1) * 128], in_=pv3), R=[psb[pbank]], W=[dst_buf])

    for i in range(NT):
        sl = i % 2
        P.dma("sync", (lambda e, sl=sl, i=i: e.dma_start(out=xst[sl][:], in_=x_t[i])), "xst%d" % sl, W=[xst_b[sl]])
        norm_tile(i, xst[sl][:], xst_b[sl], sc1_bc, sh1_bc, hbt[sl], hbt_b[sl], htmp[sl], htmp_b[sl])
        transpose_to(i, hbt[sl], hbt_b[sl], hT, hT_b[i], 3 + sl, "act" if sl == 0 else "dve")

    if dbg:
        o = dbg_out("hT", [128, 8, S], BF16)
        P.dma("sync", lambda e, o=o: e.dma_start(out=o, in_=hT[:]), "dbg", R=hT_b)

    if stage <= 1:
        return finish(nc, P, dbg_d)
    P.barrier()
    z1["cur"] = z1["start"] + 8 * S * 2
    z3["cur"] = z3["start"]

    mixedT = A.alloc(z2, "mixedT", [128, 8, S], BF16)
    mixedT_b = [Buf("mxT%d" % i) for i in range(NT)]
    wo_bf = A.alloc(z2, "wo_bf", [128, 8, D], BF16)
    B_wo = Buf("wo")
    wun = [A.alloc(z1, "wun%d" % i, [128, 8, 384], BF16) for i in range(2)]
    wun_b = [Buf() for _ in range(2)]
    qT = A.alloc(z1, "qT", [128, S], BF16)
    kT = A.alloc(z1, "kT", [128, S], BF16)
    qk_b = [Buf("qT"), Buf("kT")]
    VW = 130
    vtm = A.alloc(z1, "vtm", [128, NT, VW], BF16)
    v_b = Buf("v")
    FQ = A.alloc(z1, "FQ", [128, S], BF16)
    FK = A.alloc(z1, "FK", [128, S], BF16)
    F_b = [Buf("Fs%d" % i) for i in range(4)]
    pT = [A.alloc(z1, "pT%d" % i, [128, 512], BF16) for i in range(3)]
    pT_b = [Buf() for _ in range(3)]
    o1n = A.alloc(z1, "o1n", [128, NT, 128], F32)
    o1n_b = [Buf() for _ in range(NT)]
    ropet = [A.alloc(z1, "ropet%d" % i, [128, 2, 512], F32) for i in range(2)]
    ropet_b = [Buf() for _ in range(2)]
    qf = [A.alloc(z1, "qf%d" % i, [128, 512], F32) for i in range(2)]
    qf_b = [Buf() for _ in range(2)]
    qr = A.alloc(z1, "qr", [128, 512], F32)
    qr_b = Buf()
    epi = A.alloc(z1, "epi", [128, 16], F32)
    epi_b = Buf()
    otmp = [A.alloc(z1, "otmp%d" % i, [128, 128], F32) for i in range(2)]
    otmp_b = [Buf() for _ in range(2)]
    obf = [A.alloc(z1, "obf%d" % i, [128, 128], BF16) for i in range(4)]
    obf_b = [Buf() for _ in range(4)]
    tr_pending = []
    rperm = A.alloc(z1, "rperm", [128, 128], F32)
    B_rp = Buf()
    wfg_b16 = A.alloc(z1, "wfg", [128, 8, 8], BF16)
    bfv = A.alloc(z1, "bfv", [8, 1], F32)
    fgt = A.alloc(z3, "fgt", [8, S], F32)
    Fc = A.alloc(z3, "Fc", [8, S], F32)
    onesr = A.alloc(z3, "onesr", [8, S], BF16)
    fpc = [A.alloc(z3, "fpc%d" % i, [8, S], BF16) for i in range(3)]
    fres = fgt
    B_fg = Buf("fg")

    P.dma("sync", lambda e: e.dma_start(out=rperm[:], in_=rperm_d), "c1", W=[B_rp])
    P.dma("sync", lambda e: e.dma_start(out=bfv[:], in_=bf_d), "c1b", W=[B_fg])
    P.dma("pool", lambda e: e.dma_start(out=wfg_b16[:], in_=wfg_d.rearrange("(kc p) n -> p kc n", p=128)), "c2", W=[B_fg])
    P.dma("pool", lambda e: e.dma_start(out=wo_bf[:], in_=wo_d.rearrange("(kc p) n -> p kc n", p=128)), "wo", W=[B_wo])

    for G in range(NG):
        for kc in range(8):
            P.op("pe", (lambda e, G=G, kc=kc: e.matmul(ps[2][0:8, :], lhsT=wfg_b16[:, kc, :], rhs=hT[:, kc, G * 512:(G + 1) * 512],
                                                      start=(kc == 0), stop=(kc == 7))),
                 R=[B_fg] + hT_b[4 * G:4 * G + 4], W=[psb[2]])
        P.op("act", (lambda e, G=G: e.activation(out=fgt[:, G * 512:(G + 1) * 512], in_=ps[2][0:8, :], func=AF.Sigmoid,
                                                 bias=bfv[:, 0:1], scale=1.0)), R=[psb[2], B_fg], W=[B_fg])
    P.op("act", lambda e: e.activation(out=fgt[:], in_=fgt[:], func=AF.Ln), R=[B_fg], W=[B_fg])
    P.op("pool", lambda e: e.memset(onesr[:], 1.0), W=[B_fg])
    P.op("dve", lambda e: e.tensor_tensor_scan(out=Fc[:], data0=onesr[:], data1=fgt[:], initial=0.0,
                                               op0=ALU.mult, op1=ALU.add), R=[B_fg], W=[B_fg])
    hi, mid, lo = fpc
    P.op("dve", lambda e: e.tensor_copy(out=hi[:], in_=Fc[:]), R=[B_fg], W=[B_fg])
    P.op("dve", lambda e: e.tensor_tensor(out=fres[:], in0=Fc[:], in1=hi[:], op=ALU.subtract), R=[B_fg], W=[B_fg])
    P.op("dve", lambda e: e.tensor_copy(out=mid[:], in_=fres[:]), R=[B_fg], W=[B_fg])
    P.op("dve", lambda e: e.tensor_tensor(out=fres[:], in0=fres[:], in1=mid[:], op=ALU.subtract), R=[B_fg], W=[B_fg])
    P.op("dve", lambda e: e.tensor_copy(out=lo[:], in_=fres[:]), R=[B_fg], W=[B_fg])
    if dbg:
        o = dbg_out("Fc", [8, S])
        P.dma("sync", lambda e, o=o: e.dma_start(out=o, in_=Fc[:]), "dbg", R=[B_fg])

    QT = [qT, FQ]
    KT = [kT, FK]
    qb = [Buf("q0"), Buf("q1")]
    kb = [Buf("k0"), Buf("k1")]

    def build_faug(u):
        for hh in range(2):
            h = 2 * u + hh
            P.op("pool", (lambda e, hh=hh: e.memset(QT[hh][64:96, :], -1.0)), W=[qb[hh]])
            P.op("pool", (lambda e, hh=hh: e.memset(KT[hh][64:96, :], 1.0)), W=[kb[hh]])
            for r in range(3):
                P.dma("sync", (lambda e, hh=hh, r=r, h=h: e.dma_start(out=QT[hh][64 + r:65 + r, :], in_=fpc[r][h:h + 1, :])),
                      "faq%d" % hh, R=[B_fg], W=[qb[hh]])
                P.dma("sync", (lambda e, hh=hh, r=r, h=h: e.dma_start(out=KT[hh][67 + r:68 + r, :], in_=fpc[r][h:h + 1, :])),
                      "fak%d" % hh, R=[B_fg], W=[kb[hh]])

    n_rope = [0]
    n_qf = [0]
    n_pt = [0]
    n_o = [0]

    def project_unit(u, wsl):
        fox = u < 4
        w = wun[wsl]
        if fox:
            for hh in range(2):
                for which, dst, dbuf, c0 in ((0, QT[hh], qb[hh], 0), (1, KT[hh], kb[hh], 128)):
                    for G in range(NG):
                        pb = 6 + (G % 2)
                        for kc in range(8):
                            P.op("pe", (lambda e, pb=pb, kc=kc, cc=c0 + 64 * hh, G=G: e.matmul(
                                ps[pb][0:64, :], lhsT=w[:, kc, cc:cc + 64], rhs=hT[:, kc, G * 512:(G + 1) * 512],
                                start=(kc == 0), stop=(kc == 7))),
                                R=[wun_b[wsl]] + hT_b[4 * G:4 * G + 4], W=[psb[pb]])
                        sc = 0.125 if which == 0 else 1.0
                        P.op("act", (lambda e, pb=pb, dst=dst, G=G, sc=sc: e.activation(
                            out=dst[0:64, G * 512:(G + 1) * 512], in_=ps[pb][0:64, :], func=AF.Copy, scale=sc)),
                            R=[psb[pb]], W=[dbuf])
        else:
            for which, dstT, dbuf, c0 in ((0, qT, qb[0], 0), (1, kT, kb[0], 128)):
                for G in range(NG):
                    pb = 6 + (G % 2)
                    for kc in range(8):
                        P.op("pe", (lambda e, pb=pb, kc=kc, c0=c0, G=G: e.matmul(
                            ps[pb][:], lhsT=w[:, kc, c0:c0 + 128], rhs=hT[:, kc, G * 512:(G + 1) * 512],
                            start=(kc == 0), stop=(kc == 7))),
                            R=[wun_b[wsl]] + hT_b[4 * G:4 * G + 4], W=[psb[pb]])
                    sc = 0.125 if which == 0 else 1.0
                    rs = n_rope[0] % 2
                    n_rope[0] += 1
                    P.dma("sync", (lambda e, rs=rs, G=G: e.dma_start(out=ropet[rs][:, 0, :], in_=cos_d[:, G * 512:(G + 1) * 512])),
                          "rope%d" % rs, W=[ropet_b[rs]])
                    P.dma("sync", (lambda e, rs=rs, G=G: e.dma_start(out=ropet[rs][:, 1, :], in_=sin_d[:, G * 512:(G + 1) * 512])),
                          "rope%d" % rs, W=[ropet_b[rs]])
                    fs = n_qf[0] % 2
                    n_qf[0] += 1
                    P.op("act", (lambda e, pb=pb, fs=fs: e.copy(out=qf[fs][:], in_=ps[pb][:])), R=[psb[pb]], W=[qf_b[fs]])
                    P.op("pe", (lambda e, fs=fs: e.matmul(ps[2][:], lhsT=rperm[:], rhs=qf[fs][:], start=True, stop=True)),
                         R=[B_rp, qf_b[fs]], W=[psb[2]])
                    P.op("dve", (lambda e, rs=rs, sc=sc: e.scalar_tensor_tensor(out=qr[:], in0=ps[2][:], scalar=sc, in1=ropet[rs][:, 1, :],
                                                                          op0=ALU.mult, op1=ALU.mult)),
                         R=[psb[2], ropet_b[rs]], W=[qr_b])
                    P.op("pool", (lambda e, fs=fs, rs=rs: e.tensor_tensor(out=qf[fs][:], in0=qf[fs][:], in1=ropet[rs][:, 0, :], op=ALU.mult)),
                         R=[ropet_b[rs]], W=[qf_b[fs]])
                    P.op("dve", (lambda e, fs=fs, dstT=dstT, G=G, sc=sc: e.scalar_tensor_tensor(
                        out=dstT[:, G * 512:(G + 1) * 512], in0=qf[fs][:], scalar=sc, in1=qr[:], op0=ALU.mult, op1=ALU.add)),
                        R=[qf_b[fs], qr_b], W=[dbuf])
        P.op("pool", lambda e: e.memset(vtm[:], 1.0), W=[v_b])
        for i in range(NT):
            pb = 6 + (i % 2)
            for kc in range(8):
                P.op("pe", (lambda e, pb=pb, kc=kc, i=i: e.matmul(
                    ps[pb][:, 0:128], lhsT=hT[:, kc, i * 128:(i + 1) * 128], rhs=w[:, kc, 256:384],
                    start=(kc == 0), stop=(kc == 7))),
                    R=[wun_b[wsl], hT_b[i]], W=[psb[pb]])
            if fox:
                vout = vtm[:, i, :].rearrange("p (h c) -> p h c", c=65)[:, :, 0:64]
                vin = ps[pb][:, 0:128].rearrange("p (h c) -> p h c", c=64)
            else:
                vout = vtm[:, i, 0:128]
                vin = ps[pb][:, 0:128]
            if i % 2 == 0:
                P.op("act", (lambda e, vout=vout, vin=vin: e.copy(out=vout, in_=vin)), R=[psb[pb]], W=[v_b])
            else:
                P.op("dve", (lambda e, vout=vout, vin=vin: e.tensor_copy(out=vout, in_=vin)), R=[psb[pb]], W=[v_b])

    def attention_unit(u):
        fox = u < 4
        blocks = [(c, G, kt) for c in range(2) for G in range(NG) for kt in range(4 * G + 4)]

        def emit_qk(bi):
            c, G, kt = blocks[bi]
            sb = bi % 2
            p0 = 64 * c
            if fox:
                P.op("pe", (lambda e, sb=sb, kt=kt, G=G, c=c: e.matmul(
                    ps[sb][:], lhsT=KT[c][0:70, kt * 128:(kt + 1) * 128], rhs=QT[c][0:70, G * 512:(G + 1) * 512],
                    start=True, stop=True)), R=[qb[c], kb[c]], W=[psb[sb]])
            else:
                P.op("pe", (lambda e, sb=sb, kt=kt, G=G, p0=p0: e.matmul(
                    ps[sb][:], lhsT=kT[p0:p0 + 64, kt * 128:(kt + 1) * 128], rhs=qT[p0:p0 + 64, G * 512:(G + 1) * 512],
                    start=True, stop=True)), R=[qb[0], kb[0]], W=[psb[sb]])

        emit_qk(0)
        for bi, (c, G, kt) in enumerate(blocks):
            sb = bi % 2
            if bi + 1 < len(blocks):
                emit_qk(bi + 1)
            pt = n_pt[0] % 3
            n_pt[0] += 1
            r = kt - 4 * G
            c_lo = max(r, 0) * 128
            if r >= 0:
                P.op("dve", (lambda e, sb=sb, r=r: e.tensor_tensor(out=ps[sb][:, r * 128:(r + 1) * 128],
                                                                   in0=ps[sb][:, r * 128:(r + 1) * 128], in1=maskneg[:], op=ALU.add)),
                     R=[B_const], W=[psb[sb]])
            P.op("act", (lambda e, sb=sb, pt=pt, c_lo=c_lo: e.activation(out=pT[pt][:, c_lo:512], in_=ps[sb][:, c_lo:512], func=AF.Exp)),
                 R=[psb[sb]], W=[pT_b[pt]])
            for qq in range(max(r, 0), 4):
                qt = 4 * G + qq
                ob = 2 + qq
                if fox:
                    rhs = vtm[:, kt, 0:65] if c == 0 else vtm[:, kt, 65:130]
                    ow = 65
                else:
                    rhs = vtm[:, kt, 0:129]
                    ow = 129
                P.op("pe", (lambda e, ob=ob, pt=pt, qq=qq, rhs=rhs, ow=ow, kt=kt, qt=qt: e.matmul(
                    ps[ob][:, 0:ow], lhsT=pT[pt][:, qq * 128:(qq + 1) * 128], rhs=rhs,
                    start=(kt == 0), stop=(kt == qt))), R=[pT_b[pt], v_b], W=[psb[ob]])
            if tr_pending and kt == 2:
                for f in tr_pending:
                    f()
                del tr_pending[:]
            if kt == 4 * G + 3:
                convert_some(1)
                for qq in range(4):
                    epilogue(u, c, 4 * G + qq, 2 + qq)
        for f in tr_pending:
            f()
        del tr_pending[:]

    def epilogue(u, c, qt, ob):
        fox = u < 4
        osl = n_o[0] % 2
        n_o[0] += 1
        if fox:
            rc = epi[:, 2 * c:2 * c + 1]
            P.op("dve", (lambda e, ob=ob, rc=rc: e.reciprocal(out=rc, in_=ps[ob][:, 64:65])), R=[psb[ob]], W=[epi_b])
            P.op("act", (lambda e, ob=ob, rc=rc, qt=qt, c=c: e.activation(
                out=fo_acc[:, qt, c * 64:(c + 1) * 64], in_=ps[ob][:, 0:64], func=AF.Copy, scale=rc)),
                R=[psb[ob], epi_b], W=[fo_b[qt]])
            if c == 1:
                def tr(u=u, qt=qt):
                    P.op("pe", (lambda e, qt=qt: e.transpose(ps[7][:].bitcast(BF16)[:, 0:128], fo_acc[:, qt, :], ident_b[:])),
                         R=[fo_b[qt], B_const], W=[psb[7]])
                    P.op("dve", (lambda e, u=u, qt=qt: e.tensor_copy(out=mixedT[:, u, qt * 128:(qt + 1) * 128],
                                                                   in_=ps[7][:].bitcast(BF16)[:, 0:128])),
                         R=[psb[7]], W=[mixedT_b[qt]])
                tr_pending.append(tr)
        else:
            if c == 0:
                rc = epi[:, 4:5]
                P.op("dve", (lambda e, ob=ob, rc=rc: e.reciprocal(out=rc, in_=ps[ob][:, 128:129])), R=[psb[ob]], W=[epi_b])
                P.op("act", (lambda e, ob=ob, rc=rc, qt=qt: e.activation(out=o1n[:, qt, :], in_=ps[ob][:, 0:128], func=AF.Copy, scale=rc)),
                     R=[psb[ob], epi_b], W=[o1n_b[qt]])
            else:
                rc = epi[:, 5:6]
                nl = epi[:, 6:7]
                P.op("dve", (lambda e, ob=ob, rc=rc: e.reciprocal(out=rc, in_=ps[ob][:, 128:129])), R=[psb[ob]], W=[epi_b])
                P.op("dve", (lambda e, rc=rc, nl=nl: e.tensor_tensor(out=nl, in0=rc, in1=lam_s[:, 5:6], op=ALU.mult)),
                     R=[epi_b, B_lam], W=[epi_b])
                P.op("dve", (lambda e, ob=ob, nl=nl, qt=qt, osl=osl: e.scalar_tensor_tensor(
                    out=otmp[osl][:], in0=ps[ob][:, 0:128], scalar=nl, in1=o1n[:, qt, :], op0=ALU.mult, op1=ALU.add)),
                    R=[psb[ob], epi_b, o1n_b[qt]], W=[otmp_b[osl]])
                ss = epi[:, 8:9]
                var = epi[:, 9:10]
                std = epi[:, 10:11]
                rs = epi[:, 11:12]
                os2 = qt % 4
                P.op("act", (lambda e, osl=osl, os2=os2, ss=ss: e.activation(out=obf[os2][:], in_=otmp[osl][:], func=AF.Square, accum_out=ss)),
                     R=[otmp_b[osl]], W=[obf_b[os2], epi_b])
                P.op("dve", (lambda e, ss=ss, var=var: e.tensor_scalar(out=var, in0=ss, scalar1=1.0 / 128, scalar2=1e-5,
                                                                      op0=ALU.mult, op1=ALU.add)), R=[epi_b], W=[epi_b])
                P.op("act", (lambda e, var=var, std=std: e.activation(out=std, in_=var, func=AF.Sqrt)), R=[epi_b], W=[epi_b])
                P.op("dve", (lambda e, std=std, rs=rs: e.reciprocal(out=rs, in_=std)), R=[epi_b], W=[epi_b])
                P.op("dve", (lambda e, osl=osl, rs=rs: e.scalar_tensor_tensor(
                    out=otmp[osl][:], in0=otmp[osl][:], scalar=rs, in1=gsub[:], op0=ALU.mult, op1=ALU.mult)),
                    R=[epi_b, B_const], W=[otmp_b[osl]])
                P.op("act", (lambda e, osl=osl, os2=os2: e.activation(out=obf[os2][:], in_=otmp[osl][:], func=AF.Copy, scale=1.0 - lam_init)),
                     R=[otmp_b[osl]], W=[obf_b[os2]])

                def tr(u=u, qt=qt, os2=os2):
                    P.op("pe", (lambda e, os2=os2: e.transpose(ps[7][:].bitcast(BF16)[:, 0:128], obf[os2][:], ident_b[:])),
                         R=[obf_b[os2], B_const], W=[psb[7]])
                    P.op("dve", (lambda e, u=u, qt=qt: e.tensor_copy(out=mixedT[:, u, qt * 128:(qt + 1) * 128],
                                                                   in_=ps[7][:].bitcast(BF16)[:, 0:128])),
                         R=[psb[7]], W=[mixedT_b[qt]])
                tr_pending.append(tr)

    fo_acc = A.alloc(z1, "fo_acc", [128, NT, 128], BF16)
    fo_b = [Buf() for _ in range(NT)]

    units = list(range(nunits))
    def load_wun(ui):
        wsl = ui % 2
        u = units[ui]
        P.dma("pool", (lambda e, wsl=wsl, u=u: e.dma_start(out=wun[wsl][:], in_=wun_d[u].rearrange("(kc p) n -> p kc n", p=128))),
              "wun%d" % wsl, W=[wun_b[wsl]])

    if units:
        load_wun(0)
    for ui, u in enumerate(units):
        wsl = ui % 2
        project_unit(u, wsl)
        if ui + 1 < len(units):
            load_wun(ui + 1)
        if u < 4:
            build_faug(u)
        attention_unit(u)

    if dbg:
        for nm, t, shp in (("FQ", FQ, [128, S]), ("FK", FK, [128, S]), ("vtm", vtm, [128, NT, VW]), ("fo_acc", fo_acc, [128, NT, 128])):
            oo = dbg_out(nm, shp, BF16)
            P.dma("sync", (lambda e, oo=oo, t=t: e.dma_start(out=oo, in_=t[:])), "dbg", R=[qb[1], kb[1], v_b] + fo_b)
        o = dbg_out("mixedT", [128, 8, S], BF16)
        P.dma("sync", lambda e, o=o: e.dma_start(out=o, in_=mixedT[:]), "dbg", R=mixedT_b)
    if stage <= 2:
        return finish(nc, P, dbg_d)

    P.barrier()
    z1["cur"] = z1["start"]
    xn = A.alloc(z1, "xn", [128, NT, D], F32)
    xn_b = [Buf("xn%d" % i) for i in range(NT)]
    wtmp = [A.alloc(z1, "wtmp%d" % i, [128, 512], F32) for i in range(2)]
    wtmp_b = [Buf() for _ in range(2)]
    nw = 0
    for i in range(NT):
        P.dma("sync", (lambda e, i=i: e.dma_start(out=xn[:, i, :], in_=x_t[i])), "xn%d" % i, W=[xn_b[i]])
        for half in range(2):
            pb = 2 * (i % 2) + half
            for kc in range(8):
                P.op("pe", (lambda e, pb=pb, kc=kc, i=i, half=half: e.matmul(
                    ps[pb][:], lhsT=mixedT[:, kc, i * 128:(i + 1) * 128], rhs=wo_bf[:, kc, half * 512:(half + 1) * 512],
                    start=(kc == 0), stop=(kc == 7))), R=[mixedT_b[i], B_wo], W=[psb[pb]])
            ws = nw % 2
            nw += 1
            P.op("dve", (lambda e, pb=pb, ws=ws, half=half: e.tensor_tensor(
                out=wtmp[ws][:], in0=ps[pb][:], in1=gt1_bc[:, half * 512:(half + 1) * 512], op=ALU.mult)),
                R=[psb[pb], B_bc], W=[wtmp_b[ws]])
            P.op("pool", (lambda e, ws=ws, i=i, half=half: e.tensor_tensor(
                out=xn[:, i, half * 512:(half + 1) * 512], in0=xn[:, i, half * 512:(half + 1) * 512], in1=wtmp[ws][:], op=ALU.add)),
                R=[wtmp_b[ws]], W=[xn_b[i]])
    if dbg:
        o = dbg_out("x1", [S, D])
        P.dma("sync", (lambda e, o=o: e.dma_start(out=o.rearrange("(t p) d -> p t d", p=128), in_=xn[:])), "dbg", R=xn_b)
    if stage <= 3:
        return finish(nc, P, dbg_d)

    convert_some(len(cv_list))
    P.barrier()
    z2["cur"] = z2["start"]
    z3["cur"] = z3["start"]
    NS = 10
    GK = 1
    uvbuf = [A.alloc(z2, "uvbuf%d" % i, [128, 2 * D], BF16) for i in range(NS)]
    uv_b = [Buf() for _ in range(NS)]
    comb = A.alloc(z2, "comb", [128, 8, 256], F32)
    wpq_bf = A.alloc(z3, "wpq_bf", [128, 8, D], BF16)
    B_wpq = Buf()
    oh = A.alloc(z3, "oh", [128, 8, 256], F32)
    prod = A.alloc(z3, "prod", [128, 8, 256], F32)
    skb = A.alloc(z3, "skb", [128, 256], F32)
    B_sk = Buf()
    h2f = A.alloc(z1, "h2f", [128, D], F32)
    h2b = A.alloc(z1, "h2b", [128, D], BF16)
    h2T = A.alloc(z1, "h2T", [128, 8, 128], BF16)
    qTf = A.alloc(z1, "qTf", [128, 8, 128], F32)
    sc = prod[:].rearrange("p h (t k) -> p (h t) k", k=128)
    sc2 = oh[:].rearrange("p h (t k) -> p (h t) k", k=128)
    junkb = h2b
    junk2 = A.alloc(z3, "junk2", [128, D], BF16)
    B_junk2 = Buf("junk2")
    gateB = A.alloc(z1, "gateB", [128, 128], F32)
    m16 = A.alloc(z1, "m16", [128, 16, 16], F32)
    i16 = A.alloc(z1, "i16", [128, 16, 16], U32)
    i16f = A.alloc(z1, "i16f", [128, 16, 16], F32)
    t16 = A.alloc(z1, "t16", [128, 8, 16], F32)
    ci = A.alloc(z1, "ci", [128, 8, 16], U32)
    ca = A.alloc(z1, "ca", [128, 8, 16], U32)
    cb = A.alloc(z1, "cb", [128, 8, 16], U32)
    caf = A.alloc(z1, "caf", [128, 8, 16], F32)
    cbf = A.alloc(z1, "cbf", [128, 8, 16], F32)
    e1 = A.alloc(z1, "e1", [128, 8, 16], F32)
    e2 = A.alloc(z1, "e2", [128, 8, 16], F32)
    eif = A.alloc(z1, "eif", [128, 128], F32)
    eidx = A.alloc(z1, "eidx", [128, 128], U32)
    gex = A.alloc(z1, "gex", [128, 8, 16], F32)
    gsum = A.alloc(z1, "gsum", [128, 8], F32)
    gate = A.alloc(z1, "gate", [128, 128], F32)
    a_acc = A.alloc(z1, "a_acc", [128, 128], F32)
    wgt = A.alloc(z1, "wgt", [128, 128], F32)
    dgb = [A.alloc(z1, "dgb%d" % i, [128, 128], BF16) for i in range(2)]
    dgb_b = [Buf() for _ in range(2)]
    nstat2 = A.alloc(z1, "nstat2", [128, 8 * NT], F32)
    ytile = [A.alloc(z1, "ytile", [128, D], F32)] * 2
    ytile_b = [Buf()] * 2
    NCTX["nstat"] = nstat2
    NCTX["nstat_b"] = [Buf() for _ in range(2 * NT)]
    NCTX["junk"] = junkb
    Bt = {k: Buf(k) for k in ("h2f", "h2b", "h2T", "qTf", "sc", "sc2", "m16", "i16", "comb", "t16", "ci", "cab", "oh", "prod",
                              "e12", "eidx", "g", "a", "wgt", "htmp")}
    NCTX["junk_b"] = Bt["h2b"]
    Bt["m16b"] = Buf("m16b")
    Bt["t16b"] = Buf("t16b")
    Bt["sc"] = Bt["prod"]
    Bt["sc2"] = Bt["oh"]
    P.dma("pool", lambda e: e.dma_start(out=wpq_bf[:], in_=wpq_d.rearrange("(kc p) n -> p kc n", p=128)), "wpq", W=[B_wpq])
    P.dma("sync", lambda e: e.dma_start(out=skb[:], in_=sk_d), "skb", W=[B_sk])
    y_t = y_d.rearrange("(t p) d -> t p d", p=128)
    m16v = m16[:].rearrange("p (h t) r -> p h t r", t=2)
    i16fv = i16f[:].rearrange("p (h t) r -> p h t r", t=2)
    cn = {"ngu": 0, "ngv": 0, "ncast": 0, "ndg": 0, "nw": nw}
    eidxB = A.alloc(z1, "eidxB", [128, 128], U32)
    wgtB = A.alloc(z1, "wgtB", [128, 128], F32)
    eidx2 = [eidx, eidxB]
    h2f2 = [h2f, gt1_bc]
    h2f_b = [Buf("h2f0"), Buf("h2f1")]
    gate2 = [gate, gateB]
    gate_b = [Buf("gate0"), Buf("gate1")]
    wgt2 = [wgt, wgtB]
    eidx_b = [Buf(), Buf()]
    wgt_b = [Buf(), Buf()]

    def prepA(i):
        norm_tile(i, xn[:, i, :], xn_b[i], sc2_bc, sh2_bc, h2f2[i % 2], h2f_b[i % 2], h2f2[i % 2], h2f_b[i % 2])
        P.op("act", lambda e: e.copy(out=h2b[:], in_=h2f2[i % 2][:]), R=[h2f_b[i % 2]], W=[Bt["h2b"]])
        pv = ps[0][:].bitcast(BF16)
        for j in range(8):
            P.op("pe", (lambda e, j=j, pv=pv: e.transpose(pv[:, j * 128:(j + 1) * 128], h2b[:, j * 128:(j + 1) * 128], ident_b[:])),
                 R=[Bt["h2b"], B_const], W=[psb[0]])
        P.op("act", (lambda e, pv=pv: e.copy(out=h2T[:], in_=pv.rearrange("p (j t) -> p j t", t=128))), R=[psb[0]], W=[Bt["h2T"]])
        for hb in range(2):
            pb = 1 + hb
            for hh in range(4):
                h = hb * 4 + hh
                for kc in range(8):
                    P.op("pe", (lambda e, pb=pb, hh=hh, h=h, kc=kc: e.matmul(
                        ps[pb][:, hh * 128:(hh + 1) * 128], lhsT=wpq_bf[:, kc, h * 128:(h + 1) * 128], rhs=h2T[:, kc, :],
                        start=(kc == 0), stop=(kc == 7))), R=[B_wpq, Bt["h2T"]], W=[psb[pb]])
            P.op("act", (lambda e, pb=pb, hb=hb: e.copy(out=qTf[:, hb * 4:(hb + 1) * 4, :],
                                                       in_=ps[pb][:].rearrange("p (h t) -> p h t", t=128))), R=[psb[pb]], W=[Bt["qTf"]])
        for hp in range(4):
            pb = 2 + hp
            for hh in range(2):
                h = hp * 2 + hh
                P.op("pe", (lambda e, pb=pb, hh=hh, h=h: e.matmul(ps[pb][:, hh * 256:(hh + 1) * 256], lhsT=qTf[:, h, :], rhs=skb[:],
                                                                 start=True, stop=True)), R=[Bt["qTf"], B_sk], W=[psb[pb]])
            P.op("act", (lambda e, pb=pb, hp=hp: e.copy(out=sc[:, hp * 4:(hp + 1) * 4, :],
                                                       in_=ps[pb][:].rearrange("p (g t) -> p g t", t=128))), R=[psb[pb]], W=[Bt["sc"]])

    def prepB(i):
        for sg in range(16):
            P.op("dve", (lambda e, sg=sg: e.max(out=m16[:, sg, 0:8], in_=sc[:, sg, :])), R=[Bt["sc"]], W=[Bt["m16"]], waw_ok=True)
        for sg in range(16):
            P.op("dve", (lambda e, sg=sg: e.match_replace(out=sc2[:, sg, :], in_to_replace=m16[:, sg, 0:8], in_values=sc[:, sg, :], imm_value=NEG)),
                 R=[Bt["sc"], Bt["m16"]], W=[Bt["sc2"]], waw_ok=True)
        for sg in range(16):
            P.op("dve", (lambda e, sg=sg: e.max(out=m16[:, sg, 8:16], in_=sc2[:, sg, :])), R=[Bt["sc2"]], W=[Bt["m16b"]], waw_ok=True)
        for sg in range(16):
            P.op("dve", (lambda e, sg=sg: e.max_index(out=i16[:, sg, 0:8], in_max=m16[:, sg, 0:8], in_values=sc[:, sg, :])),
                 R=[Bt["sc"], Bt["m16"]], W=[Bt["i16"]], waw_ok=True)
        for sg in range(16):
            P.op("dve", (lambda e, sg=sg: e.max_index(out=i16[:, sg, 8:16], in_max=m16[:, sg, 8:16], in_values=sc2[:, sg, :])),
                 R=[Bt["sc2"], Bt["m16b"]], W=[Bt["i16"]], waw_ok=True)
        P.op("pool", lambda e: e.tensor_copy(out=i16f[:], in_=i16[:]), R=[Bt["i16"]], W=[Bt["i16"]])
        P.op("pool", lambda e: e.tensor_tensor(out=comb[:].rearrange("p h (a b) -> p h a b", b=16),
                                               in0=m16v[:, :, 0, :].unsqueeze(3).to_broadcast([128, 8, 16, 16]),
                                               in1=m16v[:, :, 1, :].unsqueeze(2).to_broadcast([128, 8, 16, 16]), op=ALU.add),
             R=[Bt["m16"], Bt["m16b"]], W=[Bt["comb"]])
        for h in range(8):
            P.op("dve", (lambda e, h=h: e.max(out=t16[:, h, 0:8], in_=comb[:, h, :])), R=[Bt["comb"]], W=[Bt["t16"]], waw_ok=True)
        for h in range(8):
            P.op("dve", (lambda e, h=h: e.max_index(out=ci[:, h, 0:8], in_max=t16[:, h, 0:8], in_values=comb[:, h, :])),
                 R=[Bt["comb"], Bt["t16"]], W=[Bt["ci"]], waw_ok=True)
        for h in range(8):
            P.op("dve", (lambda e, h=h: e.match_replace(out=comb[:, h, :], in_to_replace=t16[:, h, 0:8], in_values=comb[:, h, :], imm_value=NEG)),
                 R=[Bt["t16"], Bt["ci"]], W=[Bt["comb"]], waw_ok=True)
        for h in range(8):
            P.op("dve", (lambda e, h=h: e.max(out=t16[:, h, 8:16], in_=comb[:, h, :])), R=[Bt["comb"]], W=[Bt["t16b"]], waw_ok=True)
        for h in range(8):
            P.op("dve", (lambda e, h=h: e.max_index(out=ci[:, h, 8:16], in_max=t16[:, h, 8:16], in_values=comb[:, h, :])),
                 R=[Bt["comb"], Bt["t16b"]], W=[Bt["ci"]], waw_ok=True)
        P.op("dve", lambda e: e.tensor_single_scalar(out=ca[:], in_=ci[:], scalar=4, op=ALU.logical_shift_right), R=[Bt["ci"]], W=[Bt["cab"]])
        P.op("dve", lambda e: e.tensor_single_scalar(out=cb[:], in_=ci[:], scalar=15, op=ALU.bitwise_and), R=[Bt["ci"]], W=[Bt["cab"]], waw_ok=True)
        iota_b = iota16[:].unsqueeze(1).unsqueeze(1).to_broadcast([128, 8, 16, 16])
        for (cf, half, eo) in ((ca, 0, e1), (cb, 1, e2)):
            P.op("dve", (lambda e, cf=cf: e.tensor_tensor(out=oh[:].rearrange("p h (r a) -> p h r a", a=16),
                                                         in0=cf[:].unsqueeze(3).to_broadcast([128, 8, 16, 16]), in1=iota_b, op=ALU.is_equal)),
                 R=[Bt["cab"], B_const], W=[Bt["oh"]])
            P.op("pool", (lambda e, half=half: e.tensor_tensor(out=prod[:].rearrange("p h (r a) -> p h r a", a=16),
                                                             in0=oh[:].rearrange("p h (r a) -> p h r a", a=16),
                                                             in1=i16fv[:, :, half, :].unsqueeze(2).to_broadcast([128, 8, 16, 16]), op=ALU.mult)),
                 R=[Bt["oh"], Bt["i16"]], W=[Bt["prod"]])
            P.op("dve", (lambda e, eo=eo: e.tensor_reduce(out=eo[:], in_=prod[:].rearrange("p h (r a) -> p h r a", a=16), axis=AX.X, op=ALU.add)),
                 R=[Bt["prod"]], W=[Bt["e12"]])

    def prepC(i):
        P.op("dve", lambda e: e.scalar_tensor_tensor(out=eidx2[i % 2][:], in0=e1[:].rearrange("p h r -> p (h r)"), scalar=128.0,
                                                     in1=e2[:].rearrange("p h r -> p (h r)"), op0=ALU.mult, op1=ALU.add),
             R=[Bt["e12"]], W=[eidx_b[i % 2]])
        P.op("dve", lambda e: e.tensor_tensor(out=gex[:], in0=t16[:], in1=t16[:, :, 0:1].to_broadcast([128, 8, 16]), op=ALU.subtract),
             R=[Bt["t16"], Bt["t16b"]], W=[Bt["g"]])
        P.op("act", lambda e: e.activation(out=gex[:], in_=gex[:], func=AF.Exp), R=[Bt["g"]], W=[Bt["g"]])
        P.op("dve", lambda e: e.tensor_reduce(out=gsum[:], in_=gex[:], axis=AX.X, op=ALU.add), R=[Bt["g"]], W=[Bt["g"]])
        P.op("dve", lambda e: e.reciprocal(out=gsum[:], in_=gsum[:]), R=[Bt["g"]], W=[Bt["g"]])
        P.op("dve", lambda e: e.tensor_tensor(out=gate2[i % 2][:].rearrange("p (h r) -> p h r", r=16), in0=gex[:],
                                              in1=gsum[:].unsqueeze(2).to_broadcast([128, 8, 16]), op=ALU.mult), R=[Bt["g"]], W=[gate_b[i % 2]])

    ak_b = [Buf() for _ in range(128 // GK)]
    wk_b = [Buf() for _ in range(128 // GK)]
    slot_of = {}

    def ghead(i, g):
        for k in range(g * GK, (g + 1) * GK):
            s_ = cn['ngu'] % NS
            cn['ngu'] += 1
            slot_of[(i, k)] = s_
            P.dma("pool", (lambda e, s_=s_, k=k, i=i: e.indirect_dma_start(
                out=uvbuf[s_][:], out_offset=None, in_=uvb_d, in_offset=bass.IndirectOffsetOnAxis(ap=eidx2[i % 2][:, k:k + 1], axis=0))),
                "g%d" % s_, R=[eidx_b[i % 2], B_cv], W=[uv_b[s_]], skip=("dve", "dma:g%d" % s_), extra=[eidx_b[i % 2].w])
            P.op("dve", (lambda e, s_=s_, k=k, i=i: e.scalar_tensor_tensor(out=junk2[:], in0=uvbuf[s_][:, 0:D], scalar=1.0, in1=h2f2[i % 2][:],
                                                                       op0=ALU.mult, op1=ALU.mult, accum_out=a_acc[:, k:k + 1])),
                 R=[uv_b[s_], h2f_b[i % 2]], W=[B_junk2, ak_b[g]],
                 extra=([("act", P.cnt["act"])] if (g == 0 and k == 0) else []))
        P.op("act", (lambda e, g=g, i=i: e.activation(out=wgt2[i % 2][:, g * GK:(g + 1) * GK], in_=a_acc[:, g * GK:(g + 1) * GK], func=AF.Gelu)),
             R=[ak_b[g]], W=[wk_b[g]])

    def gtail(i, g):
        for k in range(g * GK, (g + 1) * GK):
            P.op("act", (lambda e, k=k, i=i: e.activation(out=wgt2[i % 2][:, k:k + 1], in_=wgt2[i % 2][:, k:k + 1], func=AF.Copy,
                                                         scale=gate2[i % 2][:, k:k + 1])), R=[gate_b[i % 2]], W=[wk_b[g]])
        for k in range(g * GK, (g + 1) * GK):
            s_ = slot_of[(i, k)]
            ds_ = cn['ndg'] % 2
            cn['ndg'] += 1
            P.op("act", (lambda e, ds_=ds_, k=k, i=i: e.activation(out=dgb[ds_][:], in_=ident_b[:], func=AF.Copy, scale=wgt2[i % 2][:, k:k + 1])),
                 R=[wk_b[g], B_const], W=[dgb_b[ds_]])
            for half in range(2):
                P.op("pe", (lambda e, ds_=ds_, s_=s_, half=half, k=k: e.matmul(
                    ps[6 + half][:], lhsT=dgb[ds_][:], rhs=uvbuf[s_][:, D + half * 512:D + (half + 1) * 512],
                    start=(k == 0), stop=(k == 127))), R=[dgb_b[ds_], uv_b[s_]], W=[psb[6 + half]], skip=("dma:g%d" % s_,))

    def fin(i):
        for half in range(2):
            ws = cn['nw'] % 2
            cn['nw'] += 1
            P.op("dve", (lambda e, ws=ws, half=half: e.tensor_tensor(out=wtmp[ws][:], in0=ps[6 + half][:],
                                                                    in1=gt2_bc[:, half * 512:(half + 1) * 512], op=ALU.mult)),
                 R=[psb[6 + half], B_bc], W=[wtmp_b[ws]])
            P.op("pool", (lambda e, ws=ws, i=i, half=half: e.tensor_tensor(
                out=xn[:, i, half * 512:(half + 1) * 512], in0=xn[:, i, half * 512:(half + 1) * 512], in1=wtmp[ws][:], op=ALU.add)),
                R=[wtmp_b[ws]], W=[xn_b[i]])
        ys = i % 2
        norm_tile(NT + i, xn[:, i, :], xn_b[i], gfin, gfin, None, None, ytile[ys], ytile_b[ys])
        P.dma("sync", (lambda e, ys=ys, i=i: e.dma_start(out=y_t[i], in_=ytile[ys][:])), "y0", R=[ytile_b[ys]])


    NGRP_K = 128 // GK
    GA, GB, GC = 8, 56, 64
    prepA(0)
    prepB(0)
    prepC(0)
    for i in range(NT):
        for g in range(NGRP_K):
            ghead(i, g)
            if g >= 1:
                gtail(i, g - 1)
            if i + 1 < NT:
                if g == GA:
                    prepA(i + 1)
                if g == GB:
                    prepB(i + 1)
                if g == GC:
                    prepC(i + 1)
        gtail(i, NGRP_K - 1)
        fin(i)

    return finish(nc, P, dbg_d)


def finish(nc, P, dbg_d):
    from contextlib import ExitStack
    P.barrier()
    P.op("act", lambda e: e.nop())
    with ExitStack() as stack:
        P.emit(nc, stack)
    return nc, dbg_d


def _fm(vec):
    v = np.asarray(vec, np.float32).reshape(-1, 128)
    return np.ascontiguousarray(v.T)


def host_consts(S):
    ident = np.eye(128, dtype=np.float32)
    kk = np.arange(128)[:, None]
    qq = np.arange(128)[None, :]
    cmask = (qq >= kk).astype(np.float32)
    pos = np.arange(S, dtype=np.float32)
    inv = (500000.0 ** (-np.arange(0, 16, 2, dtype=np.float32) / 16.0)).astype(np.float32)
    ang = pos[None, :] * inv[:, None]
    cos = np.ones((128, S), np.float32)
    sin = np.zeros((128, S), np.float32)
    rp = np.zeros((128, 128), np.float32)
    for blk in range(2):
        b = 64 * blk
        for j in range(8):
            cos[b + j] = np.cos(ang[j])
            cos[b + 8 + j] = np.cos(ang[j])
            sin[b + j] = np.sin(ang[j])
            sin[b + 8 + j] = np.sin(ang[j])
            rp[b + j, b + 8 + j] = -1.0
            rp[b + 8 + j, b + j] = 1.0
    iota16 = np.tile(np.arange(16, dtype=np.float32)[None, :], (128, 1))
    return dict(ident=ident, cmask=cmask, rope_cos=cos, rope_sin=sin,
                rpermT=np.ascontiguousarray(rp.T), iota16=iota16)


def host_shared(inp, S):
    w_in = np.asarray(inp["w_in"], np.float32)[0]
    fq, fk, fv = w_in[:, 0:512], w_in[:, 512:1024], w_in[:, 1024:1536]
    fg = w_in[:, 1536:1544]
    dq, dk, dv = w_in[:, 1544:2056], w_in[:, 2056:2568], w_in[:, 2568:3080]
    units = []
    for u in range(4):
        sl = slice(u * 128, (u + 1) * 128)
        units.append(np.concatenate([fq[:, sl], fk[:, sl], fv[:, sl]], axis=1))
    for d in range(4):
        sl = slice(d * 128, (d + 1) * 128)
        units.append(np.concatenate([dq[:, sl], dk[:, sl], dv[:, sl]], axis=1))
    w_units = np.ascontiguousarray(np.stack(units, 0))
    sk = np.asarray(inp["sub_keys"], np.float32)[0]
    skblk = np.zeros((128, 256), np.float32)
    skblk[0:64, 0:128] = sk[0].T
    skblk[64:128, 128:256] = sk[1].T
    lam = np.concatenate([np.asarray(inp[k], np.float32)[0] for k in ("lambda_q1", "lambda_k1", "lambda_q2", "lambda_k2")])
    sh = dict(
        w_ada=np.ascontiguousarray(np.asarray(inp["w_ada"], np.float32)[0]),
        b_adaT=_fm(np.asarray(inp["b_ada"])[0]),
        g_attnT=_fm(np.asarray(inp["g_attn"])[0]),
        g_ffnT=_fm(np.asarray(inp["g_ffn"])[0]),
        g_final_bc=np.ascontiguousarray(np.tile(np.asarray(inp["g_final"], np.float32)[None, :], (128, 1))),
        w_units=w_units,
        w_fg=np.ascontiguousarray(fg),
        b_f=np.ascontiguousarray(np.asarray(inp["b_f"], np.float32)[0].reshape(8, 1)),
        lam_bc=np.ascontiguousarray(np.tile(lam[None, :], (128, 1))),
        g_subln_bc=np.ascontiguousarray(np.tile(np.asarray(inp["g_subln"], np.float32)[0][None, :], (128, 1))),
        w_o=np.ascontiguousarray(np.asarray(inp["w_o"], np.float32)[0]),
        w_pq=np.ascontiguousarray(np.asarray(inp["w_pq"], np.float32)[0]),
        skblk=skblk,
        u_exp=np.ascontiguousarray(np.asarray(inp["u_experts"], np.float32)[0]),
        v_exp=np.ascontiguousarray(np.asarray(inp["v_experts"], np.float32)[0]),
    )
    sh.update(host_consts(S))
    return sh


def make_in_maps(inp, S, ncores):
    sh = host_shared(inp, S)
    x = np.asarray(inp["x"], np.float32)
    c = np.asarray(inp["c"], np.float32)
    maps = []
    for b in range(ncores):
        m = dict(sh)
        m["x"] = np.ascontiguousarray(x[b, :S])
        m["cT"] = _fm(c[b])
        maps.append(m)
    return maps


_CACHE = {}


def kernel(**inputs):
    S = SEQ
    if "nc" not in _CACHE:
        _CACHE["nc"] = build(NT=S // 128)[0]
    nc = _CACHE["nc"]
    maps = make_in_maps(inputs, S, NCORES)
    res = run_bass_kernel_spmd(nc, maps, core_ids=list(range(NCORES)))
    out = np.stack([np.asarray(r["y"], np.float32) for r in res.results], 0)
    return out
```

```python
import math
import numpy as np
import concourse.bass as bass
import concourse.mybir as mybir
from concourse.bass_utils import run_bass_kernel_spmd

F32 = mybir.dt.float32
BF16 = mybir.dt.bfloat16
U32 = mybir.dt.uint32
I32 = mybir.dt.int32
AF = mybir.ActivationFunctionType
ALU = mybir.AluOpType
AX = mybir.AxisListType

D = 1024
NCORES = 8
SEQ = 2048
NEG = -1.0e30


class Buf:
    __slots__ = ("name", "w", "r")

    def __init__(self, name=""):
        self.name = name
        self.w = None
        self.r = {}


class Prog:
    ENGS = ("sync", "act", "dve", "pool", "pe")

    def __init__(self):
        self.streams = {e: [] for e in self.ENGS}
        self.cnt = {e: 0 for e in self.ENGS}
        self.dma_tot = {}
        self.pending = {e: [] for e in self.ENGS}

    def _deps(self, R, W, extra, waw_eng=None):
        deps = []
        for b in R:
            if b.w is not None:
                deps.append(b.w)
        for b in W:
            if b.w is not None and b.w[0] != waw_eng:
                deps.append(b.w)
            for k, v in b.r.items():
                deps.append((k, v))
        for t in extra:
            if t is not None:
                deps.append(t)
        return deps

    def _mark(self, tok, R, W):
        for b in R:
            k, v = tok
            if b.r.get(k, 0) < v:
                b.r[k] = v
        for b in W:
            b.w = tok
            b.r = {}

    def op(self, eng, fn, R=(), W=(), extra=(), skip=(), waw_ok=False):
        deps = [d for d in self._deps(R, W, (), eng if waw_ok else None) if d[0] not in skip] + [t for t in extra if t is not None] + self.pending[eng]
        self.pending[eng] = []
        self.cnt[eng] += 1
        tok = (eng, self.cnt[eng])
        self.streams[eng].append((fn, deps, tok, None))
        self._mark(tok, R, W)
        return tok

    def dma(self, queue, fn, sem, R=(), W=(), extra=(), skip=()):
        deps = [d for d in self._deps(R, W, ()) if d[0] not in skip] + [t for t in extra if t is not None] + self.pending[queue]
        self.pending[queue] = []
        self.dma_tot[sem] = self.dma_tot.get(sem, 0) + 16
        tok = ("dma:" + sem, self.dma_tot[sem])
        self.streams[queue].append((fn, deps, tok, sem))
        self._mark(tok, R, W)
        return tok

    def barrier(self):
        toks = [(e, self.cnt[e]) for e in self.ENGS if e != "sync" and self.cnt[e] > 0]
        toks += [("dma:" + s, v) for s, v in self.dma_tot.items()]
        for e in self.ENGS:
            self.pending[e] = self.pending[e] + toks

    def emit(self, nc, stack):
        sems = {}
        for e in self.ENGS:
            if e != "sync":
                sems[e] = stack.enter_context(nc.semaphore("s_" + e))
        for s in self.dma_tot:
            sems["dma:" + s] = stack.enter_context(nc.semaphore("d_" + s))
        block = stack.enter_context(nc.Block())

        def run(ename):
            def body(eng):
                seen = {}
                for fn, deps, tok, dsem in self.streams[ename]:
                    need = {}
                    for k, v in deps:
                        if k == "pe" and ename == "pe":
                            continue
                        if need.get(k, 0) < v:
                            need[k] = v
                    for k, v in need.items():
                        if seen.get(k, 0) < v:
                            eng.wait_ge(sems[k], v)
                            seen[k] = v
                    ins = fn(eng)
                    if dsem is not None:
                        ins.then_inc(sems["dma:" + dsem], 16)
                    else:
                        ins.then_inc(sems[ename], 1)
            return body

        block.sync(run("sync"))
        block.scalar(run("act"))
        block.vector(run("dve"))
        block.gpsimd(run("pool"))
        block.tensor(run("pe"))


class Alloc:
    def __init__(self, nc, lo=16512, hi=229376):
        self.nc = nc
        self.lo = lo
        self.hi = hi
        self.n = 0

    def zone(self, start, size):
        return {"start": start, "end": start + size, "cur": start}

    def alloc(self, z, name, shape, dtype):
        esz = {F32: 4, BF16: 2, U32: 4, I32: 4}[dtype]
        nbytes = esz
        for s in shape[1:]:
            nbytes *= s
        nbytes = (nbytes + 63) // 64 * 64
        off = z["cur"]
        assert off + nbytes <= z["end"], (name, off, nbytes, z)
        assert off + nbytes <= self.hi
        z["cur"] = off + nbytes
        self.n += 1
        return self.nc.alloc_sbuf_tensor_at("%s_%d" % (name, self.n), list(shape), dtype, offset=off)


def build(NT=16, stage=99, dbg=False, nunits=8):
    S = NT * 128
    NG = NT // 4
    nc = bass.Bass("TRN2", target_bir_lowering=False)
    P = Prog()

    def dram_in(name, shape, dt=F32):
        return nc.dram_tensor(name, list(shape), dt, kind="ExternalInput").ap()

    x_d = dram_in("x", [S, D])
    cT_d = dram_in("cT", [128, 8])
    wada_d = dram_in("w_ada", [D, 6 * D])
    badaT_d = dram_in("b_adaT", [128, 48])
    gattnT_d = dram_in("g_attnT", [128, 8])
    gffnT_d = dram_in("g_ffnT", [128, 8])
    gfin_d = dram_in("g_final_bc", [128, D])
    wun_d = dram_in("w_units", [8, D, 384])
    wfg_d = dram_in("w_fg", [D, 8])
    bf_d = dram_in("b_f", [8, 1])
    lam_d = dram_in("lam_bc", [128, 256])
    gsub_d = dram_in("g_subln_bc", [128, 128])
    wo_d = dram_in("w_o", [D, D])
    wpq_d = dram_in("w_pq", [D, D])
    sk_d = dram_in("skblk", [128, 256])
    u_d = dram_in("u_exp", [16384, D])
    v_d = dram_in("v_exp", [16384, D])
    ident_d = dram_in("ident", [128, 128])
    cmask_d = dram_in("cmask", [128, 128])
    cos_d = dram_in("rope_cos", [128, S])
    sin_d = dram_in("rope_sin", [128, S])
    rperm_d = dram_in("rpermT", [128, 128])
    iota_d = dram_in("iota16", [128, 16])
    y_d = nc.dram_tensor("y", [S, D], F32, kind="ExternalOutput").ap()
    uvb_d = nc.dram_tensor("uv_bf16", [16384, 2 * D], BF16, kind="Internal").ap()
    B_cv = Buf("cv")
    CVR = 1024
    cv_list = [(src, off, c) for (src, off) in ((u_d, 0), (v_d, D)) for c in range(16384 // CVR)]
    cv_pos = [0]

    def convert_some(n):
        for _ in range(n):
            if cv_pos[0] >= len(cv_list):
                return
            src, off, c = cv_list[cv_pos[0]]
            cv_pos[0] += 1
            P.dma("pool", (lambda e, src=src, off=off, c=c: e.dma_start(out=uvb_d[c * CVR:(c + 1) * CVR, off:off + D],
                                                                       in_=src[c * CVR:(c + 1) * CVR, :])),
                  "cv", W=[B_cv])
    dbg_d = {}

    def dbg_out(name, shape, dt=F32):
        dbg_d[name] = nc.dram_tensor("dbg_" + name, list(shape), dt, kind="ExternalOutput").ap()
        return dbg_d[name]

    A = Alloc(nc)
    LO = 16512
    KB = 1024
    zc = A.zone(LO, 26 * KB)
    z1 = A.zone(zc["end"], 98 * KB)
    z2 = A.zone(z1["end"], 48 * KB)
    z3 = A.zone(z2["end"], 229376 - z2["end"])
    assert z3["end"] - z3["start"] >= 34 * KB, z3

    ps = [nc.alloc_psum_tensor("ps%d" % i, [128, 512], F32) for i in range(8)]
    psb = [Buf("ps%d" % i) for i in range(8)]

    ident_f = A.alloc(zc, "ident_f", [128, 128], F32)
    ident_b = A.alloc(zc, "ident_b", [128, 128], BF16)
    ones_f = A.alloc(zc, "ones_f", [128, 128], F32)
    cmask_f = A.alloc(zc, "cmask_f", [128, 128], F32)
    maskneg = A.alloc(zc, "maskneg", [128, 128], F32)
    modT = A.alloc(zc, "modT", [128, 48], F32)
    scl1T = A.alloc(zc, "scl1T", [128, 8], F32)
    scl2T = A.alloc(zc, "scl2T", [128, 8], F32)
    gt1_bc = A.alloc(zc, "gt1_bc", [128, D], F32)
    gt2_bc = A.alloc(zc, "gt2_bc", [128, D], F32)
    sc2_bc = A.alloc(zc, "sc2_bc", [128, D], F32)
    sh2_bc = A.alloc(zc, "sh2_bc", [128, D], F32)
    gfin = A.alloc(zc, "gfin", [128, D], F32)
    lam_t = A.alloc(zc, "lam_t", [128, 256], F32)
    lam_s = A.alloc(zc, "lam_s", [128, 8], F32)
    gsub = A.alloc(zc, "gsub", [128, 128], F32)
    smallc = A.alloc(zc, "smallc", [128, 64], F32)
    iota16 = A.alloc(zc, "iota16", [128, 16], F32)
    B_const = Buf("const")
    B_mod = Buf("mod")
    B_bc = Buf("bc")

    cT = A.alloc(z3, "cT", [128, 8], F32)
    scT2 = A.alloc(z3, "scT2", [128, 8, 2], F32)
    badaT = A.alloc(z3, "badaT", [128, 48], F32)
    gattnT = A.alloc(z3, "gattnT", [128, 8], F32)
    gffnT = A.alloc(z3, "gffnT", [128, 8], F32)
    sc1_bc = A.alloc(z3, "sc1_bc", [128, D], F32)
    sh1_bc = A.alloc(z3, "sh1_bc", [128, D], F32)
    diag = [A.alloc(z3, "diag%d" % i, [128, 128], F32) for i in range(2)]
    diag_b = [Buf("diag%d" % i) for i in range(2)]
    WCOLS = 256
    wst = [A.alloc(z3, "wst%d" % i, [128, 8, WCOLS], F32) for i in range(2)]
    wst_b = [Buf("wst%d" % i) for i in range(2)]

    qs = "sync"
    for (dst, src) in ((ident_f, ident_d), (cmask_f, cmask_d), (cT, cT_d), (badaT, badaT_d),
                       (gattnT, gattnT_d), (gffnT, gffnT_d), (gfin, gfin_d), (lam_t, lam_d),
                       (gsub, gsub_d), (iota16, iota_d)):
        P.dma(qs, (lambda e, d=dst, s=src: e.dma_start(out=d[:], in_=s)), "c0", W=[B_const])
    P.op("pool", lambda e: e.memset(ones_f[:], 1.0), W=[B_const])
    P.op("dve", lambda e: e.tensor_copy(out=ident_b[:], in_=ident_f[:]), R=[B_const], W=[B_const])
    P.op("dve", lambda e: e.tensor_scalar(out=maskneg[:], in0=cmask_f[:], scalar1=-1.0, scalar2=30000.0, op0=ALU.add, op1=ALU.mult),
         R=[B_const], W=[B_const])
    B_sc = Buf("scT")
    P.op("act", lambda e: e.activation(out=scT2[:, :, 0], in_=cT[:], func=AF.Silu), R=[B_const], W=[B_sc])
    P.op("act", lambda e: e.activation(out=scT2[:, :, 1], in_=cT[:], func=AF.Silu), R=[B_const], W=[B_sc])
    B_lam = Buf("lam")
    junk64 = smallc[:, 0:64]
    P.op("dve", lambda e: e.scalar_tensor_tensor(out=junk64, in0=lam_t[:, 0:64], scalar=1.0, in1=lam_t[:, 64:128],
                                                 op0=ALU.mult, op1=ALU.mult, accum_out=lam_s[:, 0:1]),
         R=[B_const], W=[B_lam])
    P.op("dve", lambda e: e.scalar_tensor_tensor(out=junk64, in0=lam_t[:, 128:192], scalar=1.0, in1=lam_t[:, 192:256],
                                                 op0=ALU.mult, op1=ALU.mult, accum_out=lam_s[:, 1:2]),
         R=[B_const, B_lam], W=[B_lam])
    P.op("act", lambda e: e.activation(out=lam_s[:, 2:4], in_=lam_s[:, 0:2], func=AF.Exp), R=[B_lam], W=[B_lam])
    lam_init = 0.8 - 0.6 * math.exp(-0.3 * 0)
    P.op("dve", lambda e: e.scalar_tensor_tensor(out=lam_s[:, 5:6], in0=lam_s[:, 3:4], scalar=-lam_init,
                                                 in1=lam_s[:, 2:3], op0=ALU.add, op1=ALU.subtract),
         R=[B_lam], W=[B_lam])

    ps_mod = ps[0]
    wada_v = wada_d.rearrange("(kc p) n -> p kc n", p=128)
    NGRP = 6 * D // WCOLS
    for g in range(NGRP):
        sl = g % 2
        P.dma("sync", (lambda e, sl=sl, g=g: e.dma_start(out=wst[sl][:], in_=wada_v[:, :, g * WCOLS:(g + 1) * WCOLS])),
              "wst%d" % sl, W=[wst_b[sl]])
        for cc in range(WCOLS // 128):
            j = g * (WCOLS // 128) + cc
            for kc in range(8):
                P.op("pe", (lambda e, sl=sl, cc=cc, kc=kc, j=j: e.matmul(
                    ps_mod[:, 2 * j:2 * j + 2], lhsT=wst[sl][:, kc, cc * 128:(cc + 1) * 128],
                    rhs=scT2[:, kc, :], start=(kc == 0), stop=(kc == 7))),
                    R=[wst_b[sl], B_sc], W=[psb[0]])
    pm_v = ps_mod[:, 0:96].rearrange("p (j t) -> p j t", t=2)
    P.op("dve", lambda e: e.tensor_tensor(out=modT[:], in0=pm_v[:, :, 0], in1=badaT[:], op=ALU.add),
         R=[psb[0], B_const], W=[B_mod])
    P.op("dve", lambda e: e.scalar_tensor_tensor(out=scl1T[:], in0=modT[:, 8:16], scalar=1.0, in1=gattnT[:],
                                                 op0=ALU.add, op1=ALU.mult), R=[B_mod, B_const], W=[B_mod])
    P.op("dve", lambda e: e.scalar_tensor_tensor(out=scl2T[:], in0=modT[:, 32:40], scalar=1.0, in1=gffnT[:],
                                                 op0=ALU.add, op1=ALU.mult), R=[B_mod, B_const], W=[B_mod])
    bc_list = ((sc1_bc, scl1T, 0), (sh1_bc, modT, 0), (gt1_bc, modT, 16),
               (sc2_bc, scl2T, 0), (sh2_bc, modT, 24), (gt2_bc, modT, 40))
    n_d = 0
    for bi, (dst, srcT, c0) in enumerate(bc_list):
        for half in range(2):
            pb = 1 + (bi * 2 + half) % 2
            for jj in range(4):
                j = half * 4 + jj
                dsl = n_d % 2
                n_d += 1
                P.op("dve", (lambda e, dsl=dsl, srcT=srcT, col=c0 + j: e.tensor_scalar(
                    out=diag[dsl][:], in0=ident_f[:], scalar1=srcT[:, col:col + 1], scalar2=None, op0=ALU.mult)),
                    R=[B_mod, B_const], W=[diag_b[dsl]])
                P.op("pe", (lambda e, dsl=dsl, pb=pb, jj=jj: e.matmul(
                    ps[pb][:, jj * 128:(jj + 1) * 128], lhsT=ones_f[:], rhs=diag[dsl][:], start=True, stop=True)),
                    R=[diag_b[dsl], B_const], W=[psb[pb]])
            P.op("act", (lambda e, dst=dst, pb=pb, half=half: e.copy(out=dst[:, half * 512:(half + 1) * 512], in_=ps[pb][:])),
                 R=[psb[pb]], W=[B_bc])

    if dbg:
        o = dbg_out("modT", [128, 48])
        P.dma("sync", lambda e, o=o: e.dma_start(out=o, in_=modT[:]), "dbg", R=[B_mod])
        o2 = dbg_out("gt1_bc", [128, D])
        P.dma("sync", lambda e, o2=o2: e.dma_start(out=o2, in_=gt1_bc[:]), "dbg", R=[B_bc])
        o3 = dbg_out("lam", [128, 8])
        P.dma("sync", lambda e, o3=o3: e.dma_start(out=o3, in_=lam_s[:]), "dbg", R=[B_lam])

    hT = A.alloc(z1, "hT", [128, 8, S], BF16)
    hT_b = [Buf("hT%d" % i) for i in range(NT)]
    xst = [A.alloc(z1, "xst%d" % i, [128, D], F32) for i in range(2)]
    xst_b = [Buf("xst%d" % i) for i in range(2)]
    htmp = [A.alloc(z1, "htmp%d" % i, [128, D], F32) for i in range(2)]
    htmp_b = [Buf() for _ in range(2)]
    hbt = [A.alloc(z1, "hbt%d" % i, [128, D], BF16) for i in range(2)]
    hbt_b = [Buf() for _ in range(2)]
    sq_junk = A.alloc(z1, "sq_junk", [128, D], BF16)
    nstat = A.alloc(z1, "nstat", [128, 4 * NT], F32)
    nstat_b = [Buf() for _ in range(NT)]
    x_t = x_d.rearrange("(t p) d -> t p d", p=128)
    B_junk = Buf("junk")

    NCTX = {"nstat": nstat, "nstat_b": nstat_b, "junk": sq_junk, "junk_b": B_junk}

    def norm_tile(i, src_ap, src_buf, scale_bc, shift_bc, out_bf, out_bf_buf, tmp, tmp_buf):
        nstat = NCTX["nstat"]
        nstat_b = NCTX["nstat_b"]
        sq_junk = NCTX["junk"]
        ss = nstat[:, 4 * i:4 * i + 1]
        var = nstat[:, 4 * i + 1:4 * i + 2]
        std = nstat[:, 4 * i + 2:4 * i + 3]
        rstd = nstat[:, 4 * i + 3:4 * i + 4]
        P.op("act", lambda e: e.activation(out=sq_junk[:], in_=src_ap, func=AF.Square, accum_out=ss),
             R=[src_buf], W=[NCTX["junk_b"], nstat_b[i]])
        P.op("dve", lambda e: e.tensor_scalar(out=var, in0=ss, scalar1=1.0 / D, scalar2=1e-6, op0=ALU.mult, op1=ALU.add),
             R=[nstat_b[i]], W=[nstat_b[i]])
        P.op("act", lambda e: e.activation(out=std, in_=var, func=AF.Sqrt), R=[nstat_b[i]], W=[nstat_b[i]])
        P.op("dve", lambda e: e.reciprocal(out=rstd, in_=std), R=[nstat_b[i]], W=[nstat_b[i]])
        P.op("dve", lambda e: e.scalar_tensor_tensor(out=tmp[:], in0=src_ap, scalar=rstd, in1=scale_bc[:],
                                                     op0=ALU.mult, op1=ALU.mult),
             R=[src_buf, nstat_b[i], B_bc, B_const], W=[tmp_buf])
        if out_bf is not None:
            P.op("pool", lambda e: e.tensor_tensor(out=out_bf[:], in0=tmp[:], in1=shift_bc[:], op=ALU.add),
                 R=[tmp_buf, B_bc], W=[out_bf_buf])

    def transpose_to(i, src_bf, src_buf, dstT, dst_buf, pbank, evac_eng):
        pv = ps[pbank][:].bitcast(BF16)
        for j in range(8):
            P.op("pe", (lambda e, j=j: e.transpose(pv[:, j * 128:(j + 1) * 128], src_bf[:, j * 128:(j + 1) * 128], ident_b[:])),
                 R=[src_buf, B_const], W=[psb[pbank]])
        pv3 = pv.rearrange("p (j t) -> p j t", t=128)
        if evac_eng == "act":
            P.op("act", lambda e: e.copy(out=dstT[:, :, i * 128:(i + 1) * 128], in_=pv3), R=[psb[pbank]], W=[dst_buf])
        else:
            P.op("dve", lambda e: e.tensor_copy(out=dstT[:, :, i * 128:(i + 1) * 128], in_=pv3), R=[psb[pbank]], W=[dst_buf])

    for i in range(NT):
        sl = i % 2
        P.dma("sync", (lambda e, sl=sl, i=i: e.dma_start(out=xst[sl][:], in_=x_t[i])), "xst%d" % sl, W=[xst_b[sl]])
        norm_tile(i, xst[sl][:], xst_b[sl], sc1_bc, sh1_bc, hbt[sl], hbt_b[sl], htmp[sl], htmp_b[sl])
        transpose_to(i, hbt[sl], hbt_b[sl], hT, hT_b[i], 3 + sl, "act" if sl == 0 else "dve")

    if dbg:
        o = dbg_out("hT", [128, 8, S], BF16)
        P.dma("sync", lambda e, o=o: e.dma_start(out=o, in_=hT[:]), "dbg", R=hT_b)

    if stage <= 1:
        return finish(nc, P, dbg_d)
    P.barrier()
    z1["cur"] = z1["start"] + 8 * S * 2
    z3["cur"] = z3["start"]

    mixedT = A.alloc(z2, "mixedT", [128, 8, S], BF16)
    mixedT_b = [Buf("mxT%d" % i) for i in range(NT)]
    wo_bf = A.alloc(z2, "wo_bf", [128, 8, D], BF16)
    B_wo = Buf("wo")
    wun = [A.alloc(z1, "wun%d" % i, [128, 8, 384], BF16) for i in range(2)]
    wun_b = [Buf() for _ in range(2)]
    qT = A.alloc(z1, "qT", [128, S], BF16)
    kT = A.alloc(z1, "kT", [128, S], BF16)
    qk_b = [Buf("qT"), Buf("kT")]
    VW = 130
    vtm = A.alloc(z1, "vtm", [128, NT, VW], BF16)
    v_b = Buf("v")
    FQ = A.alloc(z1, "FQ", [128, S], BF16)
    FK = A.alloc(z1, "FK", [128, S], BF16)
    F_b = [Buf("Fs%d" % i) for i in range(4)]
    pT = [A.alloc(z1, "pT%d" % i, [128, 512], BF16) for i in range(3)]
    pT_b = [Buf() for _ in range(3)]
    o1n = A.alloc(z1, "o1n", [128, NT, 128], F32)
    o1n_b = [Buf() for _ in range(NT)]
    ropet = [A.alloc(z1, "ropet%d" % i, [128, 2, 512], F32) for i in range(2)]
    ropet_b = [Buf() for _ in range(2)]
    qf = [A.alloc(z1, "qf%d" % i, [128, 512], F32) for i in range(2)]
    qf_b = [Buf() for _ in range(2)]
    qr = A.alloc(z1, "qr", [128, 512], F32)
    qr_b = Buf()
    epi = A.alloc(z1, "epi", [128, 16], F32)
    epi_b = Buf()
    otmp = [A.alloc(z1, "otmp%d" % i, [128, 128], F32) for i in range(2)]
    otmp_b = [Buf() for _ in range(2)]
    obf = [A.alloc(z1, "obf%d" % i, [128, 128], BF16) for i in range(4)]
    obf_b = [Buf() for _ in range(4)]
    tr_pending = []
    rperm = A.alloc(z1, "rperm", [128, 128], F32)
    B_rp = Buf()
    wfg_b16 = A.alloc(z1, "wfg", [128, 8, 8], BF16)
    bfv = A.alloc(z1, "bfv", [8, 1], F32)
    fgt = A.alloc(z3, "fgt", [8, S], F32)
    Fc = A.alloc(z3, "Fc", [8, S], F32)
    onesr = A.alloc(z3, "onesr", [8, S], BF16)
    fpc = [A.alloc(z3, "fpc%d" % i, [8, S], BF16) for i in range(3)]
    fres = fgt
    B_fg = Buf("fg")

    P.dma("sync", lambda e: e.dma_start(out=rperm[:], in_=rperm_d), "c1", W=[B_rp])
    P.dma("sync", lambda e: e.dma_start(out=bfv[:], in_=bf_d), "c1b", W=[B_fg])
    P.dma("pool", lambda e: e.dma_start(out=wfg_b16[:], in_=wfg_d.rearrange("(kc p) n -> p kc n", p=128)), "c2", W=[B_fg])
    P.dma("pool", lambda e: e.dma_start(out=wo_bf[:], in_=wo_d.rearrange("(kc p) n -> p kc n", p=128)), "wo", W=[B_wo])

    for G in range(NG):
        for kc in range(8):
            P.op("pe", (lambda e, G=G, kc=kc: e.matmul(ps[2][0:8, :], lhsT=wfg_b16[:, kc, :], rhs=hT[:, kc, G * 512:(G + 1) * 512],
                                                      start=(kc == 0), stop=(kc == 7))),
                 R=[B_fg] + hT_b[4 * G:4 * G + 4], W=[psb[2]])
        P.op("act", (lambda e, G=G: e.activation(out=fgt[:, G * 512:(G + 1) * 512], in_=ps[2][0:8, :], func=AF.Sigmoid,
                                                 bias=bfv[:, 0:1], scale=1.0)), R=[psb[2], B_fg], W=[B_fg])
    P.op("act", lambda e: e.activation(out=fgt[:], in_=fgt[:], func=AF.Ln), R=[B_fg], W=[B_fg])
    P.op("pool", lambda e: e.memset(onesr[:], 1.0), W=[B_fg])
    P.op("dve", lambda e: e.tensor_tensor_scan(out=Fc[:], data0=onesr[:], data1=fgt[:], initial=0.0,
                                               op0=ALU.mult, op1=ALU.add), R=[B_fg], W=[B_fg])
    hi, mid, lo = fpc
    P.op("dve", lambda e: e.tensor_copy(out=hi[:], in_=Fc[:]), R=[B_fg], W=[B_fg])
    P.op("dve", lambda e: e.tensor_tensor(out=fres[:], in0=Fc[:], in1=hi[:], op=ALU.subtract), R=[B_fg], W=[B_fg])
    P.op("dve", lambda e: e.tensor_copy(out=mid[:], in_=fres[:]), R=[B_fg], W=[B_fg])
    P.op("dve", lambda e: e.tensor_tensor(out=fres[:], in0=fres[:], in1=mid[:], op=ALU.subtract), R=[B_fg], W=[B_fg])
    P.op("dve", lambda e: e.tensor_copy(out=lo[:], in_=fres[:]), R=[B_fg], W=[B_fg])
    if dbg:
        o = dbg_out("Fc", [8, S])
        P.dma("sync", lambda e, o=o: e.dma_start(out=o, in_=Fc[:]), "dbg", R=[B_fg])

    QT = [qT, FQ]
    KT = [kT, FK]
    qb = [Buf("q0"), Buf("q1")]
    kb = [Buf("k0"), Buf("k1")]

    def build_faug(u):
        for hh in range(2):
            h = 2 * u + hh
            P.op("pool", (lambda e, hh=hh: e.memset(QT[hh][64:96, :], -1.0)), W=[qb[hh]])
            P.op("pool", (lambda e, hh=hh: e.memset(KT[hh][64:96, :], 1.0)), W=[kb[hh]])
            for r in range(3):
                P.dma("sync", (lambda e, hh=hh, r=r, h=h: e.dma_start(out=QT[hh][64 + r:65 + r, :], in_=fpc[r][h:h + 1, :])),
                      "faq%d" % hh, R=[B_fg], W=[qb[hh]])
                P.dma("sync", (lambda e, hh=hh, r=r, h=h: e.dma_start(out=KT[hh][67 + r:68 + r, :], in_=fpc[r][h:h + 1, :])),
                      "fak%d" % hh, R=[B_fg], W=[kb[hh]])

    n_rope = [0]
    n_qf = [0]
    n_pt = [0]
    n_o = [0]

    def project_unit(u, wsl):
        fox = u < 4
        w = wun[wsl]
        if fox:
            for hh in range(2):
                for which, dst, dbuf, c0 in ((0, QT[hh], qb[hh], 0), (1, KT[hh], kb[hh], 128)):
                    for G in range(NG):
                        pb = 6 + (G % 2)
                        for kc in range(8):
                            P.op("pe", (lambda e, pb=pb, kc=kc, cc=c0 + 64 * hh, G=G: e.matmul(
                                ps[pb][0:64, :], lhsT=w[:, kc, cc:cc + 64], rhs=hT[:, kc, G * 512:(G + 1) * 512],
                                start=(kc == 0), stop=(kc == 7))),
                                R=[wun_b[wsl]] + hT_b[4 * G:4 * G + 4], W=[psb[pb]])
                        sc = 0.125 if which == 0 else 1.0
                        P.op("act", (lambda e, pb=pb, dst=dst, G=G, sc=sc: e.activation(
                            out=dst[0:64, G * 512:(G + 1) * 512], in_=ps[pb][0:64, :], func=AF.Copy, scale=sc)),
                            R=[psb[pb]], W=[dbuf])
        else:
            for which, dstT, dbuf, c0 in ((0, qT, qb[0], 0), (1, kT, kb[0], 128)):
                for G in range(NG):
                    pb = 6 + (G % 2)
                    for kc in range(8):
                        P.op("pe", (lambda e, pb=pb, kc=kc, c0=c0, G=G: e.matmul(
                            ps[pb][:], lhsT=w[:, kc, c0:c0 + 128], rhs=hT[:, kc, G * 512:(G + 1) * 512],
                            start=(kc == 0), stop=(kc == 7))),
                            R=[wun_b[wsl]] + hT_b[4 * G:4 * G + 4], W=[psb[pb]])
                    sc = 0.125 if which == 0 else 1.0
                    rs = n_rope[0] % 2
                    n_rope[0] += 1
                    P.dma("sync", (lambda e, rs=rs, G=G: e.dma_start(out=ropet[rs][:, 0, :], in_=cos_d[:, G * 512:(G + 1) * 512])),
                          "rope%d" % rs, W=[ropet_b[rs]])
                    P.dma("sync", (lambda e, rs=rs, G=G: e.dma_start(out=ropet[rs][:, 1, :], in_=sin_d[:, G * 512:(G + 1) * 512])),
                          "rope%d" % rs, W=[ropet_b[rs]])
                    fs = n_qf[0] % 2
                    n_qf[0] += 1
                    P.op("act", (lambda e, pb=pb, fs=fs: e.copy(out=qf[fs][:], in_=ps[pb][:])), R=[psb[pb]], W=[qf_b[fs]])
                    P.op("pe", (lambda e, fs=fs: e.matmul(ps[2][:], lhsT=rperm[:], rhs=qf[fs][:], start=True, stop=True)),
                         R=[B_rp, qf_b[fs]], W=[psb[2]])
                    P.op("dve", (lambda e, rs=rs, sc=sc: e.scalar_tensor_tensor(out=qr[:], in0=ps[2][:], scalar=sc, in1=ropet[rs][:, 1, :],
                                                                          op0=ALU.mult, op1=ALU.mult)),
                         R=[psb[2], ropet_b[rs]], W=[qr_b])
                    P.op("pool", (lambda e, fs=fs, rs=rs: e.tensor_tensor(out=qf[fs][:], in0=qf[fs][:], in1=ropet[rs][:, 0, :], op=ALU.mult)),
                         R=[ropet_b[rs]], W=[qf_b[fs]])
                    P.op("dve", (lambda e, fs=fs, dstT=dstT, G=G, sc=sc: e.scalar_tensor_tensor(
                        out=dstT[:, G * 512:(G + 1) * 512], in0=qf[fs][:], scalar=sc, in1=qr[:], op0=ALU.mult, op1=ALU.add)),
                        R=[qf_b[fs], qr_b], W=[dbuf])
        P.op("pool", lambda e: e.memset(vtm[:], 1.0), W=[v_b])
        for i in range(NT):
            pb = 6 + (i % 2)
            for kc in range(8):
                P.op("pe", (lambda e, pb=pb, kc=kc, i=i: e.matmul(
                    ps[pb][:, 0:128], lhsT=hT[:, kc, i * 128:(i + 1) * 128], rhs=w[:, kc, 256:384],
                    start=(kc == 0), stop=(kc == 7))),
                    R=[wun_b[wsl], hT_b[i]], W=[psb[pb]])
            if fox:
                vout = vtm[:, i, :].rearrange("p (h c) -> p h c", c=65)[:, :, 0:64]
                vin = ps[pb][:, 0:128].rearrange("p (h c) -> p h c", c=64)
            else:
                vout = vtm[:, i, 0:128]
                vin = ps[pb][:, 0:128]
            if i % 2 == 0:
                P.op("act", (lambda e, vout=vout, vin=vin: e.copy(out=vout, in_=vin)), R=[psb[pb]], W=[v_b])
            else:
                P.op("dve", (lambda e, vout=vout, vin=vin: e.tensor_copy(out=vout, in_=vin)), R=[psb[pb]], W=[v_b])

    def attention_unit(u):
        fox = u < 4
        blocks = [(c, G, kt) for c in range(2) for G in range(NG) for kt in range(4 * G + 4)]

        def emit_qk(bi):
            c, G, kt = blocks[bi]
            sb = (0, 1, 6)[bi % 3]
            p0 = 64 * c
            if fox:
                P.op("pe", (lambda e, sb=sb, kt=kt, G=G, c=c: e.matmul(
                    ps[sb][:], lhsT=KT[c][0:70, kt * 128:(kt + 1) * 128], rhs=QT[c][0:70, G * 512:(G + 1) * 512],
                    start=True, stop=True)), R=[qb[c], kb[c]], W=[psb[sb]])
            else:
                P.op("pe", (lambda e, sb=sb, kt=kt, G=G, p0=p0: e.matmul(
                    ps[sb][:], lhsT=kT[p0:p0 + 64, kt * 128:(kt + 1) * 128], rhs=qT[p0:p0 + 64, G * 512:(G + 1) * 512],
                    start=True, stop=True)), R=[qb[0], kb[0]], W=[psb[sb]])

        emit_qk(0)
        if len(blocks) > 1:
            emit_qk(1)
        for bi, (c, G, kt) in enumerate(blocks):
            sb = (0, 1, 6)[bi % 3]
            if bi + 2 < len(blocks):
                emit_qk(bi + 2)
            pt = n_pt[0] % 3
            n_pt[0] += 1
            r = kt - 4 * G
            c_lo = max(r, 0) * 128
            if r >= 0:
                P.op("dve", (lambda e, sb=sb, r=r: e.tensor_tensor(out=ps[sb][:, r * 128:(r + 1) * 128],
                                                                   in0=ps[sb][:, r * 128:(r + 1) * 128], in1=maskneg[:], op=ALU.add)),
                     R=[B_const], W=[psb[sb]])
            P.op("act", (lambda e, sb=sb, pt=pt, c_lo=c_lo: e.activation(out=pT[pt][:, c_lo:512], in_=ps[sb][:, c_lo:512], func=AF.Exp)),
                 R=[psb[sb]], W=[pT_b[pt]])
            for qq in range(max(r, 0), 4):
                qt = 4 * G + qq
                ob = 2 + qq
                if fox:
                    rhs = vtm[:, kt, 0:65] if c == 0 else vtm[:, kt, 65:130]
                    ow = 65
                else:
                    rhs = vtm[:, kt, 0:129]
                    ow = 129
                P.op("pe", (lambda e, ob=ob, pt=pt, qq=qq, rhs=rhs, ow=ow, kt=kt, qt=qt: e.matmul(
                    ps[ob][:, 0:ow], lhsT=pT[pt][:, qq * 128:(qq + 1) * 128], rhs=rhs,
                    start=(kt == 0), stop=(kt == qt))), R=[pT_b[pt], v_b], W=[psb[ob]])
            if tr_pending and kt == 2:
                for f in tr_pending:
                    f()
                del tr_pending[:]
            if kt == 4 * G + 3:
                convert_some(1)
                for qq in range(4):
                    epilogue(u, c, 4 * G + qq, 2 + qq)
        for f in tr_pending:
            f()
        del tr_pending[:]

    def epilogue(u, c, qt, ob):
        fox = u < 4
        osl = n_o[0] % 2
        n_o[0] += 1
        if fox:
            rc = epi[:, 2 * c:2 * c + 1]
            P.op("dve", (lambda e, ob=ob, rc=rc: e.reciprocal(out=rc, in_=ps[ob][:, 64:65])), R=[psb[ob]], W=[epi_b])
            P.op("act", (lambda e, ob=ob, rc=rc, qt=qt, c=c: e.activation(
                out=fo_acc[:, qt, c * 64:(c + 1) * 64], in_=ps[ob][:, 0:64], func=AF.Copy, scale=rc)),
                R=[psb[ob], epi_b], W=[fo_b[qt]])
            if c == 1:
                def tr(u=u, qt=qt):
                    P.op("pe", (lambda e, qt=qt: e.transpose(ps[7][:].bitcast(BF16)[:, 0:128], fo_acc[:, qt, :], ident_b[:])),
                         R=[fo_b[qt], B_const], W=[psb[7]])
                    P.op("dve", (lambda e, u=u, qt=qt: e.tensor_copy(out=mixedT[:, u, qt * 128:(qt + 1) * 128],
                                                                   in_=ps[7][:].bitcast(BF16)[:, 0:128])),
                         R=[psb[7]], W=[mixedT_b[qt]])
                tr_pending.append(tr)
        else:
            if c == 0:
                rc = epi[:, 4:5]
                P.op("dve", (lambda e, ob=ob, rc=rc: e.reciprocal(out=rc, in_=ps[ob][:, 128:129])), R=[psb[ob]], W=[epi_b])
                P.op("act", (lambda e, ob=ob, rc=rc, qt=qt: e.activation(out=o1n[:, qt, :], in_=ps[ob][:, 0:128], func=AF.Copy, scale=rc)),
                     R=[psb[ob], epi_b], W=[o1n_b[qt]])
            else:
                rc = epi[:, 5:6]
                nl = epi[:, 6:7]
                P.op("dve", (lambda e, ob=ob, rc=rc: e.reciprocal(out=rc, in_=ps[ob][:, 128:129])), R=[psb[ob]], W=[epi_b])
                P.op("dve", (lambda e, rc=rc, nl=nl: e.tensor_tensor(out=nl, in0=rc, in1=lam_s[:, 5:6], op=ALU.mult)),
                     R=[epi_b, B_lam], W=[epi_b])
                P.op("dve", (lambda e, ob=ob, nl=nl, qt=qt, osl=osl: e.scalar_tensor_tensor(
                    out=otmp[osl][:], in0=ps[ob][:, 0:128], scalar=nl, in1=o1n[:, qt, :], op0=ALU.mult, op1=ALU.add)),
                    R=[psb[ob], epi_b, o1n_b[qt]], W=[otmp_b[osl]])
                ss = epi[:, 8:9]
                var = epi[:, 9:10]
                std = epi[:, 10:11]
                rs = epi[:, 11:12]
                os2 = qt % 4
                P.op("act", (lambda e, osl=osl, os2=os2, ss=ss: e.activation(out=obf[os2][:], in_=otmp[osl][:], func=AF.Square, accum_out=ss)),
                     R=[otmp_b[osl]], W=[obf_b[os2], epi_b])
                P.op("dve", (lambda e, ss=ss, var=var: e.tensor_scalar(out=var, in0=ss, scalar1=1.0 / 128, scalar2=1e-5,
                                                                      op0=ALU.mult, op1=ALU.add)), R=[epi_b], W=[epi_b])
                P.op("act", (lambda e, var=var, std=std: e.activation(out=std, in_=var, func=AF.Sqrt)), R=[epi_b], W=[epi_b])
                P.op("dve", (lambda e, std=std, rs=rs: e.reciprocal(out=rs, in_=std)), R=[epi_b], W=[epi_b])
                P.op("dve", (lambda e, osl=osl, rs=rs: e.scalar_tensor_tensor(
                    out=otmp[osl][:], in0=otmp[osl][:], scalar=rs, in1=gsub[:], op0=ALU.mult, op1=ALU.mult)),
                    R=[epi_b, B_const], W=[otmp_b[osl]])
                P.op("act", (lambda e, osl=osl, os2=os2: e.activation(out=obf[os2][:], in_=otmp[osl][:], func=AF.Copy, scale=1.0 - lam_init)),
                     R=[otmp_b[osl]], W=[obf_b[os2]])

                def tr(u=u, qt=qt, os2=os2):
                    P.op("pe", (lambda e, os2=os2: e.transpose(ps[7][:].bitcast(BF16)[:, 0:128], obf[os2][:], ident_b[:])),
                         R=[obf_b[os2], B_const], W=[psb[7]])
                    P.op("dve", (lambda e, u=u, qt=qt: e.tensor_copy(out=mixedT[:, u, qt * 128:(qt + 1) * 128],
                                                                   in_=ps[7][:].bitcast(BF16)[:, 0:128])),
                         R=[psb[7]], W=[mixedT_b[qt]])
                tr_pending.append(tr)

    fo_acc = A.alloc(z1, "fo_acc", [128, NT, 128], BF16)
    fo_b = [Buf() for _ in range(NT)]

    units = list(range(nunits))
    def load_wun(ui):
        wsl = ui % 2
        u = units[ui]
        P.dma("pool", (lambda e, wsl=wsl, u=u: e.dma_start(out=wun[wsl][:], in_=wun_d[u].rearrange("(kc p) n -> p kc n", p=128))),
              "wun%d" % wsl, W=[wun_b[wsl]])

    if units:
        load_wun(0)
    for ui, u in enumerate(units):
        wsl = ui % 2
        project_unit(u, wsl)
        if ui + 1 < len(units):
            load_wun(ui + 1)
        if u < 4:
            build_faug(u)
        attention_unit(u)

    if dbg:
        for nm, t, shp in (("FQ", FQ, [128, S]), ("FK", FK, [128, S]), ("vtm", vtm, [128, NT, VW]), ("fo_acc", fo_acc, [128, NT, 128])):
            oo = dbg_out(nm, shp, BF16)
            P.dma("sync", (lambda e, oo=oo, t=t: e.dma_start(out=oo, in_=t[:])), "dbg", R=[qb[1], kb[1], v_b] + fo_b)
        o = dbg_out("mixedT", [128, 8, S], BF16)
        P.dma("sync", lambda e, o=o: e.dma_start(out=o, in_=mixedT[:]), "dbg", R=mixedT_b)
    if stage <= 2:
        return finish(nc, P, dbg_d)

    P.barrier()
    z1["cur"] = z1["start"]
    xn = A.alloc(z1, "xn", [128, NT, D], F32)
    xn_b = [Buf("xn%d" % i) for i in range(NT)]
    wtmp = [A.alloc(z1, "wtmp%d" % i, [128, 512], F32) for i in range(2)]
    wtmp_b = [Buf() for _ in range(2)]
    nw = 0
    for i in range(NT):
        P.dma("sync", (lambda e, i=i: e.dma_start(out=xn[:, i, :], in_=x_t[i])), "xn%d" % i, W=[xn_b[i]])
        for half in range(2):
            pb = 2 * (i % 2) + half
            for kc in range(8):
                P.op("pe", (lambda e, pb=pb, kc=kc, i=i, half=half: e.matmul(
                    ps[pb][:], lhsT=mixedT[:, kc, i * 128:(i + 1) * 128], rhs=wo_bf[:, kc, half * 512:(half + 1) * 512],
                    start=(kc == 0), stop=(kc == 7))), R=[mixedT_b[i], B_wo], W=[psb[pb]])
            ws = nw % 2
            nw += 1
            P.op("dve", (lambda e, pb=pb, ws=ws, half=half: e.tensor_tensor(
                out=wtmp[ws][:], in0=ps[pb][:], in1=gt1_bc[:, half * 512:(half + 1) * 512], op=ALU.mult)),
                R=[psb[pb], B_bc], W=[wtmp_b[ws]])
            P.op("pool", (lambda e, ws=ws, i=i, half=half: e.tensor_tensor(
                out=xn[:, i, half * 512:(half + 1) * 512], in0=xn[:, i, half * 512:(half + 1) * 512], in1=wtmp[ws][:], op=ALU.add)),
                R=[wtmp_b[ws]], W=[xn_b[i]])
    if dbg:
        o = dbg_out("x1", [S, D])
        P.dma("sync", (lambda e, o=o: e.dma_start(out=o.rearrange("(t p) d -> p t d", p=128), in_=xn[:])), "dbg", R=xn_b)
    if stage <= 3:
        return finish(nc, P, dbg_d)

    convert_some(len(cv_list))
    P.barrier()
    z2["cur"] = z2["start"]
    z3["cur"] = z3["start"]
    NS = 10
    GK = 1
    uvbuf = [A.alloc(z2, "uvbuf%d" % i, [128, 2 * D], BF16) for i in range(NS)]
    uv_b = [Buf() for _ in range(NS)]
    comb = A.alloc(z2, "comb", [128, 8, 256], F32)
    wpq_bf = A.alloc(z3, "wpq_bf", [128, 8, D], BF16)
    B_wpq = Buf()
    oh = A.alloc(z3, "oh", [128, 8, 256], F32)
    prod = A.alloc(z3, "prod", [128, 8, 256], F32)
    skb = A.alloc(z3, "skb", [128, 256], F32)
    B_sk = Buf()
    h2f = A.alloc(z1, "h2f", [128, D], F32)
    h2b = A.alloc(z1, "h2b", [128, D], BF16)
    h2T = A.alloc(z1, "h2T", [128, 8, 128], BF16)
    qTf = A.alloc(z1, "qTf", [128, 8, 128], F32)
    sc = prod[:].rearrange("p h (t k) -> p (h t) k", k=128)
    sc2 = oh[:].rearrange("p h (t k) -> p (h t) k", k=128)
    junkb = h2b
    junk2 = A.alloc(z3, "junk2", [128, D], BF16)
    B_junk2 = Buf("junk2")
    gateB = A.alloc(z1, "gateB", [128, 128], F32)
    m16 = A.alloc(z1, "m16", [128, 16, 16], F32)
    i16 = A.alloc(z1, "i16", [128, 16, 16], U32)
    i16f = A.alloc(z1, "i16f", [128, 16, 16], F32)
    t16 = A.alloc(z1, "t16", [128, 8, 16], F32)
    ci = A.alloc(z1, "ci", [128, 8, 16], U32)
    ca = A.alloc(z1, "ca", [128, 8, 16], U32)
    cb = A.alloc(z1, "cb", [128, 8, 16], U32)
    caf = A.alloc(z1, "caf", [128, 8, 16], F32)
    cbf = A.alloc(z1, "cbf", [128, 8, 16], F32)
    e1 = A.alloc(z1, "e1", [128, 8, 16], F32)
    e2 = A.alloc(z1, "e2", [128, 8, 16], F32)
    eif = A.alloc(z1, "eif", [128, 128], F32)
    eidx = A.alloc(z1, "eidx", [128, 128], U32)
    gex = A.alloc(z1, "gex", [128, 8, 16], F32)
    gsum = A.alloc(z1, "gsum", [128, 8], F32)
    gate = A.alloc(z1, "gate", [128, 128], F32)
    a_acc = A.alloc(z1, "a_acc", [128, 128], F32)
    wgt = A.alloc(z1, "wgt", [128, 128], F32)
    dgb = [A.alloc(z1, "dgb%d" % i, [128, 128], BF16) for i in range(2)]
    dgb_b = [Buf() for _ in range(2)]
    nstat2 = A.alloc(z1, "nstat2", [128, 8 * NT], F32)
    ytile = [A.alloc(z1, "ytile", [128, D], F32)] * 2
    ytile_b = [Buf()] * 2
    NCTX["nstat"] = nstat2
    NCTX["nstat_b"] = [Buf() for _ in range(2 * NT)]
    NCTX["junk"] = junkb
    Bt = {k: Buf(k) for k in ("h2f", "h2b", "h2T", "qTf", "sc", "sc2", "m16", "i16", "comb", "t16", "ci", "cab", "oh", "prod",
                              "e12", "eidx", "g", "a", "wgt", "htmp")}
    NCTX["junk_b"] = Bt["h2b"]
    Bt["m16b"] = Buf("m16b")
    Bt["t16b"] = Buf("t16b")
    Bt["sc"] = Bt["prod"]
    Bt["sc2"] = Bt["oh"]
    P.dma("pool", lambda e: e.dma_start(out=wpq_bf[:], in_=wpq_d.rearrange("(kc p) n -> p kc n", p=128)), "wpq", W=[B_wpq])
    P.dma("sync", lambda e: e.dma_start(out=skb[:], in_=sk_d), "skb", W=[B_sk])
    y_t = y_d.rearrange("(t p) d -> t p d", p=128)
    m16v = m16[:].rearrange("p (h t) r -> p h t r", t=2)
    i16fv = i16f[:].rearrange("p (h t) r -> p h t r", t=2)
    cn = {"ngu": 0, "ngv": 0, "ncast": 0, "ndg": 0, "nw": nw}
    eidxB = A.alloc(z1, "eidxB", [128, 128], U32)
    wgtB = A.alloc(z1, "wgtB", [128, 128], F32)
    eidx2 = [eidx, eidxB]
    h2f2 = [h2f, gt1_bc]
    h2f_b = [Buf("h2f0"), Buf("h2f1")]
    gate2 = [gate, gateB]
    gate_b = [Buf("gate0"), Buf("gate1")]
    wgt2 = [wgt, wgtB]
    eidx_b = [Buf(), Buf()]
    wgt_b = [Buf(), Buf()]

    def prepA(i):
        norm_tile(i, xn[:, i, :], xn_b[i], sc2_bc, sh2_bc, h2f2[i % 2], h2f_b[i % 2], h2f2[i % 2], h2f_b[i % 2])
        P.op("act", lambda e: e.copy(out=h2b[:], in_=h2f2[i % 2][:]), R=[h2f_b[i % 2]], W=[Bt["h2b"]])
        pv = ps[0][:].bitcast(BF16)
        for j in range(8):
            P.op("pe", (lambda e, j=j, pv=pv: e.transpose(pv[:, j * 128:(j + 1) * 128], h2b[:, j * 128:(j + 1) * 128], ident_b[:])),
                 R=[Bt["h2b"], B_const], W=[psb[0]])
        P.op("act", (lambda e, pv=pv: e.copy(out=h2T[:], in_=pv.rearrange("p (j t) -> p j t", t=128))), R=[psb[0]], W=[Bt["h2T"]])
        for hb in range(2):
            pb = 1 + hb
            for hh in range(4):
                h = hb * 4 + hh
                for kc in range(8):
                    P.op("pe", (lambda e, pb=pb, hh=hh, h=h, kc=kc: e.matmul(
                        ps[pb][:, hh * 128:(hh + 1) * 128], lhsT=wpq_bf[:, kc, h * 128:(h + 1) * 128], rhs=h2T[:, kc, :],
                        start=(kc == 0), stop=(kc == 7))), R=[B_wpq, Bt["h2T"]], W=[psb[pb]])
            P.op("act", (lambda e, pb=pb, hb=hb: e.copy(out=qTf[:, hb * 4:(hb + 1) * 4, :],
                                                       in_=ps[pb][:].rearrange("p (h t) -> p h t", t=128))), R=[psb[pb]], W=[Bt["qTf"]])
        for hp in range(4):
            pb = 2 + hp
            for hh in range(2):
                h = hp * 2 + hh
                P.op("pe", (lambda e, pb=pb, hh=hh, h=h: e.matmul(ps[pb][:, hh * 256:(hh + 1) * 256], lhsT=qTf[:, h, :], rhs=skb[:],
                                                                 start=True, stop=True)), R=[Bt["qTf"], B_sk], W=[psb[pb]])
            P.op("act", (lambda e, pb=pb, hp=hp: e.copy(out=sc[:, hp * 4:(hp + 1) * 4, :],
                                                       in_=ps[pb][:].rearrange("p (g t) -> p g t", t=128))), R=[psb[pb]], W=[Bt["sc"]])

    def prepB(i):
        for sg in range(16):
            P.op("dve", (lambda e, sg=sg: e.max(out=m16[:, sg, 0:8], in_=sc[:, sg, :])), R=[Bt["sc"]], W=[Bt["m16"]], waw_ok=True)
        for sg in range(16):
            P.op("dve", (lambda e, sg=sg: e.match_replace(out=sc2[:, sg, :], in_to_replace=m16[:, sg, 0:8], in_values=sc[:, sg, :], imm_value=NEG)),
                 R=[Bt["sc"], Bt["m16"]], W=[Bt["sc2"]], waw_ok=True)
        for sg in range(16):
            P.op("dve", (lambda e, sg=sg: e.max(out=m16[:, sg, 8:16], in_=sc2[:, sg, :])), R=[Bt["sc2"]], W=[Bt["m16b"]], waw_ok=True)
        for sg in range(16):
            P.op("dve", (lambda e, sg=sg: e.max_index(out=i16[:, sg, 0:8], in_max=m16[:, sg, 0:8], in_values=sc[:, sg, :])),
                 R=[Bt["sc"], Bt["m16"]], W=[Bt["i16"]], waw_ok=True)
        for sg in range(16):
            P.op("dve", (lambda e, sg=sg: e.max_index(out=i16[:, sg, 8:16], in_max=m16[:, sg, 8:16], in_values=sc2[:, sg, :])),
                 R=[Bt["sc2"], Bt["m16b"]], W=[Bt["i16"]], waw_ok=True)
        P.op("pool", lambda e: e.tensor_copy(out=i16f[:], in_=i16[:]), R=[Bt["i16"]], W=[Bt["i16"]])
        P.op("pool", lambda e: e.tensor_tensor(out=comb[:].rearrange("p h (a b) -> p h a b", b=16),
                                               in0=m16v[:, :, 0, :].unsqueeze(3).to_broadcast([128, 8, 16, 16]),
                                               in1=m16v[:, :, 1, :].unsqueeze(2).to_broadcast([128, 8, 16, 16]), op=ALU.add),
             R=[Bt["m16"], Bt["m16b"]], W=[Bt["comb"]])
        for h in range(8):
            P.op("dve", (lambda e, h=h: e.max(out=t16[:, h, 0:8], in_=comb[:, h, :])), R=[Bt["comb"]], W=[Bt["t16"]], waw_ok=True)
        for h in range(8):
            P.op("dve", (lambda e, h=h: e.max_index(out=ci[:, h, 0:8], in_max=t16[:, h, 0:8], in_values=comb[:, h, :])),
                 R=[Bt["comb"], Bt["t16"]], W=[Bt["ci"]], waw_ok=True)
        for h in range(8):
            P.op("dve", (lambda e, h=h: e.match_replace(out=comb[:, h, :], in_to_replace=t16[:, h, 0:8], in_values=comb[:, h, :], imm_value=NEG)),
                 R=[Bt["t16"], Bt["ci"]], W=[Bt["comb"]], waw_ok=True)
        for h in range(8):
            P.op("dve", (lambda e, h=h: e.max(out=t16[:, h, 8:16], in_=comb[:, h, :])), R=[Bt["comb"]], W=[Bt["t16b"]], waw_ok=True)
        for h in range(8):
            P.op("dve", (lambda e, h=h: e.max_index(out=ci[:, h, 8:16], in_max=t16[:, h, 8:16], in_values=comb[:, h, :])),
                 R=[Bt["comb"], Bt["t16b"]], W=[Bt["ci"]], waw_ok=True)
        P.op("dve", lambda e: e.tensor_single_scalar(out=ca[:], in_=ci[:], scalar=4, op=ALU.logical_shift_right), R=[Bt["ci"]], W=[Bt["cab"]])
        P.op("dve", lambda e: e.tensor_single_scalar(out=cb[:], in_=ci[:], scalar=15, op=ALU.bitwise_and), R=[Bt["ci"]], W=[Bt["cab"]], waw_ok=True)
        iota_b = iota16[:].unsqueeze(1).unsqueeze(1).to_broadcast([128, 8, 16, 16])
        for (cf, half, eo) in ((ca, 0, e1), (cb, 1, e2)):
            P.op("dve", (lambda e, cf=cf: e.tensor_tensor(out=oh[:].rearrange("p h (r a) -> p h r a", a=16),
                                                         in0=cf[:].unsqueeze(3).to_broadcast([128, 8, 16, 16]), in1=iota_b, op=ALU.is_equal)),
                 R=[Bt["cab"], B_const], W=[Bt["oh"]])
            P.op("pool", (lambda e, half=half: e.tensor_tensor(out=prod[:].rearrange("p h (r a) -> p h r a", a=16),
                                                             in0=oh[:].rearrange("p h (r a) -> p h r a", a=16),
                                                             in1=i16fv[:, :, half, :].unsqueeze(2).to_broadcast([128, 8, 16, 16]), op=ALU.mult)),
                 R=[Bt["oh"], Bt["i16"]], W=[Bt["prod"]])
            P.op("dve", (lambda e, eo=eo: e.tensor_reduce(out=eo[:], in_=prod[:].rearrange("p h (r a) -> p h r a", a=16), axis=AX.X, op=ALU.add)),
                 R=[Bt["prod"]], W=[Bt["e12"]])

    def prepC(i):
        P.op("dve", lambda e: e.scalar_tensor_tensor(out=eidx2[i % 2][:], in0=e1[:].rearrange("p h r -> p (h r)"), scalar=128.0,
                                                     in1=e2[:].rearrange("p h r -> p (h r)"), op0=ALU.mult, op1=ALU.add),
             R=[Bt["e12"]], W=[eidx_b[i % 2]])
        P.op("dve", lambda e: e.tensor_tensor(out=gex[:], in0=t16[:], in1=t16[:, :, 0:1].to_broadcast([128, 8, 16]), op=ALU.subtract),
             R=[Bt["t16"], Bt["t16b"]], W=[Bt["g"]])
        P.op("act", lambda e: e.activation(out=gex[:], in_=gex[:], func=AF.Exp), R=[Bt["g"]], W=[Bt["g"]])
        P.op("dve", lambda e: e.tensor_reduce(out=gsum[:], in_=gex[:], axis=AX.X, op=ALU.add), R=[Bt["g"]], W=[Bt["g"]])
        P.op("dve", lambda e: e.reciprocal(out=gsum[:], in_=gsum[:]), R=[Bt["g"]], W=[Bt["g"]])
        P.op("dve", lambda e: e.tensor_tensor(out=gate2[i % 2][:].rearrange("p (h r) -> p h r", r=16), in0=gex[:],
                                              in1=gsum[:].unsqueeze(2).to_broadcast([128, 8, 16]), op=ALU.mult), R=[Bt["g"]], W=[gate_b[i % 2]])

    ak_b = [Buf() for _ in range(128 // GK)]
    wk_b = [Buf() for _ in range(128 // GK)]
    slot_of = {}

    def ghead(i, g):
        for k in range(g * GK, (g + 1) * GK):
            s_ = cn['ngu'] % NS
            cn['ngu'] += 1
            slot_of[(i, k)] = s_
            P.dma("pool", (lambda e, s_=s_, k=k, i=i: e.indirect_dma_start(
                out=uvbuf[s_][:], out_offset=None, in_=uvb_d, in_offset=bass.IndirectOffsetOnAxis(ap=eidx2[i % 2][:, k:k + 1], axis=0))),
                "g%d" % s_, R=[eidx_b[i % 2], B_cv], W=[uv_b[s_]], skip=("dve", "dma:g%d" % s_), extra=[eidx_b[i % 2].w])
            P.op("dve", (lambda e, s_=s_, k=k, i=i: e.scalar_tensor_tensor(out=junk2[:], in0=uvbuf[s_][:, 0:D], scalar=1.0, in1=h2f2[i % 2][:],
                                                                       op0=ALU.mult, op1=ALU.mult, accum_out=a_acc[:, k:k + 1])),
                 R=[uv_b[s_], h2f_b[i % 2]], W=[B_junk2, ak_b[g]],
                 extra=([("act", P.cnt["act"])] if (g == 0 and k == 0) else []))
        P.op("act", (lambda e, g=g, i=i: e.activation(out=wgt2[i % 2][:, g * GK:(g + 1) * GK], in_=a_acc[:, g * GK:(g + 1) * GK], func=AF.Gelu)),
             R=[ak_b[g]], W=[wk_b[g]])

    def gtail(i, g):
        P.op("dve", (lambda e, g=g, i=i: e.tensor_tensor(out=wgt2[i % 2][:, g * GK:(g + 1) * GK], in0=wgt2[i % 2][:, g * GK:(g + 1) * GK],
                                                        in1=gate2[i % 2][:, g * GK:(g + 1) * GK], op=ALU.mult)), R=[gate_b[i % 2]], W=[wk_b[g]])
        for k in range(g * GK, (g + 1) * GK):
            s_ = slot_of[(i, k)]
            ds_ = cn['ndg'] % 2
            cn['ndg'] += 1
            P.op("act", (lambda e, ds_=ds_, k=k, i=i: e.activation(out=dgb[ds_][:], in_=ident_b[:], func=AF.Copy, scale=wgt2[i % 2][:, k:k + 1])),
                 R=[wk_b[g], B_const], W=[dgb_b[ds_]])
            for half in range(2):
                P.op("pe", (lambda e, ds_=ds_, s_=s_, half=half, k=k: e.matmul(
                    ps[6 + half][:], lhsT=dgb[ds_][:], rhs=uvbuf[s_][:, D + half * 512:D + (half + 1) * 512],
                    start=(k == 0), stop=(k == 127))), R=[dgb_b[ds_], uv_b[s_]], W=[psb[6 + half]], skip=("dma:g%d" % s_,))

    def fin(i):
        for half in range(2):
            ws = cn['nw'] % 2
            cn['nw'] += 1
            P.op("dve", (lambda e, ws=ws, half=half: e.tensor_tensor(out=wtmp[ws][:], in0=ps[6 + half][:],
                                                                    in1=gt2_bc[:, half * 512:(half + 1) * 512], op=ALU.mult)),
                 R=[psb[6 + half], B_bc], W=[wtmp_b[ws]])
            P.op("pool", (lambda e, ws=ws, i=i, half=half: e.tensor_tensor(
                out=xn[:, i, half * 512:(half + 1) * 512], in0=xn[:, i, half * 512:(half + 1) * 512], in1=wtmp[ws][:], op=ALU.add)),
                R=[wtmp_b[ws]], W=[xn_b[i]])
        ys = i % 2
        norm_tile(NT + i, xn[:, i, :], xn_b[i], gfin, gfin, None, None, ytile[ys], ytile_b[ys])
        P.dma("sync", (lambda e, ys=ys, i=i: e.dma_start(out=y_t[i], in_=ytile[ys][:])), "y0", R=[ytile_b[ys]])


    NGRP_K = 128 // GK
    GA, GB, GC = 8, 56, 64
    prepA(0)
    prepB(0)
    prepC(0)
    for i in range(NT):
        for g in range(NGRP_K):
            ghead(i, g)
            if g >= 1:
                gtail(i, g - 1)
            if i + 1 < NT:
                if g == GA:
                    prepA(i + 1)
                if g == GB:
                    prepB(i + 1)
                if g == GC:
                    prepC(i + 1)
        gtail(i, NGRP_K - 1)
        fin(i)

    return finish(nc, P, dbg_d)


def finish(nc, P, dbg_d):
    from contextlib import ExitStack
    P.barrier()
    P.op("act", lambda e: e.nop())
    with ExitStack() as stack:
        P.emit(nc, stack)
    return nc, dbg_d


def _fm(vec):
    v = np.asarray(vec, np.float32).reshape(-1, 128)
    return np.ascontiguousarray(v.T)


def host_consts(S):
    ident = np.eye(128, dtype=np.float32)
    kk = np.arange(128)[:, None]
    qq = np.arange(128)[None, :]
    cmask = (qq >= kk).astype(np.float32)
    pos = np.arange(S, dtype=np.float32)
    inv = (500000.0 ** (-np.arange(0, 16, 2, dtype=np.float32) / 16.0)).astype(np.float32)
    ang = pos[None, :] * inv[:, None]
    cos = np.ones((128, S), np.float32)
    sin = np.zeros((128, S), np.float32)
    rp = np.zeros((128, 128), np.float32)
    for blk in range(2):
        b = 64 * blk
        for j in range(8):
            cos[b + j] = np.cos(ang[j])
            cos[b + 8 + j] = np.cos(ang[j])
            sin[b + j] = np.sin(ang[j])
            sin[b + 8 + j] = np.sin(ang[j])
            rp[b + j, b + 8 + j] = -1.0
            rp[b + 8 + j, b + j] = 1.0
    iota16 = np.tile(np.arange(16, dtype=np.float32)[None, :], (128, 1))
    return dict(ident=ident, cmask=cmask, rope_cos=cos, rope_sin=sin,
                rpermT=np.ascontiguousarray(rp.T), iota16=iota16)


def host_shared(inp, S):
    w_in = np.asarray(inp["w_in"], np.float32)[0]
    fq, fk, fv = w_in[:, 0:512], w_in[:, 512:1024], w_in[:, 1024:1536]
    fg = w_in[:, 1536:1544]
    dq, dk, dv = w_in[:, 1544:2056], w_in[:, 2056:2568], w_in[:, 2568:3080]
    units = []
    for u in range(4):
        sl = slice(u * 128, (u + 1) * 128)
        units.append(np.concatenate([fq[:, sl], fk[:, sl], fv[:, sl]], axis=1))
    for d in range(4):
        sl = slice(d * 128, (d + 1) * 128)
        units.append(np.concatenate([dq[:, sl], dk[:, sl], dv[:, sl]], axis=1))
    w_units = np.ascontiguousarray(np.stack(units, 0))
    sk = np.asarray(inp["sub_keys"], np.float32)[0]
    skblk = np.zeros((128, 256), np.float32)
    skblk[0:64, 0:128] = sk[0].T
    skblk[64:128, 128:256] = sk[1].T
    lam = np.concatenate([np.asarray(inp[k], np.float32)[0] for k in ("lambda_q1", "lambda_k1", "lambda_q2", "lambda_k2")])
    sh = dict(
        w_ada=np.ascontiguousarray(np.asarray(inp["w_ada"], np.float32)[0]),
        b_adaT=_fm(np.asarray(inp["b_ada"])[0]),
        g_attnT=_fm(np.asarray(inp["g_attn"])[0]),
        g_ffnT=_fm(np.asarray(inp["g_ffn"])[0]),
        g_final_bc=np.ascontiguousarray(np.tile(np.asarray(inp["g_final"], np.float32)[None, :], (128, 1))),
        w_units=w_units,
        w_fg=np.ascontiguousarray(fg),
        b_f=np.ascontiguousarray(np.asarray(inp["b_f"], np.float32)[0].reshape(8, 1)),
        lam_bc=np.ascontiguousarray(np.tile(lam[None, :], (128, 1))),
        g_subln_bc=np.ascontiguousarray(np.tile(np.asarray(inp["g_subln"], np.float32)[0][None, :], (128, 1))),
        w_o=np.ascontiguousarray(np.asarray(inp["w_o"], np.float32)[0]),
        w_pq=np.ascontiguousarray(np.asarray(inp["w_pq"], np.float32)[0]),
        skblk=skblk,
        u_exp=np.ascontiguousarray(np.asarray(inp["u_experts"], np.float32)[0]),
        v_exp=np.ascontiguousarray(np.asarray(inp["v_experts"], np.float32)[0]),
    )
    sh.update(host_consts(S))
    return sh


def make_in_maps(inp, S, ncores):
    sh = host_shared(inp, S)
    x = np.asarray(inp["x"], np.float32)
    c = np.asarray(inp["c"], np.float32)
    maps = []
    for b in range(ncores):
        m = dict(sh)
        m["x"] = np.ascontiguousarray(x[b, :S])
        m["cT"] = _fm(c[b])
        maps.append(m)
    return maps


_CACHE = {}


def kernel(**inputs):
    S = SEQ
    if "nc" not in _CACHE:
        _CACHE["nc"] = build(NT=S // 128)[0]
    nc = _CACHE["nc"]
    maps = make_in_maps(inputs, S, NCORES)
    res = run_bass_kernel_spmd(nc, maps, core_ids=list(range(NCORES)))
    out = np.stack([np.asarray(r["y"], np.float32) for r in res.results], 0)
    return out
```
